# Optimizing a Trainium2 kernel written in Bass

```python
import jax, jax.numpy as jnp
from jax import lax
import numpy as np

D_MODEL = 1024
BATCH = 32
SEQ = 256
DEPTH = 1
DEC_BATCH = 4
DEC_SEQ = 4096
PAST_LEN = 512

GRID_W = 64
N_RET_HEADS = 4
RET_DK = 128
RET_DV = 128
RET_QK_WIDTH = N_RET_HEADS * RET_DK
RET_WIDTH = N_RET_HEADS * RET_DV
POOL_WINDOWS = (2, 4, 8, 16)
N_POOL_GROUPS = 4
POOL_GROUP = 128
POOL_WIDTH = N_POOL_GROUPS * POOL_GROUP
MIX_WIDTH = RET_WIDTH + POOL_WIDTH
IN_WIDTH = 2 * RET_QK_WIDTH + 2 * RET_WIDTH + POOL_WIDTH
D_FF = 2816
CHUNK = 128
ROPE_BASE = 10000.0
N_MOD = 9
EPS = 1e-6

kernel_name = "hybrid_retention_pool_macaron_dit_step"


def _rmsnorm(x, g):
    xf = x.astype(jnp.float32)
    y = xf * lax.rsqrt(jnp.mean(xf * xf, axis=-1, keepdims=True) + EPS) * g.astype(jnp.float32)
    return y.astype(x.dtype)


def _modulate(h, shift, scale):
    return h * (1.0 + scale[:, None, :]) + shift[:, None, :]


def _swiglu(h, w1, w3, w2):
    return (jax.nn.silu(h @ w1) * (h @ w3)) @ w2


def _rotary_2d(x):
    L = x.shape[1]
    rows = L // GRID_W
    row = jnp.repeat(jnp.arange(rows, dtype=jnp.float32), GRID_W)
    col = jnp.tile(jnp.arange(GRID_W, dtype=jnp.float32), rows)
    n_half = RET_DK // 4
    freqs = ROPE_BASE ** (-jnp.arange(n_half, dtype=jnp.float32) / n_half)
    ang = jnp.concatenate([row[:, None] * freqs, col[:, None] * freqs], axis=-1)
    cos = jnp.cos(ang)[None, :, None, :]
    sin = jnp.sin(ang)[None, :, None, :]
    xf = x.astype(jnp.float32)
    x1, x2 = xf[..., : RET_DK // 2], xf[..., RET_DK // 2:]
    out = jnp.concatenate([x1 * cos - x2 * sin, x1 * sin + x2 * cos], axis=-1)
    return out.astype(x.dtype)


def _retention_dir(q, k, v, log_gamma, s0):
    q = q.astype(jnp.float32)
    k = k.astype(jnp.float32)
    v = v.astype(jnp.float32)
    B, H, L, dk = q.shape
    dv = v.shape[-1]
    nc = L // CHUNK
    qc = q.reshape(B, H, nc, CHUNK, dk)
    kc = k.reshape(B, H, nc, CHUNK, dk)
    vc = v.reshape(B, H, nc, CHUNK, dv)
    idx = jnp.arange(CHUNK, dtype=jnp.float32)
    lg = log_gamma.astype(jnp.float32)
    rel = idx[:, None] - idx[None, :]
    decay_mat = jnp.where(rel >= 0, jnp.exp(lg[:, None, None] * jnp.maximum(rel, 0.0)), 0.0)
    scores = jnp.einsum('bhnid,bhnjd->bhnij', qc, kc) * decay_mat[None, :, None]
    inner = jnp.einsum('bhnij,bhnjv->bhniv', scores, vc)
    k_decay = jnp.exp(lg[:, None] * (CHUNK - 1 - idx)[None, :])
    kv_chunk = jnp.einsum('bhnjd,hj,bhnjv->nbhdv', kc, k_decay, vc)
    chunk_decay = jnp.exp(lg * CHUNK)[None, :, None, None]

    def step(s, kv):
        return s * chunk_decay + kv, s

    s_final, s_prev = lax.scan(step, s0.astype(jnp.float32), kv_chunk)
    q_decay = jnp.exp(lg[:, None] * (idx + 1.0)[None, :])
    cross = jnp.einsum('bhnid,nbhdv,hi->bhniv', qc, s_prev, q_decay)
    o = (inner + cross).reshape(B, H, L, dv)
    return o, s_final


def _bidir_retention(q, k, v, lg_f, lg_b, s0_f, s0_b):
    o_f, s_f = _retention_dir(q, k, v, lg_f, s0_f)
    o_b, s_b = _retention_dir(jnp.flip(q, 2), jnp.flip(k, 2), jnp.flip(v, 2), lg_b, s0_b)
    return o_f + jnp.flip(o_b, 2), s_f, s_b


def _pool_mixer(u, pool_w, pool_scale):
    B, L, _ = u.shape
    uf = u.astype(jnp.float32)
    cs = jnp.concatenate([jnp.zeros((B, 1, POOL_WIDTH), jnp.float32), jnp.cumsum(uf, axis=1)], axis=1)
    t = jnp.arange(L)
    outs = []
    for gi, w in enumerate(POOL_WINDOWS):
        sl = slice(gi * POOL_GROUP, (gi + 1) * POOL_GROUP)
        csg = cs[..., sl]
        lo = jnp.clip(t - w // 2, 0, L)
        hi = jnp.clip(t + w // 2, 0, L)
        mean = (csg[:, hi] - csg[:, lo]) / (hi - lo).astype(jnp.float32)[None, :, None]
        outs.append(jnp.einsum('blc,cd->bld', mean - uf[..., sl], pool_w[gi].astype(jnp.float32)))
    out = jnp.concatenate(outs, axis=-1) * pool_scale.astype(jnp.float32)
    return out.astype(u.dtype)


def _layer(x, cond, is_latent, s0_f, s0_b, ada_w, ada_b, norm_ffn1, ffn1_w1, ffn1_w3, ffn1_w2,
           norm_mix, w_in, ret_decay_fwd, ret_decay_bwd, ret_gn, pool_w, pool_scale, w_out,
           norm_ffn2, ffn2_w1, ffn2_w3, ffn2_w2):
    B, L, _ = x.shape
    mods = (jax.nn.silu(cond.astype(jnp.float32)) @ ada_w.astype(jnp.float32) + ada_b.astype(jnp.float32)).astype(x.dtype)
    sh1, sc1, g1, sh2, sc2, g2, sh3, sc3, g3 = jnp.split(mods, N_MOD, axis=-1)

    h = x + 0.5 * g1[:, None, :] * _swiglu(_modulate(_rmsnorm(x, norm_ffn1), sh1, sc1), ffn1_w1, ffn1_w3, ffn1_w2)

    a = _modulate(_rmsnorm(h, norm_mix), sh2, sc2)
    proj = a @ w_in
    q, k, v, gate, u = jnp.split(
        proj, [RET_QK_WIDTH, 2 * RET_QK_WIDTH, 2 * RET_QK_WIDTH + RET_WIDTH, 2 * RET_QK_WIDTH + 2 * RET_WIDTH], axis=-1)
    q = q.reshape(B, L, N_RET_HEADS, RET_DK)
    k = k.reshape(B, L, N_RET_HEADS, RET_DK) * (RET_DK ** -0.5)
    if is_latent:
        q = _rotary_2d(q)
        k = _rotary_2d(k)
    q = q.transpose(0, 2, 1, 3)
    k = k.transpose(0, 2, 1, 3)
    v = v.reshape(B, L, N_RET_HEADS, RET_DV).transpose(0, 2, 1, 3)
    lg_f = -jnp.exp(ret_decay_fwd.astype(jnp.float32))
    lg_b = -jnp.exp(ret_decay_bwd.astype(jnp.float32))
    o, s_f, s_b = _bidir_retention(q, k, v, lg_f, lg_b, s0_f, s0_b)
    o = o * lax.rsqrt(jnp.mean(o * o, axis=-1, keepdims=True) + EPS)
    o = o.transpose(0, 2, 1, 3).reshape(B, L, RET_WIDTH) * ret_gn.astype(jnp.float32)
    ret_out = o.astype(x.dtype) * jax.nn.silu(gate)
    pool_out = _pool_mixer(u, pool_w, pool_scale)
    mix = jnp.concatenate([ret_out, pool_out], axis=-1) @ w_out
    h = h + g2[:, None, :] * mix

    h = h + 0.5 * g3[:, None, :] * _swiglu(_modulate(_rmsnorm(h, norm_ffn2), sh3, sc3), ffn2_w1, ffn2_w3, ffn2_w2)
    return h, s_f, s_b


def setup_inputs(seed: int = 0) -> dict:
    key = jax.random.key(seed)
    ks = jax.random.split(key, 32)
    D = D_MODEL
    f32 = jnp.float32

    def nrm(k, shape, scale):
        return jax.random.normal(k, shape, f32) * scale

    base_gamma = 1.0 - 2.0 ** (-5.0 - np.arange(N_RET_HEADS, dtype=np.float32))
    base_param = jnp.asarray(np.log(-np.log(base_gamma)), f32)
    st_shape = (DEC_BATCH, DEPTH, N_RET_HEADS, RET_DK, RET_DV)
    return {
        "x_prompt": nrm(ks[0], (BATCH, SEQ, D), 1.0),
        "x_sample": nrm(ks[1], (DEC_BATCH, DEC_SEQ, D), 1.0),
        "state_ret_fwd": nrm(ks[2], st_shape, 0.5),
        "state_ret_bwd": nrm(ks[3], st_shape, 0.5),
        "c": nrm(ks[4], (DEC_BATCH, D), 1.0),
        "c_ctx": nrm(ks[5], (D,), 1.0),
        "ada_w": nrm(ks[6], (DEPTH, D, N_MOD * D), 0.5 * D ** -0.5),
        "ada_b": nrm(ks[7], (DEPTH, N_MOD * D), 0.01),
        "norm_ffn1": 1.0 + nrm(ks[8], (DEPTH, D), 0.01),
        "ffn1_w1": nrm(ks[9], (DEPTH, D, D_FF), D ** -0.5),
        "ffn1_w3": nrm(ks[10], (DEPTH, D, D_FF), D ** -0.5),
        "ffn1_w2": nrm(ks[11], (DEPTH, D_FF, D), D_FF ** -0.5),
        "norm_mix": 1.0 + nrm(ks[12], (DEPTH, D), 0.01),
        "w_in": nrm(ks[13], (DEPTH, D, IN_WIDTH), D ** -0.5),
        "ret_decay_fwd": base_param[None, :] + nrm(ks[14], (DEPTH, N_RET_HEADS), 0.05),
        "ret_decay_bwd": base_param[None, :] + nrm(ks[15], (DEPTH, N_RET_HEADS), 0.05),
        "ret_gn": 1.0 + nrm(ks[16], (DEPTH, RET_WIDTH), 0.01),
        "pool_w": nrm(ks[17], (DEPTH, N_POOL_GROUPS, POOL_GROUP, POOL_GROUP), POOL_GROUP ** -0.5),
        "pool_scale": 1.0 + nrm(ks[18], (DEPTH, POOL_WIDTH), 0.02),
        "w_out": nrm(ks[19], (DEPTH, MIX_WIDTH, D), MIX_WIDTH ** -0.5),
        "norm_ffn2": 1.0 + nrm(ks[20], (DEPTH, D), 0.01),
        "ffn2_w1": nrm(ks[21], (DEPTH, D, D_FF), D ** -0.5),
        "ffn2_w3": nrm(ks[22], (DEPTH, D, D_FF), D ** -0.5),
        "ffn2_w2": nrm(ks[23], (DEPTH, D_FF, D), D_FF ** -0.5),
        "norm_final": 1.0 + nrm(ks[24], (D,), 0.01),
    }


def reference(x_prompt, x_sample, state_ret_fwd, state_ret_bwd, c, c_ctx, ada_w, ada_b, norm_ffn1,
              ffn1_w1, ffn1_w3, ffn1_w2, norm_mix, w_in, ret_decay_fwd, ret_decay_bwd, ret_gn, pool_w,
              pool_scale, w_out, norm_ffn2, ffn2_w1, ffn2_w3, ffn2_w2, norm_final):
    ctx = x_prompt
    lat = x_sample
    new_f = []
    new_b = []
    for l in range(DEPTH):
        p = (ada_w[l], ada_b[l], norm_ffn1[l], ffn1_w1[l], ffn1_w3[l], ffn1_w2[l], norm_mix[l], w_in[l],
             ret_decay_fwd[l], ret_decay_bwd[l], ret_gn[l], pool_w[l], pool_scale[l], w_out[l],
             norm_ffn2[l], ffn2_w1[l], ffn2_w3[l], ffn2_w2[l])
        zeros = jnp.zeros((ctx.shape[0], N_RET_HEADS, RET_DK, RET_DV), jnp.float32)
        ctx, s_f, s_b = _layer(ctx, c_ctx[None, :], False, zeros, zeros, *p)
        new_f.append(s_f.astype(x_prompt.dtype))
        new_b.append(s_b.astype(x_prompt.dtype))
        lat, _, _ = _layer(lat, c, True, state_ret_fwd[:, l], state_ret_bwd[:, l], *p)
    y_prompt = _rmsnorm(ctx, norm_final)
    y_sample = _rmsnorm(lat, norm_final)
    new_state_ret_fwd = jnp.stack(new_f, axis=1)
    new_state_ret_bwd = jnp.stack(new_b, axis=1)
    return (y_prompt, y_sample, new_state_ret_fwd, new_state_ret_bwd)
```

```python
import contextlib
import numpy as np
import concourse.bass as bass
import concourse.mybir as mybir
from concourse.bass_utils import run_bass_kernel_spmd

F32 = mybir.dt.float32
BF16 = mybir.dt.bfloat16
ALU = mybir.AluOpType
AF = mybir.ActivationFunctionType
ENGS = ("pe", "act", "dve", "pool", "sp")

D = 1024
DFF = 2816
NFT = 22
NT = 512
EPS = 1e-6
RING_SLOTS = 4
RING_ELEMS = 4096


class Reg:
    __slots__ = ("name", "last_w", "readers", "aliases", "lo", "hi", "psum")

    def __init__(self, name):
        self.name = name
        self.psum = False
        self.last_w = None
        self.readers = {}
        self.aliases = []
        self.lo = self.hi = None


class Tile:
    def __init__(self, ap, reg):
        self.ap = ap
        self.reg = reg


class FW:
    def __init__(self, nc, n_dma_sems=24, dry=False):
        self.nc = nc
        self.dry = dry
        self.ops = []
        self.stack = contextlib.ExitStack()
        self.n_dma_sems = n_dma_sems
        self.sb_regs = []
        self.tag = ""
        self.sb_ptr = nc.sbuf_base
        self.sb_top = nc.sbuf_top

    def reg(self, name):
        return Reg(name)

    def sbuf(self, name, shape, dtype, at=None):
        esz = 4 if dtype == F32 else 2
        nbytes = int(np.prod(shape[1:])) * esz
        if at is None:
            off = (self.sb_ptr + 31) // 32 * 32
            self.sb_ptr = off + nbytes
            assert self.sb_ptr <= self.sb_top, f"SBUF overflow at {name}: {self.sb_ptr} > {self.sb_top}"
        else:
            off = at
        t = self.nc.alloc_sbuf_tensor_at(name, list(shape), dtype, offset=off)
        r = Reg(name)
        r.lo, r.hi = off, off + nbytes
        for o in self.sb_regs:
            if o.lo < r.hi and r.lo < o.hi:
                o.aliases.append(r)
                r.aliases.append(o)
        self.sb_regs.append(r)
        return Tile(t, r)

    def reserve(self, nbytes):
        off = (self.sb_ptr + 31) // 32 * 32
        self.sb_ptr = off + nbytes
        assert self.sb_ptr <= self.sb_top, f"SBUF overflow (reserve): {self.sb_ptr} > {self.sb_top}"
        return off

    def psum(self, name, shape, dtype):
        t = self.stack.enter_context(self.nc.psum_tensor(name, list(shape), dtype))
        r = Reg(name)
        r.psum = True
        return Tile(t, r)

    def dram(self, name, shape, dtype, kind="ExternalInput", **kw):
        t = self.nc.dram_tensor(name, list(shape), dtype, kind=kind, **kw)
        return Tile(t.ap(), Reg(name))

    def view(self, ap, name):
        return Tile(ap, Reg(name))

    def _deps(self, idx, eng, is_dma, reads, writes):
        ps_reads = [t for t in reads if t.reg.psum]
        if ps_reads:
            reads = [t for t in reads if not t.reg.psum]
            writes = list(writes) + [t for t in ps_reads if all(t.reg is not w.reg for w in writes)]
        deps = set()
        for t in reads:
            r = t.reg
            for rr in [r] + r.aliases:
                if rr.last_w is not None:
                    deps.add(rr.last_w)
        for t in writes:
            r = t.reg
            for rr in [r] + r.aliases:
                if rr.last_w is not None:
                    deps.add(rr.last_w)
                deps.update(rr.readers.values())
        key = ("dma", idx) if is_dma else eng
        for t in reads:
            t.reg.readers[key] = idx
        for t in writes:
            t.reg.last_w = idx
            t.reg.readers = {}
        deps.discard(idx)
        return deps

    def op(self, eng, fn, reads=(), writes=()):
        idx = len(self.ops)
        deps = self._deps(idx, eng, False, reads, writes)
        self.ops.append(dict(eng=eng, fn=fn, deps=deps, is_dma=False, flag=False, tag=self.tag))

    def dma(self, eng, out, in_, reads=(), writes=(), **kw):
        idx = len(self.ops)
        deps = self._deps(idx, eng, True, reads, writes)
        self.ops.append(dict(eng=eng, fn=None, out=out, in_=in_, kw=kw, deps=deps, is_dma=True, flag=False))

    def custom_dma(self, eng, fn, reads=(), writes=(), inc=16):
        idx = len(self.ops)
        deps = self._deps(idx, eng, True, reads, writes)
        self.ops.append(dict(eng=eng, fn=fn, deps=deps, is_dma=True, flag=False, inc=inc))

    def emit(self):
        nc, ops = self.nc, self.ops
        for o in ops:
            comp, dmas = {}, []
            for d in o["deps"]:
                p = ops[d]
                if p["is_dma"]:
                    dmas.append(d)
                else:
                    if p["eng"] == "pe" and o["eng"] == "pe" and not o["is_dma"]:
                        continue
                    if p["eng"] not in comp or comp[p["eng"]] < d:
                        comp[p["eng"]] = d
            o["cdeps"] = comp
            o["ddeps"] = sorted(dmas)
            for d in comp.values():
                ops[d]["flag"] = True
        cnt = {e: 0 for e in ENGS}
        dma_i = {"hw": 0, "sw": 0}
        n_hw = self.n_dma_sems // 2
        sem_uses = [0] * self.n_dma_sems
        for o in ops:
            if o["is_dma"]:
                if o["fn"] is not None:
                    s = self.n_dma_sems - 1
                elif o["eng"] == "pool":
                    s = n_hw + dma_i["sw"] % (self.n_dma_sems - 1 - n_hw)
                    dma_i["sw"] += 1
                else:
                    s = dma_i["hw"] % n_hw
                    dma_i["hw"] += 1
                o["dsem"] = s
                o["dprev"] = sem_uses[s]
                sem_uses[s] += o.get("inc", 16)
                o["dval"] = sem_uses[s]
            elif o["flag"]:
                cnt[o["eng"]] += 1
                o["cnt"] = cnt[o["eng"]]
        st = self.stack
        esem = {e: st.enter_context(nc.semaphore(f"s_{e}")) for e in ENGS if e != "sp"}
        dsem = [st.enter_context(nc.semaphore(f"s_dma{k}")) for k in range(self.n_dma_sems)]
        block = st.enter_context(nc.Block())
        per_eng = {e: [] for e in ENGS}
        for i, o in enumerate(ops):
            per_eng[o["eng"]].append(i)
        self.stats = {e: len(v) for e, v in per_eng.items()}

        def run(eng_name, E):
            seen_c = {e: 0 for e in ENGS}
            seen_d = [0] * self.n_dma_sems
            for i in per_eng[eng_name]:
                o = ops[i]
                for pe_, d in o["cdeps"].items():
                    v = ops[d]["cnt"]
                    if seen_c[pe_] < v:
                        E.wait_ge(esem[pe_], v)
                        seen_c[pe_] = v
                for d in o["ddeps"]:
                    p = ops[d]
                    if seen_d[p["dsem"]] < p["dval"]:
                        E.wait_ge(dsem[p["dsem"]], p["dval"])
                        seen_d[p["dsem"]] = p["dval"]
                if o["is_dma"]:
                    s = o["dsem"]
                    if seen_d[s] < o["dprev"]:
                        E.wait_ge(dsem[s], o["dprev"])
                        seen_d[s] = o["dprev"]
                    if o["fn"] is None:
                        ins = E.dma_start(out=o["out"], in_=o["in_"], **o["kw"])
                    else:
                        ins = o["fn"](E)
                    ins.then_inc(dsem[s], o.get("inc", 16))
                else:
                    ins = o["fn"](E)
                    if o["flag"]:
                        ins.then_inc(esem[eng_name], 1)
            if eng_name == "sp":
                for s in range(self.n_dma_sems):
                    if sem_uses[s] > seen_d[s]:
                        E.wait_ge(dsem[s], sem_uses[s])
                for e in ENGS:
                    if e != "sp" and cnt[e] > 0:
                        E.wait_ge(esem[e], cnt[e])

        @block.tensor
        def _(E):
            run("pe", E)

        @block.scalar
        def _(E):
            run("act", E)

        @block.vector
        def _(E):
            run("dve", E)

        @block.gpsimd
        def _(E):
            run("pool", E)

        @block.sync
        def _(E):
            run("sp", E)

    def close(self):
        self.stack.close()


class Ring:
    def __init__(self, f, slots, seq, tiles):
        self.f = f
        self.slots = slots
        self.seq = seq
        self.tiles = tiles
        self.rec = []
        self.i = 0
        self.loaded = 0

    def get(self, name, apfn, k, n):
        i = self.i
        self.i += 1
        self.rec.append((name, apfn, k, n))
        S = len(self.slots)
        if self.seq is not None:
            while self.loaded < min(len(self.seq), i + S - 1):
                j = self.loaded
                nm, fn, kk, nn = self.seq[j]
                st = self.tiles[nm]
                slot = self.slots[j % S]
                dst = slot.ap[:, 0:kk * nn].rearrange("p (k n) -> p k n", n=nn)
                self.f.dma("pool", dst, fn(st.ap), reads=[st], writes=[slot])
                self.loaded += 1
        slot = self.slots[i % S]
        return slot, slot.ap[:, 0:k * n].rearrange("p (k n) -> p k n", n=n)


def record(nc, f, ring_seq, n_cores=8):
    dr = f.dram
    x_T = dr("x_T", [128, 8, 3072], F32)
    y_T = dr("y_T", [128, 8, 3072], F32, kind="ExternalOutput")
    nsf = dr("nsf", [4, 4, 128, 128], F32, kind="ExternalOutput")
    nsb = dr("nsb", [4, 4, 128, 128], F32, kind="ExternalOutput")
    condT_d = dr("condT", [128, 8 * (3 if n_cores >= 4 else 2)], F32)
    dec_d = dr("dec", [1, 16], F32)
    sinit_d = dr("s_init", [128, 512], F32)
    rot_d = dr("rot", [128, 2, 2048], F32)
    msel_d = dr("msel", [128, 2], F32)
    bands_d = dr("bands", [128, 9 * 512], F32)
    invc_d = dr("invcnt", [1, 2048], F32)
    ctab_d = dr("ctab", [128, 6 * 128 + 2], F32)
    ident_d = dr("ident", [128, 128], F32)
    rmat_d = dr("rmat", [128, 128], F32)
    vecs_d = dr("vecs", [128, 8 * 4 + 4 + 4 + 72], F32)
    AG = 4 if n_cores >= 4 else 2
    NCD = 3 if AG == 4 else 2
    CPC = 72 // AG
    ada_w = dr("ada_w", [D, CPC * 128], F32)
    csel_d = dr("csel", [128, 2], F32)
    mpay = dr("mpay", [128, CPC * NCD], F32, kind="Internal", addr_space="Local")
    mgat = dr("mgat", [AG * 128, CPC * NCD], F32, kind="Internal", addr_space="Local")
    w1 = [dr("ffn1_w1", [D, DFF], F32), dr("ffn2_w1", [D, DFF], F32)]
    w3 = [dr("ffn1_w3", [D, DFF], F32), dr("ffn2_w3", [D, DFF], F32)]
    w2 = [dr("ffn1_w2", [DFF, D], F32), dr("ffn2_w2", [DFF, D], F32)]
    w_in = dr("w_in", [D, 3584], F32)
    w_out = dr("w_out", [D, D], F32)
    pool_w_d = dr("pool_w", [128, 512], F32)
    w_in_c = dr("w_in_c", [D, 3584], BF16, kind="Internal")
    w_out_c = dr("w_out_c", [D, D], BF16, kind="Internal")
    hscr = dr("hscr", [4, 128, 8 * NT], F32, kind="Internal")
    sscr = dr("sscr", [4, 128, 512], F32, kind="Internal")
    ascr = dr("ascr", [2, 128, 8 * NT], BF16, kind="Internal")
    kvscr = dr("kvscr", [4, 3, 128, 2048], BF16, kind="Internal")
    pay = dr("pay", [256, 512], F32, kind="Internal", addr_space="Local")
    gat = dr("gat", [512, 512], F32, kind="Internal", addr_space="Local")

    sb = f.sbuf
    class Multi:
        def __init__(self, t, n):
            self.ap = t.ap
            self.k = [Tile(t.ap, Reg(f"{t.reg.name}_{j}")) for j in range(n)]
            self.all = list(self.k)

    hT = [Multi(sb(f"hT{i}", [128, 8, NT], F32), 8) for i in range(3)]
    slots = [sb(f"ring{i}", [128, RING_ELEMS], BF16) for i in range(RING_SLOTS)]
    wt = dict(ada_w=ada_w, ffn1_w1=w1[0], ffn2_w1=w1[1], ffn1_w3=w3[0], ffn2_w3=w3[1], ffn1_w2=w2[0], ffn2_w2=w2[1],
              w_in=w_in, w_out=w_out, w_in_c=w_in_c, w_out_c=w_out_c)
    ring = Ring(f, slots, ring_seq, wt)
    u_save = sb("u_save", [128, 8, 512], BF16)
    rotT = sb("rotT", [128, 2, 2048], BF16)
    bands = sb("bands", [128, 5, 4, 128], BF16)
    invc = sb("invc", [128, 2, 4, 128], F32)
    Dmask = sb("Dmask", [128, 2, 4, 128], F32)
    QD = sb("QD", [128, 2, 2, 4, 128], BF16)
    KD = sb("KD", [128, 16], F32)
    CDt = sb("CDt", [128, 16], F32)
    lgT = sb("lgT", [128, 16], F32)
    modT = Multi(sb("modT", [128, 72, 2], F32), 2)
    vecs = sb("vecs", [128, 112], F32)
    mder = Multi(sb("mder", [128, 2, 9, 8], F32), 6)
    condT = sb("condT", [128, 8 * NCD], F32)
    scT = sb("scT", [128, 8, NCD], BF16)
    G2 = sb("G2", [128, 72 * NCD], F32)
    csel = sb("csel", [128, 2], F32)
    ident = sb("ident", [128, 128], F32)
    identb = sb("identb", [128, 128], BF16)
    rmat = sb("rmat", [128, 128], BF16)
    ones_d = sb("ones_d", [128, 128], BF16)
    ones_v = sb("ones_v", [128, 128], BF16)
    epsb = sb("epsb", [128, 1], F32)
    msel = sb("msel", [128, 2], F32)
    poolw = sb("poolw", [128, 4, 128], BF16)
    S32f = Multi(sb("S32f", [128, 512], F32), 4)
    S32b = Multi(sb("S32b", [128, 512], F32), 4)
    S32t = Multi(sb("S32t", [128, 512], F32), 4)
    sq8 = None
    Sf_bf = sb("Sf_bf", [128, 4, 512], BF16)
    Sb_bf = sb("Sb_bf", [128, 4, 512], BF16)
    sst = [sb(f"sst{i}", [128, 512], F32) for i in range(2)]
    rstd = [sb(f"rstd{i}", [128, NT], F32) for i in range(2)]
    tmp32 = [sb(f"tmp32_{i}", [128, NT], F32) for i in range(3)]
    stmp = [sb(f"stmp{i}", [128, NT], BF16) for i in range(3)]
    ptb = [sb(f"ptb{i}", [128, 128], BF16) for i in range(12)]
    aT = [Multi(sb(f"aT{i}", [128, 8, NT], BF16), 8) for i in range(2)]
    SCR = f.reserve(36864)
    ctab = sb("ctab", [128, 6 * 128 + 2], F32, at=SCR)
    gT = [sb(f"gT{ft}", [128, NT], BF16, at=SCR + ft * 1024) for ft in range(NFT)]
    yT = [sb(f"yT{kc}", [128, NT], F32, at=SCR + kc * 2048) for kc in range(8)]
    qT = [sb(f"qT{h}", [128, NT], BF16, at=SCR + h * 1024) for h in range(4)]
    kT = [sb(f"kT{h}", [128, NT], BF16, at=SCR + 4096 + h * 1024) for h in range(4)]
    qdr = [sb(f"qdr{h}", [128, NT], BF16, at=SCR + 8192 + h * 1024) for h in range(4)] + [sb(f"qdr{4 + h}", [128, NT], BF16) for h in range(2)]
    sg = [sb(f"sg{h}", [128, NT], BF16, at=SCR + 12288 + h * 1024) for h in range(4)]
    v_tok = [sb(f"v_tok{c}", [128, 512], BF16, at=SCR + 16384 + c * 1024) for c in range(4)]
    u_tok = [sb(f"u_tok{c}", [128, 512], BF16, at=SCR + 20480 + c * 1024) for c in range(4)]
    dmT = [sb(f"dmT{g}", [128, NT], BF16, at=SCR + 24576 + g * 1024) for g in range(4)]
    kdf = [sb(f"kdf{c}", [128, 512], BF16, at=SCR + 28672 + c * 1024) for c in range(4)]
    kdb = [sb(f"kdb{c}", [128, 512], BF16, at=SCR + 32768 + c * 1024) for c in range(4)]
    sq8 = [sb(f"sq8_{i}", [128, 8, NT], BF16, at=SCR + i * 8192) for i in range(2)]
    G_S = sb("G_S", [128, 2, 512], F32, at=SCR + 24576)
    G_U = sb("G_U", [128, 2, 512], F32, at=SCR + 8192)
    u_halo = sb("u_halo", [128, 512], BF16)


    PB = [f.psum(f"pb{i}", [128, NT], F32) for i in range(8)]
    PQ = [Tile(p.ap[:, 0:128], p.reg) for p in PB]
    PSB = [Tile(p.ap[:, 0:256].bitcast(BF16), p.reg) for p in PB]
    ctr = dict(pb=0, pq=0, psb=0, t32=0, st=0, ptb=0, rs=0, xs=0, ys=0, sst=0, qd=0, ssp=0)

    busy_banks = set()

    def nxt(key, lst):
        if key in ("pq", "psb", "pb"):
            while (ctr["pb"] % 8) in busy_banks:
                ctr["pb"] += 1
            v = lst[ctr["pb"] % 8]
            ctr["pb"] += 1
            return v
        v = lst[ctr[key] % len(lst)]
        ctr[key] += 1
        return v

    class Stats:
        def __init__(self, n=8, ones=None):
            while (ctr["pb"] % 8) in busy_banks:
                ctr["pb"] += 1
            self.idx = ctr["pb"] % 8
            ctr["pb"] += 1
            busy_banks.add(self.idx)
            self.pb = PB[self.idx]
            self.n = n
            self.i = 0
            self.ones = ones if ones is not None else ones_d

        def add(self, src_t, src_ap):
            s_ = nxt("st", stmp)
            act(s_, s_.ap[:], src_ap, AF.Square, [src_t])
            i = self.i
            self.i += 1
            return lambda: mm(self.pb, self.pb.ap[:], self.ones.ap[:], s_.ap[:], i == 0, i == self.n - 1, [self.ones, s_])

        def finish(self):
            r = nxt("rs", rstd)
            act(r, r.ap[:], self.pb.ap[:], AF.Ln, [self.pb, epsb], bias=epsb.ap[:, 0:1], scale=1.0)
            act(r, r.ap[:], r.ap[:], AF.Exp, [r], scale=-0.5)
            busy_banks.discard(self.idx)
            return r

    op, dma = f.op, f.dma
    import os as _os
    DBG = bool(_os.environ.get("K_DEBUG"))

    def dump(name, t, shape, ap=None):
        if not DBG:
            return
        d_ = dr("dbg_" + name, list(shape), F32 if True else None, kind="ExternalOutput")
        src = ap if ap is not None else t.ap[:]
        if len(shape) == 3:
            dst = d_.ap[:, :, :]
        else:
            dst = d_.ap[:, :]
        dma("pool", dst, src, reads=[t], writes=[d_])

    def wl(t):
        return t if isinstance(t, list) else [t]

    def mm(out_t, out_ap, lhsT, rhs, start, stop, reads):
        op("pe", lambda E: E.matmul(out_ap, lhsT=lhsT, rhs=rhs, start=start, stop=stop), reads=reads, writes=wl(out_t))

    def act(out_t, out_ap, in_ap, func, reads, scale=None, bias=None):
        kw = {}
        if scale is not None:
            kw["scale"] = scale
        if bias is not None:
            kw["bias"] = bias
        op("act", lambda E: E.activation(out=out_ap, in_=in_ap, func=func, **kw), reads=reads, writes=wl(out_t))

    def tt(out_t, out_ap, in0, in1, alu, reads, eng="dve"):
        op(eng, lambda E: E.tensor_tensor(out=out_ap, in0=in0, in1=in1, op=alu), reads=reads, writes=wl(out_t))

    def stt(out_t, out_ap, in0, scalar, in1, op0, op1, reads):
        op("dve", lambda E: E.scalar_tensor_tensor(out=out_ap, in0=in0, scalar=scalar, in1=in1, op0=op0, op1=op1),
           reads=reads, writes=wl(out_t))

    def ts(out_t, out_ap, in0, s1, op0, reads, s2=None, op1=None, eng="dve"):
        if op1 is None:
            op(eng, lambda E: E.tensor_scalar(out=out_ap, in0=in0, scalar1=s1, scalar2=None, op0=op0), reads=reads, writes=wl(out_t))
        else:
            op(eng, lambda E: E.tensor_scalar(out=out_ap, in0=in0, scalar1=s1, scalar2=s2, op0=op0, op1=op1), reads=reads, writes=wl(out_t))

    dma("sp", condT.ap[:], condT_d.ap[:, :], reads=[condT_d], writes=[condT])
    dma("sp", vecs.ap[:], vecs_d.ap[:, :], reads=[vecs_d], writes=[vecs])
    dma("sp", ident.ap[:], ident_d.ap[:, :], reads=[ident_d], writes=[ident])
    dma("sp", ctab.ap[:], ctab_d.ap[:, :], reads=[ctab_d], writes=[ctab])
    dma("sp", lgT.ap[:], dec_d.ap[0:1, :].partition_broadcast(128), reads=[dec_d], writes=[lgT])
    dma("sp", msel.ap[:], msel_d.ap[:, :], reads=[msel_d], writes=[msel])
    dma("sp", invc.ap[:].rearrange("p a g t -> p (a g t)"), invc_d.ap[0:1, 0:1024].partition_broadcast(128), reads=[invc_d], writes=[invc])
    dma("sp", S32f.ap[:], sinit_d.ap[:, :], reads=[sinit_d], writes=S32f.all)
    op("dve", lambda E: E.memset(epsb.ap[:], EPS), writes=[epsb])
    op("dve", lambda E: E.memset(ones_d.ap[:], 1.0 / D), writes=[ones_d])
    op("dve", lambda E: E.memset(ones_v.ap[:], 1.0 / 128.0), writes=[ones_v])
    op("dve", lambda E: E.tensor_copy(out=identb.ap[:], in_=ident.ap[:]), reads=[ident], writes=[identb])
    act(lgT, lgT.ap[:], lgT.ap[:], AF.Exp, [lgT])
    ts(lgT, lgT.ap[:], lgT.ap[:], -1.0, ALU.mult, [lgT])
    REL1, M1s, REL2, M2s, IDX1, IDXR = [ctab.ap[:, i * 128:(i + 1) * 128] for i in range(6)]
    kidx = ctab.ap[:, 768:770]
    SC = 128.0 ** -0.5
    for st_ in range(2):
        for h in range(4):
            cf = st_ * 8 + h
            cb = st_ * 8 + 4 + h
            t1 = nxt("t32", tmp32)
            act(t1, t1.ap[:, 0:128], REL1, AF.Exp, [ctab, lgT], scale=lgT.ap[:, cf:cf + 1])
            tt(t1, t1.ap[:, 0:128], t1.ap[:, 0:128], M1s, ALU.mult, [t1, ctab])
            t2 = nxt("t32", tmp32)
            act(t2, t2.ap[:, 0:128], REL2, AF.Exp, [ctab, lgT], scale=lgT.ap[:, cb:cb + 1])
            tt(t2, t2.ap[:, 0:128], t2.ap[:, 0:128], M2s, ALU.mult, [t2, ctab])
            tt(Dmask, Dmask.ap[:, st_, h, :], t1.ap[:, 0:128], t2.ap[:, 0:128], ALU.add, [t1, t2])
            act(QD, QD.ap[:, st_, 0, h, :], IDX1, AF.Exp, [ctab, lgT], scale=lgT.ap[:, cf:cf + 1])
            act(QD, QD.ap[:, st_, 1, h, :], IDXR, AF.Exp, [ctab, lgT], scale=lgT.ap[:, cb:cb + 1])
            act(KD, KD.ap[:, cf:cf + 1], kidx[:, 0:1], AF.Exp, [ctab, lgT], scale=lgT.ap[:, cf:cf + 1])
            act(KD, KD.ap[:, cb:cb + 1], kidx[:, 1:2], AF.Exp, [ctab, lgT], scale=lgT.ap[:, cb:cb + 1])
    ts(KD, KD.ap[:], KD.ap[:], SC, ALU.mult, [KD])
    act(CDt, CDt.ap[:], lgT.ap[:], AF.Exp, [lgT], scale=128.0)

    nfin = vecs.ap[:, 24:32]
    gn = vecs.ap[:, 32:36]
    pscale = vecs.ap[:, 36:40]

    def load_x(tok0, h):
        f.tag = "load_x"
        dma("sp", h.ap[:], x_T.ap[:, :, tok0:tok0 + NT], reads=[x_T], writes=h.all)

    def rms_rstd(src_tiles, src_aps, ones, n):
        f.tag = "rms_rstd"
        pb = nxt("pb", PB)
        for i in range(n):
            s = nxt("st", stmp)
            act(s, s.ap[:], src_aps[i], AF.Square, [src_tiles[i]])
            mm(pb, pb.ap[:], ones.ap[:], s.ap[:], i == 0, i == n - 1, [ones, s])
        r = nxt("rs", rstd)
        act(r, r.ap[:], pb.ap[:], AF.Ln, [pb, epsb], bias=epsb.ap[:, 0:1], scale=1.0)
        act(r, r.ap[:], r.ap[:], AF.Exp, [r], scale=-0.5)
        return r

    def h_sq(h, slot):
        f.tag = "rms_rstd"
        for half in range(2):
            act(sq8[slot], sq8[slot].ap[:, half * 4:(half + 1) * 4, :], h.ap[:, half * 4:(half + 1) * 4, :], AF.Square, h.k[half * 4:(half + 1) * 4])

    def h_rstd(h, slot):
        f.tag = "rms_rstd"
        pb = nxt("pb", PB)
        for kc in range(8):
            mm(pb, pb.ap[:], ones_d.ap[:], sq8[slot].ap[:, kc, :], kc == 0, kc == 7, [ones_d, sq8[slot]])
        r = nxt("rs", rstd)
        act(r, r.ap[:], pb.ap[:], AF.Ln, [pb, epsb], bias=epsb.ap[:, 0:1], scale=1.0)
        act(r, r.ap[:], r.ap[:], AF.Exp, [r], scale=-0.5)
        return r

    def norm_mod(h, a, cond, m, stats=None, slot=0):
        r = stats.finish() if stats is not None else h_rstd(h, slot)
        f.tag = "norm_mod"
        for kc in range(8):
            t = nxt("t32", tmp32)
            tt(t, t.ap[:], h.ap[:, kc, :], r.ap[:], ALU.mult, [h.k[kc], r], eng=("pool" if kc in (2, 5) else "dve"))
            act(a.k[kc], a.ap[:, kc, :], t.ap[:], AF.Identity, [t, mder.k[cond * 3 + m]],
                scale=mder.ap[:, cond, 3 * m, kc:kc + 1], bias=mder.ap[:, cond, 3 * m + 1, kc:kc + 1])

    fills = []
    for c0_ in range(0, 2560, 512):
        fills.append((w_in_c, w_in, c0_))
    for c0_ in range(0, 1024, 512):
        fills.append((w_out_c, w_out, c0_))

    def fill_one():
        if fills:
            dst_, src_, c0_ = fills.pop(0)
            dma("pool", dst_.ap[:, c0_:c0_ + 512], src_.ap[:, c0_:c0_ + 512], reads=[src_], writes=[dst_])

    def ffn_group(tiles, cond, m, which, want_stats=True):
        f.tag = "ffn_group"
        stats = None
        pend = []
        n1, n3, n2 = [f"ffn{which + 1}_w{x}" for x in (1, 3, 2)]
        for half in range(2):
            ft0 = half * 11
            col, rem = ft0 * 128, 11 * 128
            while rem > 0:
                ncol = min(512, rem)
                cut = lambda a_, col=col, ncol=ncol: a_.rearrange("(kc p) n -> p kc n", p=128)[:, :, col: col + ncol]
                s1, v1 = ring.get(n1, cut, 8, ncol)
                fill_one()
                s3, v3 = ring.get(n3, cut, 8, ncol)
                fill_one()
                for j in range(ncol // 128):
                    ftl = (col - ft0 * 128) // 128 + j
                    for ti, (h, a) in enumerate(tiles):
                        p1 = nxt("pb", PB)
                        for kc in range(8):
                            mm(p1, p1.ap[:], v1[:, kc, j * 128:(j + 1) * 128], a.ap[:, kc, :], kc == 0, kc == 7, [s1, a.k[kc]])
                        p3 = nxt("pb", PB)
                        for kc in range(8):
                            mm(p3, p3.ap[:], v3[:, kc, j * 128:(j + 1) * 128], a.ap[:, kc, :], kc == 0, kc == 7, [s3, a.k[kc]])
                        s = nxt("st", stmp)
                        act(s, s.ap[:], p1.ap[:], AF.Silu, [p1])
                        g = gT[ti * 11 + ftl]
                        tt(g, g.ap[:], p3.ap[:], s.ap[:], ALU.mult, [p3, s])
                    if ftl % 2 == 1:
                        bg()
                        f.tag = "ffn_group"
                col += ncol
                rem -= ncol
            if half == 1 and want_stats:
                stats = [Stats() for _ in tiles]
            for d2 in range(4):
                s2, v2 = ring.get(n2, lambda a_, ft0=ft0, d2=d2: a_.rearrange("(ft p) n -> p ft n", p=128)[:, ft0:ft0 + 11, d2 * 256:(d2 + 1) * 256], 11, 256)
                for dj in range(2):
                    dc = d2 * 2 + dj
                    for ti, (h, a) in enumerate(tiles):
                        py = nxt("pb", PB)
                        for ftl in range(11):
                            g = gT[ti * 11 + ftl]
                            mm(py, py.ap[:], v2[:, ftl, dj * 128:(dj + 1) * 128], g.ap[:], ftl == 0, ftl == 10, [s2, g])
                        stt(h.k[dc], h.ap[:, dc, :], py.ap[:], mder.ap[:, cond, 3 * m + 2, dc:dc + 1], h.ap[:, dc, :], ALU.mult, ALU.add, [py, mder.k[cond * 3 + m], h.k[dc]])
                        if half == 1 and want_stats:
                            nxt_pend = stats[ti].add(h.k[dc], h.ap[:, dc, :])
                            for p_ in pend:
                                p_()
                            pend = [nxt_pend]
                            f.tag = "ffn_group"
        for p_ in pend:
            p_()
        return stats

    bgq = []

    def bg():
        if bgq:
            tg = f.tag
            bgq.pop(0)()
            f.tag = tg

    def w_unit(c0):
        while fills:
            fill_one()
        return ring.get("w_in_c", lambda a_, c0=c0: a_.rearrange("(kc p) n -> p kc n", p=128)[:, :, c0:c0 + 512], 8, 512)

    def proj_fm(a, c0, c0_sw, outs, is_sample, tok_off, silu=False):
        f.tag = "proj_fm"
        s, v = w_unit(c0)
        rot = c0_sw is not None and is_sample
        pend = None
        for h in range(4):
            p = nxt("pb", PB)
            for kc in range(8):
                mm(p, p.ap[:], v[:, kc, h * 128:(h + 1) * 128], a.ap[:, kc, :], kc == 0, kc == 7, [s, a.k[kc]])
            if rot:
                p_idx = PB.index(p)
                busy_banks.add(p_idx)
                qb = nxt("st", stmp)
                act(qb, qb.ap[:], p.ap[:], AF.Copy, [p])

                def fin(h=h, p=p, p_idx=p_idx, qb=qb):
                    p2 = nxt("pb", PB)
                    mm(p2, p2.ap[:], rmat.ap[:], qb.ap[:], True, True, [rmat, qb])
                    t1 = nxt("t32", tmp32)
                    tt(t1, t1.ap[:], p.ap[:], rotT.ap[:, 0, tok_off:tok_off + NT], ALU.mult, [p, rotT])
                    busy_banks.discard(p_idx)
                    t2 = nxt("t32", tmp32)
                    tt(t2, t2.ap[:], p2.ap[:], rotT.ap[:, 1, tok_off:tok_off + NT], ALU.mult, [p2, rotT])
                    tt(outs[h], outs[h].ap[:], t1.ap[:], t2.ap[:], ALU.add, [t1, t2], eng="pool")

                if pend is not None:
                    pend()
                pend = fin
            else:
                act(outs[h], outs[h].ap[:], p.ap[:], AF.Silu if silu else AF.Copy, [p])
            bg()
        if pend is not None:
            pend()

    def proj_tm(a, c0, outs, chunks):
        f.tag = "proj_tm"
        s, v = w_unit(c0)
        for c in chunks:
            p = nxt("pb", PB)
            for kc in range(8):
                mm(p, p.ap[:], a.ap[:, kc, c * 128:(c + 1) * 128], v[:, kc, :], kc == 0, kc == 7, [s, a.k[kc]])
            act(outs[c], outs[c].ap[:], p.ap[:], AF.Copy, [p])
            bg()

    def k_tok_decay(st_, want_f, want_b):
        f.tag = "k_tok_decay"
        for c in range(4):
            pb_ = nxt("psb", PSB)
            for h in range(4):
                op("pe", lambda E, o=pb_.ap[:, h * 128:(h + 1) * 128], i=kT[h].ap[:, c * 128:(c + 1) * 128]: E.transpose(o, i, identb.ap[:]),
                   reads=[kT[h], identb], writes=[pb_])
            for h in range(4):
                hs = slice(h * 128, (h + 1) * 128)
                if want_f:
                    act(kdf[c], kdf[c].ap[:, hs], pb_.ap[:, hs], AF.Copy, [pb_, KD], scale=KD.ap[:, st_ * 8 + h: st_ * 8 + h + 1])
                if want_b:
                    ts(kdb[c], kdb[c].ap[:, hs], pb_.ap[:, hs], KD.ap[:, st_ * 8 + 4 + h: st_ * 8 + 4 + h + 1], ALU.mult, [pb_, KD])

    def kv_mm(kd, c, h):
        f.tag = "kv_mm"
        pq = nxt("pq", PQ)
        mm(pq, pq.ap, kd[c].ap[:, h * 128:(h + 1) * 128], v_tok[c].ap[:, h * 128:(h + 1) * 128], True, True, [kd[c], v_tok[c]])
        return pq

    def scan_step(S32, kd, c, h, cdcol):
        pq = kv_mm(kd, c, h)
        hs = slice(h * 128, (h + 1) * 128)
        stt(S32.k[h], S32.ap[:, hs], S32.ap[:, hs], CDt.ap[:, cdcol:cdcol + 1], pq.ap, ALU.mult, ALU.add, [S32.k[h], CDt, pq])

    def scan_chunk(S32, kd, c, cd0, pre=None):
        f.tag = "kv_mm"
        pb_ = nxt("pb", PB)
        for h in range(4):
            hs = slice(h * 128, (h + 1) * 128)
            mm(pb_, pb_.ap[:, hs], kd[c].ap[:, hs], v_tok[c].ap[:, hs], True, True, [kd[c], v_tok[c]])
        for h in range(4):
            hs = slice(h * 128, (h + 1) * 128)
            if pre is not None:
                pre(h, hs)
            stt(S32.k[h], S32.ap[:, hs], S32.ap[:, hs], CDt.ap[:, cd0 + h:cd0 + h + 1], pb_.ap[:, hs], ALU.mult, ALU.add, [S32.k[h], CDt, pb_])

    def retention_out(mix, st_, have_f, have_b):
        st = {}

        def s1(h):
            f.tag = "retention_out"
            qf = nxt("qd", qdr)
            qb = nxt("qd", qdr)
            for c in range(4):
                cs = slice(c * 128, (c + 1) * 128)
                if have_f[c]:
                    tt(qf, qf.ap[:, cs], qT[h].ap[:, cs], QD.ap[:, st_, 0, h, :], ALU.mult, [qT[h], QD], eng="pool")
                if have_b[c]:
                    tt(qb, qb.ap[:, cs], qT[h].ap[:, cs], QD.ap[:, st_, 1, h, :], ALU.mult, [qT[h], QD], eng="pool")
            pts = []
            for c in range(4):
                cs = slice(c * 128, (c + 1) * 128)
                pq = nxt("pq", PQ)
                mm(pq, pq.ap, kT[h].ap[:, cs], qT[h].ap[:, cs], True, True, [kT[h], qT[h]])
                pt = nxt("ptb", ptb)
                tt(pt, pt.ap[:], pq.ap, Dmask.ap[:, st_, h, :], ALU.mult, [pq, Dmask])
                pts.append(pt)
            st[h] = dict(qf=qf, qb=qb, pts=pts)

        def s2(h):
            f.tag = "retention_out"
            hs = slice(h * 128, (h + 1) * 128)
            d_ = st[h]
            po = nxt("pb", PB)
            po_idx = PB.index(po)
            busy_banks.add(po_idx)
            for c in range(4):
                cs = slice(c * 128, (c + 1) * 128)
                last = not (have_f[c] or have_b[c])
                mm(po, po.ap[:, cs], v_tok[c].ap[:, hs], d_["pts"][c].ap[:], True, last, [v_tok[c], d_["pts"][c]])
                if have_f[c]:
                    mm(po, po.ap[:, cs], Sf_bf.ap[:, c, hs], d_["qf"].ap[:, cs], False, not have_b[c], [Sf_bf, d_["qf"]])
                if have_b[c]:
                    mm(po, po.ap[:, cs], Sb_bf.ap[:, c, hs], d_["qb"].ap[:, cs], False, True, [Sb_bf, d_["qb"]])
            s_ = nxt("st", stmp)
            act(s_, s_.ap[:], po.ap[:], AF.Square, [po])
            d_.update(po=po, po_idx=po_idx, s_=s_)

        def s3(h):
            f.tag = "ret_epi"
            d_ = st[h]
            po, s_ = d_["po"], d_["s_"]
            p2 = nxt("pb", PB)
            mm(p2, p2.ap[:], ones_v.ap[:], s_.ap[:], True, True, [ones_v, s_])
            r = nxt("rs", rstd)
            act(r, r.ap[:], p2.ap[:], AF.Ln, [p2, epsb], bias=epsb.ap[:, 0:1], scale=1.0)
            act(r, r.ap[:], r.ap[:], AF.Exp, [r], scale=-0.5)
            t = nxt("t32", tmp32)
            stt(t, t.ap[:], po.ap[:], gn[:, h:h + 1], r.ap[:], ALU.mult, ALU.mult, [po, vecs, r])
            busy_banks.discard(d_["po_idx"])
            tt(mix.k[h], mix.ap[:, h, :], t.ap[:], sg[h].ap[:], ALU.mult, [t, sg[h]], eng="pool")

        s1(0)
        s1(1)
        s2(0)
        s1(2)
        s2(1)
        s3(0)
        s1(3)
        s2(2)
        s3(1)
        s2(3)
        s3(2)
        s3(3)

    def pool_mix(mix, srcs, cats):
        def band(g):
            f.tag = "pool_mix"
            gs_ = slice(g * 128, (g + 1) * 128)
            pb_ = nxt("pb", PB)
            for c in range(4):
                n = len(srcs[c])
                for i, (ut, bi) in enumerate(srcs[c]):
                    mm(pb_, pb_.ap[:, c * 128:(c + 1) * 128], ut.ap[:, gs_], bands.ap[:, bi, g, :], i == 0, i == n - 1, [ut, bands])
            for c in range(4):
                cs = slice(c * 128, (c + 1) * 128)
                tt(dmT[g], dmT[g].ap[:, cs], pb_.ap[:, cs], invc.ap[:, cats[c], g, :], ALU.mult, [pb_, invc])

        def proj(g):
            f.tag = "pool_mix"
            p = nxt("pb", PB)
            mm(p, p.ap[:], poolw.ap[:, g, :], dmT[g].ap[:], True, True, [poolw, dmT[g]])
            act(mix.k[4 + g], mix.ap[:, 4 + g, :], p.ap[:], AF.Copy, [p, vecs], scale=pscale[:, g:g + 1])

        band(0)
        band(1)
        proj(0)
        band(2)
        proj(1)
        band(3)
        proj(2)
        proj(3)

    def w_out_stage(h, mix, cond):
        f.tag = "w_out_stage"
        stats = Stats()
        pend = []
        for half in range(2):
            s, v = ring.get("w_out_c", lambda a_, half=half: a_.rearrange("(kc p) n -> p kc n", p=128)[:, :, half * 512:(half + 1) * 512], 8, 512)
            for j in range(4):
                dc = half * 4 + j
                p = nxt("pb", PB)
                for kc in range(8):
                    mm(p, p.ap[:], v[:, kc, j * 128:(j + 1) * 128], mix.ap[:, kc, :], kc == 0, kc == 7, [s, mix.k[kc]])
                stt(h.k[dc], h.ap[:, dc, :], p.ap[:], mder.ap[:, cond, 5, dc:dc + 1], h.ap[:, dc, :], ALU.mult, ALU.add, [p, mder.k[cond * 3 + 1], h.k[dc]])
                nxt_pend = stats.add(h.k[dc], h.ap[:, dc, :])
                for p_ in pend:
                    p_()
                pend = [nxt_pend]
        for p_ in pend:
            p_()
        return stats

    def final_out(h, tok0, stats):
        r = stats.finish()
        f.tag = "final_out"
        for kc in range(8):
            stt(h.k[kc], h.ap[:, kc, :], h.ap[:, kc, :], nfin[:, kc:kc + 1], r.ap[:], ALU.mult, ALU.mult, [h.k[kc], vecs, r])
        dma("sp", y_T.ap[:, :, tok0:tok0 + NT], h.ap[:], reads=h.all, writes=[y_T])

    def prompt_mixer(pt_, h, a):
        proj_fm(a, 512, 3072, kT, False, 0)
        k_tok_decay(0, True, True)
        proj_tm(a, 1024, v_tok, range(4))
        proj_fm(a, 0, 2560, qT, False, 0)
        proj_fm(a, 1536, None, sg, False, 0, silu=True)
        proj_tm(a, 2048, u_tok, range(4))
        for sq_ in range(2):
            c0, c1 = 2 * sq_, 2 * sq_ + 1
            seq = pt_ * 2 + sq_
            stf = nxt("sst", sst)
            stb = nxt("sst", sst)
            for hh in range(4):
                hs = slice(hh * 128, (hh + 1) * 128)
                pq = kv_mm(kdf, c0, hh)
                act(Sf_bf, Sf_bf.ap[:, c1, hs], pq.ap, AF.Copy, [pq])
                op("dve", lambda E, o=S32t.ap[:, hs], i=pq.ap: E.tensor_copy(out=o, in_=i), reads=[pq], writes=[S32t.k[hh]])
                pq2 = kv_mm(kdf, c1, hh)
                stt(stf, stf.ap[:, hs], S32t.ap[:, hs], CDt.ap[:, hh:hh + 1], pq2.ap, ALU.mult, ALU.add, [S32t.k[hh], CDt, pq2])
                pq3 = kv_mm(kdb, c1, hh)
                act(Sb_bf, Sb_bf.ap[:, c0, hs], pq3.ap, AF.Copy, [pq3])
                op("dve", lambda E, o=S32b.ap[:, hs], i=pq3.ap: E.tensor_copy(out=o, in_=i), reads=[pq3], writes=[S32b.k[hh]])
                pq4 = kv_mm(kdb, c0, hh)
                stt(stb, stb.ap[:, hs], S32b.ap[:, hs], CDt.ap[:, 4 + hh:5 + hh], pq4.ap, ALU.mult, ALU.add, [S32b.k[hh], CDt, pq4])
            dma("sp", nsf.ap[seq].rearrange("h d v -> d h v"), stf.ap[:].rearrange("p (h v) -> p h v", v=128), reads=[stf], writes=[nsf])
            dma("sp", nsb.ap[seq].rearrange("h d v -> d h v"), stb.ap[:].rearrange("p (h v) -> p h v", v=128), reads=[stb], writes=[nsb])
        retention_out(a, 0, [False, True, False, True], [True, False, True, False])
        pool_mix(a, [[(u_tok[0], 0), (u_tok[1], 1)], [(u_tok[1], 2), (u_tok[0], 3)],
                     [(u_tok[2], 0), (u_tok[3], 1)], [(u_tok[3], 2), (u_tok[2], 3)]], [0, 1, 0, 1])
        st2 = w_out_stage(h, a, 0)
        norm_mod(h, a, 0, 2, st2)

    H1 = {0: hT[2], 1: hT[0], 2: hT[1], 3: hT[2]}
    H2 = {3: hT[2], 2: hT[1], 1: hT[0], 0: hT[2]}
    load_x(0, hT[0])
    h_sq(hT[0], 0)
    dma("sp", csel.ap[:], csel_d.ap[:, :], reads=[csel_d], writes=[csel])
    act(scT, scT.ap[:].rearrange("p k c -> p (k c)"), condT.ap[:], AF.Silu, [condT])
    modps = nxt("pb", PB)
    col = 0
    while col < CPC * 128:
        ncol = min(512, CPC * 128 - col)
        slot, wv = ring.get("ada_w", lambda a_, col=col, ncol=ncol: a_.rearrange("(kc p) n -> p kc n", p=128)[:, :, col:col + ncol], 8, ncol)
        for j4 in range(ncol // 128):
            j = col // 128 + j4
            for kc in range(8):
                mm(modps, modps.ap[:, NCD * j:NCD * j + NCD], wv[:, kc, j4 * 128:(j4 + 1) * 128], scT.ap[:, kc, :], kc == 0, kc == 7, [slot, scT])
        col += ncol
    mloc = nxt("t32", tmp32)
    op("dve", lambda E: E.tensor_copy(out=mloc.ap[:, 0:CPC * NCD], in_=modps.ap[:, 0:CPC * NCD]), reads=[modps], writes=[mloc])
    dma("sp", mpay.ap[:, :], mloc.ap[:, 0:CPC * NCD], reads=[mloc], writes=[mpay])
    load_x(NT, hT[1])
    h_sq(hT[1], 1)
    load_x(1024, H1[0])
    f.custom_dma("pool", lambda E: E.collective_compute("AllGather", ALU.bypass, replica_groups=[list(range(AG * g, AG * g + AG)) for g in range(n_cores // AG)],
                                                        ins=[mpay.ap[:, :]], outs=[mgat.ap[:, :]]), reads=[mpay], writes=[mgat], inc=1)
    dma("pool", rotT.ap[:], rot_d.ap[:, :, :], reads=[rot_d], writes=[rotT])
    dma("pool", rmat.ap[:], rmat_d.ap[:, :], reads=[rmat_d], writes=[rmat])
    dma("pool", bands.ap[:, 0:4].rearrange("p a g t -> p (a g t)"), bands_d.ap[:, 0:2048], reads=[bands_d], writes=[bands])
    dma("pool", poolw.ap[:].rearrange("p g t -> p (g t)"), pool_w_d.ap[:, :], reads=[pool_w_d], writes=[poolw])
    dma("sp", G2.ap[:].rearrange("p (r n) -> p r n", r=AG), mgat.ap.rearrange("(r p) n -> p r n", p=128), reads=[mgat], writes=[G2])
    G2v = G2.ap[:].rearrange("p (j c) -> p j c", c=NCD)
    adab = vecs.ap[:, 40:112]
    tt(modT.k[0], modT.ap[:, :, 0], G2v[:, :, 0], adab, ALU.add, [G2, vecs])
    if NCD == 2:
        tt(modT.k[1], modT.ap[:, :, 1], G2v[:, :, 1], adab, ALU.add, [G2, vecs])
    else:
        tsel = nxt("t32", tmp32)
        ts(tsel, tsel.ap[:, 0:72], G2v[:, :, 1], csel.ap[:, 0:1], ALU.mult, [G2, csel])
        stt(tsel, tsel.ap[:, 0:72], G2v[:, :, 2], csel.ap[:, 1:2], tsel.ap[:, 0:72], ALU.mult, ALU.add, [G2, csel, tsel])
        tt(modT.k[1], modT.ap[:, :, 1], tsel.ap[:, 0:72], adab, ALU.add, [tsel, vecs])
    for c in range(2):
        for m in range(3):
            shj, scj, gj = 3 * m, 3 * m + 1, 3 * m + 2
            nrm = vecs.ap[:, m * 8:(m + 1) * 8]
            stt(mder.k[c * 3 + m], mder.ap[:, c, 3 * m + 0, :], modT.ap[:, scj * 8:(scj + 1) * 8, c], 1.0, nrm, ALU.add, ALU.mult, [modT.k[c], vecs])
            ts(mder.k[c * 3 + m], mder.ap[:, c, 3 * m + 1, :], modT.ap[:, shj * 8:(shj + 1) * 8, c], 1.0, ALU.mult, [modT.k[c]])
            ts(mder.k[c * 3 + m], mder.ap[:, c, 3 * m + 2, :], modT.ap[:, gj * 8:(gj + 1) * 8, c], (1.0 if m == 1 else 0.5), ALU.mult, [modT.k[c]])
    dump("modT", modT.k[0], [128, 72, 2], modT.ap[:])
    dump("mder", mder.k[0], [128, 2 * 9 * 8], mder.ap[:].rearrange("p a b c -> p (a b c)"))
    dump("Dmask", Dmask, [128, 8 * 128], Dmask.ap[:].rearrange("p a b c -> p (a b c)"))
    dump("QD", QD, [128, 16 * 128], QD.ap[:].rearrange("p a b c d -> p (a b c d)"))
    dump("KD", KD, [128, 16])
    dump("CDt", CDt, [128, 16])
    dump("lgT", lgT, [128, 16])
    grp = [(hT[0], aT[0]), (hT[1], aT[1])]
    for pt_ in range(2):
        norm_mod(hT[pt_], aT[pt_], 0, 0, slot=pt_)
    st1 = ffn_group(grp, 0, 0, 0)
    for pt_ in range(2):
        norm_mod(hT[pt_], aT[pt_], 0, 1, st1[pt_])
    for pt_ in range(2):
        prompt_mixer(pt_, hT[pt_], aT[pt_])
    st3 = ffn_group(grp, 0, 2, 1)
    for pt_ in range(2):
        final_out(hT[pt_], pt_ * NT, st3[pt_])

    dma("sp", invc.ap[:].rearrange("p a g t -> p (a g t)"), invc_d.ap[0:1, 1024:2048].partition_broadcast(128), reads=[invc_d], writes=[invc])
    dma("pool", bands.ap[:].rearrange("p a g t -> p (a g t)"), bands_d.ap[:, 2048:4608], reads=[bands_d], writes=[bands])

    for g0 in (0, 2):
        if g0 == 0:
            load_x(1024 + NT, H1[1])
        for i in (g0, g0 + 1):
            h_sq(H1[i], i % 2)
        if g0 == 0:
            load_x(1024 + 2 * NT, H1[2])
        for i in (g0, g0 + 1):
            norm_mod(H1[i], aT[i % 2], 1, 0, slot=i % 2)
        st1 = ffn_group([(H1[g0], aT[g0 % 2]), (H1[g0 + 1], aT[(g0 + 1) % 2])], 1, 0, 0)
        while bgq:
            bg()
        for k_, i in enumerate((g0, g0 + 1)):
            norm_mod(H1[i], aT[i % 2], 1, 1, st1[k_])
            if i == 0:
                dma("sp", hscr.ap[i], H1[i].ap[:].rearrange("p k t -> p (k t)"), reads=H1[i].all, writes=[hscr])
                load_x(1024 + 3 * NT, H1[3])
            if g0 == 0:
                dma("sp", ascr.ap[i], aT[i % 2].ap[:].rearrange("p k t -> p (k t)"), reads=aT[i % 2].all, writes=[ascr])
        for i in (g0, g0 + 1):
            h, a = H1[i], aT[i % 2]
            proj_fm(a, 512, 3072, kT, True, i * NT)
            while bgq:
                bg()
            k_tok_decay(1, True, False)
            proj_tm(a, 1024, v_tok, range(4))
            for j_ in range(4):
                for w_, lst_ in enumerate((kT, v_tok, kdf)):
                    dma("sp", kvscr.ap[i, w_][:, j_ * 512:(j_ + 1) * 512], lst_[j_].ap[:], reads=[lst_[j_]], writes=[kvscr])
            proj_tm(a, 2048, u_tok, [0, 3])
            op("pool", lambda E, o=u_save.ap[:, 2 * i, :], s=u_tok[0].ap[:]: E.tensor_copy(out=o, in_=s), reads=[u_tok[0]], writes=[u_save])
            op("pool", lambda E, o=u_save.ap[:, 2 * i + 1, :], s=u_tok[3].ap[:]: E.tensor_copy(out=o, in_=s), reads=[u_tok[3]], writes=[u_save])
            dma("sp", sscr.ap[i], S32f.ap[:], reads=S32f.all, writes=[sscr])
            for c in range(4):
                bgq.append(lambda c=c: scan_chunk(S32f, kdf, c, 8))
            if i == 3:
                while bgq:
                    bg()

    dma("sp", pay.ap[0:128, :], S32f.ap[:], reads=S32f.all, writes=[pay])
    dma("pool", pay.ap[128:256, :], u_save.ap[:, 7, :], reads=[u_save], writes=[pay])
    f.custom_dma("pool", lambda E: E.collective_compute("AllGather", ALU.bypass, replica_groups=[[2 * g, 2 * g + 1] for g in range(n_cores // 2)],
                                                        ins=[pay.ap[:, :]], outs=[gat.ap[:, :]]), reads=[pay], writes=[gat], inc=1)

    def load_kv(i):
        for j_ in range(4):
            for w_, lst_ in enumerate((kT, v_tok, kdf)):
                dma("sp", lst_[j_].ap[:], kvscr.ap[i, w_][:, j_ * 512:(j_ + 1) * 512], reads=[kvscr], writes=[lst_[j_]])

    first = True
    for g0 in (3, 1):
        for i in (g0, g0 - 1):
            h, a = H2[i], aT[i % 2]
            if i in (2, 0):
                load_kv(i)
            k_tok_decay(1, False, True)
            if first:
                first = False
                gv = gat.ap.rearrange("(r s p) n -> s p r n", r=2, s=2, p=128)
                dma("sp", G_S.ap[:], gv[0], reads=[gat], writes=[G_S])
                dma("sp", G_U.ap[:], gv[1], reads=[gat], writes=[G_U])
                ts(S32b.all, S32b.ap[:], G_S.ap[:, 0, :], msel.ap[:, 0:1], ALU.mult, [G_S, msel])
                stt(S32b.all, S32b.ap[:], G_S.ap[:, 1, :], msel.ap[:, 1:2], S32b.ap[:], ALU.mult, ALU.add, [G_S, msel] + S32b.all)
                t = nxt("t32", tmp32)
                ts(t, t.ap[:], G_U.ap[:, 0, :], msel.ap[:, 0:1], ALU.mult, [G_U, msel])
                stt(u_halo, u_halo.ap[:], G_U.ap[:, 1, :], msel.ap[:, 1:2], t.ap[:], ALU.mult, ALU.add, [G_U, msel, t])
            dma("sp", S32t.ap[:], sscr.ap[i], reads=[sscr], writes=S32t.all)

            def fwd_c(c):
                cpf = lambda hh, hs: act(Sf_bf, Sf_bf.ap[:, c, hs], S32t.ap[:, hs], AF.Copy, [S32t.k[hh]])
                if c < 3:
                    scan_chunk(S32t, kdf, c, 8, pre=cpf)
                else:
                    for hh in range(4):
                        cpf(hh, slice(hh * 128, (hh + 1) * 128))

            def bwd_c(c):
                cpb = lambda hh, hs: act(Sb_bf, Sb_bf.ap[:, c, hs], S32b.ap[:, hs], AF.Copy, [S32b.k[hh]])
                scan_chunk(S32b, kdb, c, 12, pre=cpb)

            for k2 in range(4):
                bgq.append(lambda c=3 - k2: bwd_c(c))
                bgq.append(lambda c=k2: fwd_c(c))
            proj_fm(a, 0, 2560, qT, True, i * NT)
            proj_fm(a, 1536, None, sg, True, i * NT, silu=True)
            proj_tm(a, 2048, u_tok, range(4))
            while bgq:
                bg()
            retention_out(a, 1, [True] * 4, [True] * 4)
            u_prev = Tile(u_save.ap[:, 2 * (i - 1) + 1, :], u_save.reg) if i > 0 else None
            u_next = Tile(u_save.ap[:, 2 * (i + 1), :], u_save.reg) if i < 3 else None
            srcs = []
            for c in range(4):
                l = [(u_tok[c], 0 if (i == 0 and c == 0) else 1)]
                if c > 0:
                    l.append((u_tok[c - 1], 3))
                elif u_prev is not None:
                    l.append((u_prev, 3))
                if c < 3:
                    l.append((u_tok[c + 1], 2))
                elif u_next is not None:
                    l.append((u_next, 2))
                else:
                    l.append((u_halo, 4))
                srcs.append(l)
            pool_mix(a, srcs, [0 if (i == 0 and c == 0) else 1 for c in range(4)])
            st2 = w_out_stage(h, a, 1)
            norm_mod(h, a, 1, 2, st2)
        st3 = ffn_group([(H2[g0], aT[g0 % 2]), (H2[g0 - 1], aT[(g0 - 1) % 2])], 1, 2, 1)
        if g0 == 3:
            load_kv(1)
            for i2 in (1, 0):
                dma("sp", aT[i2 % 2].ap[:].rearrange("p k t -> p (k t)"), ascr.ap[i2], reads=[ascr], writes=aT[i2 % 2].all)
        for k_, i in enumerate((g0, g0 - 1)):
            final_out(H2[i], 1024 + i * NT, st3[k_])
            if i == 3:
                dma("sp", H2[0].ap[:].rearrange("p k t -> p (k t)"), hscr.ap[0], reads=[hscr], writes=H2[0].all)
    return ring


_CACHE = {}


def build(n_cores=8):
    if ("nc", n_cores) in _CACHE:
        return _CACHE[("nc", n_cores)]
    nc0 = bass.Bass("TRN2", target_bir_lowering=False)
    f0 = FW(nc0, dry=True)
    r0 = record(nc0, f0, None, n_cores)
    seq = list(r0.rec)
    f0.close()
    nc = bass.Bass("TRN2", target_bir_lowering=False)
    f = FW(nc)
    r = record(nc, f, seq, n_cores)
    f.emit()
    f.close()
    _CACHE["sbuf_used"] = (f.sb_ptr - nc.sbuf_base, nc.sbuf_top - nc.sbuf_base)
    _CACHE[("nc", n_cores)] = nc
    _CACHE["stats"] = f.stats
    _CACHE["ops"] = [dict(eng=o["eng"], tag=o.get("tag", "dma")) for o in f.ops]
    return nc


def _pool_tables(role_b):
    WS = (2, 4, 8, 16)

    def mats(pos_t, pos_s, L, same):
        M = np.zeros((4, 128, 128), np.float32)
        V = np.zeros((4, 128), np.float32)
        for g, w in enumerate(WS):
            lo = np.clip(pos_t - w // 2, 0, L)
            hi = np.clip(pos_t + w // 2, 0, L)
            cnt = (hi - lo).astype(np.float32)
            inw = (pos_s[:, None] >= lo[None, :]) & (pos_s[:, None] < hi[None, :])
            M[g] = inw.astype(np.float32)
            if same:
                M[g][np.arange(128), np.arange(128)] -= cnt
            V[g] = 1.0 / cnt
        return M, V

    ar = np.arange(128)
    out_m, out_v = [], []
    m, v0 = mats(ar, ar, 256, True); out_m.append(m)
    m, _ = mats(ar, 128 + ar, 256, False); out_m.append(m)
    m, v1 = mats(128 + ar, 128 + ar, 256, True); out_m.append(m)
    m, _ = mats(128 + ar, ar, 256, False); out_m.append(m)
    L = 4096
    lpos = (lambda lc: 4095 - (lc * 128 + ar)) if role_b else (lambda lc: lc * 128 + ar)
    ppos = (lambda lc: lc * 128 + ar) if role_b else (lambda lc: 4095 - (lc * 128 + ar))
    m, v2 = mats(lpos(0), lpos(0), L, True); out_m.append(m)
    m, v3 = mats(lpos(1), lpos(1), L, True); out_m.append(m)
    m, _ = mats(lpos(1), lpos(2), L, False); out_m.append(m)
    m, _ = mats(lpos(1), lpos(0), L, False); out_m.append(m)
    m, _ = mats(lpos(15), ppos(15), L, False); out_m.append(m)
    bands = np.stack(out_m, 0)
    bands = np.ascontiguousarray(bands.transpose(2, 0, 1, 3)).reshape(128, 9 * 512)
    invc = np.stack([v0, v1, v2, v3], 0).reshape(1, 2048)
    return bands.astype(np.float32), invc.astype(np.float32)


def _rot_tables(role_b):
    l = np.arange(2048)
    pos = (4095 - l) if role_b else l
    row = (pos // 64).astype(np.float32)
    col = (pos % 64).astype(np.float32)
    n_half = 32
    freqs = (np.float32(10000.0) ** (-np.arange(n_half, dtype=np.float32) / np.float32(n_half))).astype(np.float32)
    ang = np.concatenate([row[:, None] * freqs, col[:, None] * freqs], axis=-1).astype(np.float32)
    cos = np.cos(ang).astype(np.float32).T
    sin = np.sin(ang).astype(np.float32).T
    C = np.concatenate([cos, cos], 0)
    S = np.concatenate([-sin, sin], 0)
    return np.ascontiguousarray(np.stack([C, S], 1)).astype(np.float32)


def _rmat():
    r = np.zeros((128, 128), np.float32)
    d = np.arange(128)
    r[(d + 64) % 128, d] = 1.0
    return r


def _ctab():
    j = np.arange(128, dtype=np.float32)[:, None]
    i = np.arange(128, dtype=np.float32)[None, :]
    s = np.float32(128.0 ** -0.5)
    rel1 = np.maximum(i - j, 0)
    m1 = (i >= j).astype(np.float32) * s
    rel2 = np.maximum(j - i, 0)
    m2 = (j >= i).astype(np.float32) * s
    idx1 = np.broadcast_to(i + 1, (128, 128))
    idxr = np.broadcast_to(128 - i, (128, 128))
    kidx = np.concatenate([127 - j, j], 1)
    return np.ascontiguousarray(np.concatenate([rel1, m1, rel2, m2, idx1, idxr, kidx], 1)).astype(np.float32)


def kernel(x_prompt, x_sample, state_ret_fwd, state_ret_bwd, c, c_ctx, ada_w, ada_b, norm_ffn1,
           ffn1_w1, ffn1_w3, ffn1_w2, norm_mix, w_in, ret_decay_fwd, ret_decay_bwd, ret_gn, pool_w,
           pool_scale, w_out, norm_ffn2, ffn2_w1, ffn2_w3, ffn2_w2, norm_final):
    in_maps = _prep(x_prompt, x_sample, state_ret_fwd, state_ret_bwd, c, c_ctx, ada_w, ada_b, norm_ffn1,
                    ffn1_w1, ffn1_w3, ffn1_w2, norm_mix, w_in, ret_decay_fwd, ret_decay_bwd, ret_gn, pool_w,
                    pool_scale, w_out, norm_ffn2, ffn2_w1, ffn2_w3, ffn2_w2, norm_final)
    nc = build()
    res = run_bass_kernel_spmd(nc, in_maps, core_ids=list(range(8)))
    return _assemble(res.results)


def _prep(x_prompt, x_sample, state_ret_fwd, state_ret_bwd, c, c_ctx, ada_w, ada_b, norm_ffn1,
          ffn1_w1, ffn1_w3, ffn1_w2, norm_mix, w_in, ret_decay_fwd, ret_decay_bwd, ret_gn, pool_w,
          pool_scale, w_out, norm_ffn2, ffn2_w1, ffn2_w3, ffn2_w2, norm_final, cores=range(8)):
    f32 = lambda a: np.ascontiguousarray(np.asarray(a, dtype=np.float32))
    x_prompt, x_sample = f32(x_prompt), f32(x_sample)
    w_in0 = f32(w_in)[0]
    sw = np.concatenate([np.r_[h * 128 + 64:h * 128 + 128, h * 128:h * 128 + 64] for h in range(4)])
    w_in_aug = np.ascontiguousarray(np.concatenate([w_in0, w_in0[:, sw], w_in0[:, 512 + sw]], axis=1))
    fm = lambda v, n: np.ascontiguousarray(f32(v).reshape(n, 128).T)
    vecs = np.concatenate([fm(norm_ffn1[0], 8), fm(norm_mix[0], 8), fm(norm_ffn2[0], 8), fm(norm_final, 8),
                           fm(ret_gn[0], 4), fm(pool_scale[0], 4), fm(ada_b[0], 72)], axis=1)
    pool_w_l = np.ascontiguousarray(f32(pool_w)[0].transpose(1, 0, 2).reshape(128, 512))
    ada_full = f32(ada_w)[0]
    n_cores = len(list(cores))
    shared = dict(ffn1_w1=f32(ffn1_w1)[0], ffn1_w3=f32(ffn1_w3)[0], ffn1_w2=f32(ffn1_w2)[0],
                  ffn2_w1=f32(ffn2_w1)[0], ffn2_w3=f32(ffn2_w3)[0], ffn2_w2=f32(ffn2_w2)[0], w_in=w_in_aug,
                  w_out=f32(w_out)[0], pool_w=pool_w_l,  vecs=np.ascontiguousarray(vecs), ctab=_ctab(),
                  ident=np.eye(128, dtype=np.float32), rmat=_rmat())
    tabs = {rb: (_pool_tables(rb), _rot_tables(rb)) for rb in (False, True)}
    df, db = f32(ret_decay_fwd)[0], f32(ret_decay_bwd)[0]
    in_maps = []
    for core in cores:
        b, rb = core // 2, bool(core % 2)
        xp = x_prompt[4 * core:4 * core + 4].reshape(1024, D)
        xs_ = x_sample[b, 2048:4096][::-1] if rb else x_sample[b, 0:2048]
        x_tok = np.concatenate([xp, xs_], 0)
        x_T = np.ascontiguousarray(x_tok.T.reshape(8, 128, 3072).transpose(1, 0, 2))
        ag = 4 if n_cores >= 4 else 2
        if ag == 4:
            b_lo = (core // 4) * 2
            cond = np.stack([f32(c_ctx), f32(c)[b_lo], f32(c)[b_lo + 1]], 0)
        else:
            cond = np.stack([f32(c_ctx), f32(c)[b]], 0)
        ncd = cond.shape[0]
        condT = np.ascontiguousarray(cond.reshape(ncd, 8, 128).transpose(2, 1, 0).reshape(128, 8 * ncd))
        csel = np.zeros((128, 2), np.float32)
        csel[:, b % 2] = 1.0
        cw = 9216 // ag
        dec = np.concatenate([df, db, (db if rb else df), (df if rb else db)]).reshape(1, 16)
        st = f32(state_ret_bwd if rb else state_ret_fwd)[b, 0]
        s_init = np.ascontiguousarray(st.transpose(1, 0, 2).reshape(128, 512))
        msel = np.zeros((128, 2), np.float32)
        msel[:, 0 if rb else 1] = 1.0
        (bands, invc), rot = tabs[rb]
        m = dict(shared)
        m.update(ada_w=np.ascontiguousarray(ada_full[:, (core % ag) * cw:(core % ag + 1) * cw]), csel=csel, x_T=x_T, condT=condT, dec=np.ascontiguousarray(dec.astype(np.float32)), s_init=s_init,
                 rot=rot, msel=msel, bands=bands, invcnt=invc)
        in_maps.append(m)
    return in_maps


def _assemble(results, cores=range(8)):
    y_prompt = np.empty((32, 256, D), np.float32)
    y_sample = np.empty((4, 4096, D), np.float32)
    new_f = np.empty((32, 1, 4, 128, 128), np.float32)
    new_b = np.empty((32, 1, 4, 128, 128), np.float32)
    for k_, core in enumerate(cores):
        r = results[k_]
        b, rb = core // 2, bool(core % 2)
        y = np.asarray(r["y_T"], dtype=np.float32).transpose(2, 1, 0).reshape(3072, D)
        y_prompt[4 * core:4 * core + 4] = y[0:1024].reshape(4, 256, D)
        if rb:
            y_sample[b, 2048:4096] = y[1024:][::-1]
        else:
            y_sample[b, 0:2048] = y[1024:]
        new_f[4 * core:4 * core + 4, 0] = np.asarray(r["nsf"], dtype=np.float32)
        new_b[4 * core:4 * core + 4, 0] = np.asarray(r["nsb"], dtype=np.float32)
    return (y_prompt, y_sample, new_f, new_b)
```

```python
import contextlib
import numpy as np
import concourse.bass as bass
import concourse.mybir as mybir
from concourse.bass_utils import run_bass_kernel_spmd

F32 = mybir.dt.float32
BF16 = mybir.dt.bfloat16
ALU = mybir.AluOpType
AF = mybir.ActivationFunctionType
ENGS = ("pe", "act", "dve", "pool", "sp")

D = 1024
DFF = 2816
NFT = 22
NT = 512
EPS = 1e-6
RING_SLOTS = 4
RING_ELEMS = 4096


class Reg:
    __slots__ = ("name", "last_w", "readers", "aliases", "lo", "hi", "psum")

    def __init__(self, name):
        self.name = name
        self.psum = False
        self.last_w = None
        self.readers = {}
        self.aliases = []
        self.lo = self.hi = None


class Tile:
    def __init__(self, ap, reg):
        self.ap = ap
        self.reg = reg


class FW:
    def __init__(self, nc, n_dma_sems=24, dry=False):
        self.nc = nc
        self.dry = dry
        self.ops = []
        self.stack = contextlib.ExitStack()
        self.n_dma_sems = n_dma_sems
        self.sb_regs = []
        self.tag = ""
        self.sb_ptr = nc.sbuf_base
        self.sb_top = nc.sbuf_top

    def reg(self, name):
        return Reg(name)

    def sbuf(self, name, shape, dtype, at=None):
        esz = 4 if dtype == F32 else 2
        nbytes = int(np.prod(shape[1:])) * esz
        if at is None:
            off = (self.sb_ptr + 31) // 32 * 32
            self.sb_ptr = off + nbytes
            assert self.sb_ptr <= self.sb_top, f"SBUF overflow at {name}: {self.sb_ptr} > {self.sb_top}"
        else:
            off = at
        t = self.nc.alloc_sbuf_tensor_at(name, list(shape), dtype, offset=off)
        r = Reg(name)
        r.lo, r.hi = off, off + nbytes
        for o in self.sb_regs:
            if o.lo < r.hi and r.lo < o.hi:
                o.aliases.append(r)
                r.aliases.append(o)
        self.sb_regs.append(r)
        return Tile(t, r)

    def reserve(self, nbytes):
        off = (self.sb_ptr + 31) // 32 * 32
        self.sb_ptr = off + nbytes
        assert self.sb_ptr <= self.sb_top, f"SBUF overflow (reserve): {self.sb_ptr} > {self.sb_top}"
        return off

    def psum(self, name, shape, dtype):
        t = self.stack.enter_context(self.nc.psum_tensor(name, list(shape), dtype))
        r = Reg(name)
        r.psum = True
        return Tile(t, r)

    def dram(self, name, shape, dtype, kind="ExternalInput", **kw):
        t = self.nc.dram_tensor(name, list(shape), dtype, kind=kind, **kw)
        return Tile(t.ap(), Reg(name))

    def view(self, ap, name):
        return Tile(ap, Reg(name))

    def _deps(self, idx, eng, is_dma, reads, writes):
        ps_reads = [t for t in reads if t.reg.psum]
        if ps_reads:
            reads = [t for t in reads if not t.reg.psum]
            writes = list(writes) + [t for t in ps_reads if all(t.reg is not w.reg for w in writes)]
        deps = set()
        for t in reads:
            r = t.reg
            for rr in [r] + r.aliases:
                if rr.last_w is not None:
                    deps.add(rr.last_w)
        for t in writes:
            r = t.reg
            for rr in [r] + r.aliases:
                if rr.last_w is not None:
                    deps.add(rr.last_w)
                deps.update(rr.readers.values())
        key = ("dma", idx) if is_dma else eng
        for t in reads:
            t.reg.readers[key] = idx
        for t in writes:
            t.reg.last_w = idx
            t.reg.readers = {}
        deps.discard(idx)
        return deps

    def op(self, eng, fn, reads=(), writes=()):
        idx = len(self.ops)
        deps = self._deps(idx, eng, False, reads, writes)
        self.ops.append(dict(eng=eng, fn=fn, deps=deps, is_dma=False, flag=False, tag=self.tag))

    def dma(self, eng, out, in_, reads=(), writes=(), **kw):
        idx = len(self.ops)
        deps = self._deps(idx, eng, True, reads, writes)
        self.ops.append(dict(eng=eng, fn=None, out=out, in_=in_, kw=kw, deps=deps, is_dma=True, flag=False))

    def custom_dma(self, eng, fn, reads=(), writes=(), inc=16):
        idx = len(self.ops)
        deps = self._deps(idx, eng, True, reads, writes)
        self.ops.append(dict(eng=eng, fn=fn, deps=deps, is_dma=True, flag=False, inc=inc))

    def emit(self):
        nc, ops = self.nc, self.ops
        for o in ops:
            comp, dmas = {}, []
            for d in o["deps"]:
                p = ops[d]
                if p["is_dma"]:
                    dmas.append(d)
                else:
                    if p["eng"] == "pe" and o["eng"] == "pe" and not o["is_dma"]:
                        continue
                    if p["eng"] not in comp or comp[p["eng"]] < d:
                        comp[p["eng"]] = d
            o["cdeps"] = comp
            o["ddeps"] = sorted(dmas)
            for d in comp.values():
                ops[d]["flag"] = True
        cnt = {e: 0 for e in ENGS}
        dma_i = {"hw": 0, "sw": 0}
        n_hw = self.n_dma_sems // 2
        sem_uses = [0] * self.n_dma_sems
        for o in ops:
            if o["is_dma"]:
                if o["fn"] is not None:
                    s = self.n_dma_sems - 1
                elif o["eng"] == "pool":
                    s = n_hw + dma_i["sw"] % (self.n_dma_sems - 1 - n_hw)
                    dma_i["sw"] += 1
                else:
                    s = dma_i["hw"] % n_hw
                    dma_i["hw"] += 1
                o["dsem"] = s
                o["dprev"] = sem_uses[s]
                sem_uses[s] += o.get("inc", 16)
                o["dval"] = sem_uses[s]
            elif o["flag"]:
                cnt[o["eng"]] += 1
                o["cnt"] = cnt[o["eng"]]
        st = self.stack
        esem = {e: st.enter_context(nc.semaphore(f"s_{e}")) for e in ENGS if e != "sp"}
        dsem = [st.enter_context(nc.semaphore(f"s_dma{k}")) for k in range(self.n_dma_sems)]
        block = st.enter_context(nc.Block())
        per_eng = {e: [] for e in ENGS}
        for i, o in enumerate(ops):
            per_eng[o["eng"]].append(i)
        self.stats = {e: len(v) for e, v in per_eng.items()}

        def run(eng_name, E):
            seen_c = {e: 0 for e in ENGS}
            seen_d = [0] * self.n_dma_sems
            for i in per_eng[eng_name]:
                o = ops[i]
                for pe_, d in o["cdeps"].items():
                    v = ops[d]["cnt"]
                    if seen_c[pe_] < v:
                        E.wait_ge(esem[pe_], v)
                        seen_c[pe_] = v
                for d in o["ddeps"]:
                    p = ops[d]
                    if seen_d[p["dsem"]] < p["dval"]:
                        E.wait_ge(dsem[p["dsem"]], p["dval"])
                        seen_d[p["dsem"]] = p["dval"]
                if o["is_dma"]:
                    s = o["dsem"]
                    if seen_d[s] < o["dprev"]:
                        E.wait_ge(dsem[s], o["dprev"])
                        seen_d[s] = o["dprev"]
                    if o["fn"] is None:
                        ins = E.dma_start(out=o["out"], in_=o["in_"], **o["kw"])
                    else:
                        ins = o["fn"](E)
                    ins.then_inc(dsem[s], o.get("inc", 16))
                else:
                    ins = o["fn"](E)
                    if o["flag"]:
                        ins.then_inc(esem[eng_name], 1)
            if eng_name == "sp":
                for s in range(self.n_dma_sems):
                    if sem_uses[s] > seen_d[s]:
                        E.wait_ge(dsem[s], sem_uses[s])
                for e in ENGS:
                    if e != "sp" and cnt[e] > 0:
                        E.wait_ge(esem[e], cnt[e])

        @block.tensor
        def _(E):
            run("pe", E)

        @block.scalar
        def _(E):
            run("act", E)

        @block.vector
        def _(E):
            run("dve", E)

        @block.gpsimd
        def _(E):
            run("pool", E)

        @block.sync
        def _(E):
            run("sp", E)

    def close(self):
        self.stack.close()


class Ring:
    def __init__(self, f, slots, seq, tiles):
        self.f = f
        self.slots = slots
        self.seq = seq
        self.tiles = tiles
        self.rec = []
        self.i = 0
        self.loaded = 0

    def get(self, name, apfn, k, n):
        i = self.i
        self.i += 1
        self.rec.append((name, apfn, k, n))
        S = len(self.slots)
        if self.seq is not None:
            while self.loaded < min(len(self.seq), i + S - 1):
                j = self.loaded
                nm, fn, kk, nn = self.seq[j]
                st = self.tiles[nm]
                slot = self.slots[j % S]
                dst = slot.ap[:, 0:kk * nn].rearrange("p (k n) -> p k n", n=nn)
                self.f.dma("pool", dst, fn(st.ap), reads=[st], writes=[slot])
                self.loaded += 1
        slot = self.slots[i % S]
        return slot, slot.ap[:, 0:k * n].rearrange("p (k n) -> p k n", n=n)


def record(nc, f, ring_seq, n_cores=8):
    dr = f.dram
    x_T = dr("x_T", [128, 8, 3072], F32)
    y_T = dr("y_T", [128, 8, 3072], F32, kind="ExternalOutput")
    nsf = dr("nsf", [4, 4, 128, 128], F32, kind="ExternalOutput")
    nsb = dr("nsb", [4, 4, 128, 128], F32, kind="ExternalOutput")
    condT_d = dr("condT", [128, 8 * (3 if n_cores >= 4 else 2)], F32)
    dec_d = dr("dec", [1, 16], F32)
    sinit_d = dr("s_init", [128, 512], F32)
    rot_d = dr("rot", [128, 2, 2048], F32)
    msel_d = dr("msel", [128, 2], F32)
    bands_d = dr("bands", [128, 9 * 512], F32)
    invc_d = dr("invcnt", [1, 2048], F32)
    ctab_d = dr("ctab", [128, 6 * 128 + 2], F32)
    ident_d = dr("ident", [128, 128], F32)
    rmat_d = dr("rmat", [128, 128], F32)
    vecs_d = dr("vecs", [128, 8 * 4 + 4 + 4 + 72], F32)
    AG = 4 if n_cores >= 4 else 2
    NCD = 3 if AG == 4 else 2
    CPC = 72 // AG
    ada_w = dr("ada_w", [D, CPC * 128], F32)
    csel_d = dr("csel", [128, 2], F32)
    mpay = dr("mpay", [128, CPC * NCD], F32, kind="Internal", addr_space="Local")
    mgat = dr("mgat", [AG * 128, CPC * NCD], F32, kind="Internal", addr_space="Local")
    w1 = [dr("ffn1_w1", [D, DFF], F32), dr("ffn2_w1", [D, DFF], F32)]
    w3 = [dr("ffn1_w3", [D, DFF], F32), dr("ffn2_w3", [D, DFF], F32)]
    w2 = [dr("ffn1_w2", [DFF, D], F32), dr("ffn2_w2", [DFF, D], F32)]
    w_in = dr("w_in", [D, 3584], F32)
    w_out = dr("w_out", [D, D], F32)
    pool_w_d = dr("pool_w", [128, 512], F32)
    w_in_c = dr("w_in_c", [D, 3584], BF16, kind="Internal")
    w_out_c = dr("w_out_c", [D, D], BF16, kind="Internal")
    hscr = dr("hscr", [4, 128, 8 * NT], F32, kind="Internal")
    sscr = dr("sscr", [4, 128, 512], F32, kind="Internal")
    ascr = dr("ascr", [2, 128, 8 * NT], BF16, kind="Internal")
    kvscr = dr("kvscr", [4, 3, 128, 2048], BF16, kind="Internal")
    pay = dr("pay", [256, 512], F32, kind="Internal", addr_space="Local")
    gat = dr("gat", [512, 512], F32, kind="Internal", addr_space="Local")

    sb = f.sbuf
    class Multi:
        def __init__(self, t, n):
            self.ap = t.ap
            self.k = [Tile(t.ap, Reg(f"{t.reg.name}_{j}")) for j in range(n)]
            self.all = list(self.k)

    hT = [Multi(sb(f"hT{i}", [128, 8, NT], F32), 8) for i in range(3)]
    slots = [sb(f"ring{i}", [128, RING_ELEMS], BF16) for i in range(RING_SLOTS)]
    wt = dict(ada_w=ada_w, ffn1_w1=w1[0], ffn2_w1=w1[1], ffn1_w3=w3[0], ffn2_w3=w3[1], ffn1_w2=w2[0], ffn2_w2=w2[1],
              w_in=w_in, w_out=w_out, w_in_c=w_in_c, w_out_c=w_out_c)
    ring = Ring(f, slots, ring_seq, wt)
    u_save = sb("u_save", [128, 8, 512], BF16)
    rotT = sb("rotT", [128, 2, 2048], BF16)
    bands = sb("bands", [128, 5, 4, 128], BF16)
    invc = sb("invc", [128, 2, 4, 128], F32)
    Dmask = sb("Dmask", [128, 2, 4, 128], F32)
    QD = sb("QD", [128, 2, 2, 4, 128], BF16)
    KD = sb("KD", [128, 16], F32)
    CDt = sb("CDt", [128, 16], F32)
    lgT = sb("lgT", [128, 16], F32)
    modT = Multi(sb("modT", [128, 72, 2], F32), 2)
    vecs = sb("vecs", [128, 112], F32)
    mder = Multi(sb("mder", [128, 2, 9, 8], F32), 6)
    condT = sb("condT", [128, 8 * NCD], F32)
    scT = sb("scT", [128, 8, NCD], BF16)
    G2 = sb("G2", [128, 72 * NCD], F32)
    csel = sb("csel", [128, 2], F32)
    ident = sb("ident", [128, 128], F32)
    identb = sb("identb", [128, 128], BF16)
    rmat = sb("rmat", [128, 128], BF16)
    ones_d = sb("ones_d", [128, 128], BF16)
    ones_v = sb("ones_v", [128, 128], BF16)
    epsb = sb("epsb", [128, 1], F32)
    msel = sb("msel", [128, 2], F32)
    poolw = sb("poolw", [128, 4, 128], BF16)
    S32f = Multi(sb("S32f", [128, 512], F32), 4)
    S32b = Multi(sb("S32b", [128, 512], F32), 4)
    S32t = Multi(sb("S32t", [128, 512], F32), 4)
    sq8 = None
    Sf_bf = sb("Sf_bf", [128, 4, 512], BF16)
    Sb_bf = sb("Sb_bf", [128, 4, 512], BF16)
    sst = [sb(f"sst{i}", [128, 512], F32) for i in range(2)]
    rstd = [sb(f"rstd{i}", [128, NT], F32) for i in range(2)]
    tmp32 = [sb(f"tmp32_{i}", [128, NT], F32) for i in range(3)]
    stmp = [sb(f"stmp{i}", [128, NT], BF16) for i in range(3)]
    ptb = [sb(f"ptb{i}", [128, 128], BF16) for i in range(12)]
    aT = [Multi(sb(f"aT{i}", [128, 8, NT], BF16), 8) for i in range(2)]
    SCR = f.reserve(36864)
    ctab = sb("ctab", [128, 6 * 128 + 2], F32, at=SCR)
    gT = [sb(f"gT{ft}", [128, NT], BF16, at=SCR + ft * 1024) for ft in range(NFT)]
    yT = [sb(f"yT{kc}", [128, NT], F32, at=SCR + kc * 2048) for kc in range(8)]
    qT = [sb(f"qT{h}", [128, NT], BF16, at=SCR + h * 1024) for h in range(4)]
    kT = [sb(f"kT{h}", [128, NT], BF16, at=SCR + 4096 + h * 1024) for h in range(4)]
    qdr = [sb(f"qdr{h}", [128, NT], BF16, at=SCR + 8192 + h * 1024) for h in range(4)] + [sb(f"qdr{4 + h}", [128, NT], BF16) for h in range(2)]
    sg = [sb(f"sg{h}", [128, NT], BF16, at=SCR + 12288 + h * 1024) for h in range(4)]
    v_tok = [sb(f"v_tok{c}", [128, 512], BF16, at=SCR + 16384 + c * 1024) for c in range(4)]
    u_tok = [sb(f"u_tok{c}", [128, 512], BF16, at=SCR + 20480 + c * 1024) for c in range(4)]
    dmT = [sb(f"dmT{g}", [128, NT], BF16, at=SCR + 24576 + g * 1024) for g in range(4)]
    kdf = [sb(f"kdf{c}", [128, 512], BF16, at=SCR + 28672 + c * 1024) for c in range(4)]
    kdb = [sb(f"kdb{c}", [128, 512], BF16, at=SCR + 32768 + c * 1024) for c in range(4)]
    sq8 = [sb(f"sq8_{i}", [128, 8, NT], BF16, at=SCR + i * 8192) for i in range(2)]
    G_S = sb("G_S", [128, 2, 512], F32, at=SCR + 24576)
    G_U = sb("G_U", [128, 2, 512], F32, at=SCR + 8192)
    u_halo = sb("u_halo", [128, 512], BF16)


    PB = [f.psum(f"pb{i}", [128, NT], F32) for i in range(8)]
    PQ = [Tile(p.ap[:, 0:128], p.reg) for p in PB]
    PSB = [Tile(p.ap[:, 0:256].bitcast(BF16), p.reg) for p in PB]
    ctr = dict(pb=0, pq=0, psb=0, t32=0, st=0, ptb=0, rs=0, xs=0, ys=0, sst=0, qd=0, ssp=0)

    busy_banks = set()

    def nxt(key, lst):
        if key in ("pq", "psb", "pb"):
            while (ctr["pb"] % 8) in busy_banks:
                ctr["pb"] += 1
            v = lst[ctr["pb"] % 8]
            ctr["pb"] += 1
            return v
        v = lst[ctr[key] % len(lst)]
        ctr[key] += 1
        return v

    class Stats:
        def __init__(self, n=8, ones=None):
            while (ctr["pb"] % 8) in busy_banks:
                ctr["pb"] += 1
            self.idx = ctr["pb"] % 8
            ctr["pb"] += 1
            busy_banks.add(self.idx)
            self.pb = PB[self.idx]
            self.n = n
            self.i = 0
            self.ones = ones if ones is not None else ones_d

        def add(self, src_t, src_ap):
            s_ = nxt("st", stmp)
            act(s_, s_.ap[:], src_ap, AF.Square, [src_t])
            i = self.i
            self.i += 1
            return lambda: mm(self.pb, self.pb.ap[:], self.ones.ap[:], s_.ap[:], i == 0, i == self.n - 1, [self.ones, s_])

        def finish(self):
            r = nxt("rs", rstd)
            act(r, r.ap[:], self.pb.ap[:], AF.Ln, [self.pb, epsb], bias=epsb.ap[:, 0:1], scale=1.0)
            act(r, r.ap[:], r.ap[:], AF.Exp, [r], scale=-0.5)
            busy_banks.discard(self.idx)
            return r

    op, dma = f.op, f.dma
    import os as _os
    DBG = bool(_os.environ.get("K_DEBUG"))

    def dump(name, t, shape, ap=None):
        if not DBG:
            return
        d_ = dr("dbg_" + name, list(shape), F32 if True else None, kind="ExternalOutput")
        src = ap if ap is not None else t.ap[:]
        if len(shape) == 3:
            dst = d_.ap[:, :, :]
        else:
            dst = d_.ap[:, :]
        dma("pool", dst, src, reads=[t], writes=[d_])

    def wl(t):
        return t if isinstance(t, list) else [t]

    def mm(out_t, out_ap, lhsT, rhs, start, stop, reads):
        op("pe", lambda E: E.matmul(out_ap, lhsT=lhsT, rhs=rhs, start=start, stop=stop), reads=reads, writes=wl(out_t))

    def act(out_t, out_ap, in_ap, func, reads, scale=None, bias=None):
        kw = {}
        if scale is not None:
            kw["scale"] = scale
        if bias is not None:
            kw["bias"] = bias
        op("act", lambda E: E.activation(out=out_ap, in_=in_ap, func=func, **kw), reads=reads, writes=wl(out_t))

    def tt(out_t, out_ap, in0, in1, alu, reads, eng="dve"):
        op(eng, lambda E: E.tensor_tensor(out=out_ap, in0=in0, in1=in1, op=alu), reads=reads, writes=wl(out_t))

    def stt(out_t, out_ap, in0, scalar, in1, op0, op1, reads):
        op("dve", lambda E: E.scalar_tensor_tensor(out=out_ap, in0=in0, scalar=scalar, in1=in1, op0=op0, op1=op1),
           reads=reads, writes=wl(out_t))

    def ts(out_t, out_ap, in0, s1, op0, reads, s2=None, op1=None, eng="dve"):
        if op1 is None:
            op(eng, lambda E: E.tensor_scalar(out=out_ap, in0=in0, scalar1=s1, scalar2=None, op0=op0), reads=reads, writes=wl(out_t))
        else:
            op(eng, lambda E: E.tensor_scalar(out=out_ap, in0=in0, scalar1=s1, scalar2=s2, op0=op0, op1=op1), reads=reads, writes=wl(out_t))

    dma("sp", condT.ap[:], condT_d.ap[:, :], reads=[condT_d], writes=[condT])
    dma("sp", vecs.ap[:], vecs_d.ap[:, :], reads=[vecs_d], writes=[vecs])
    dma("sp", ident.ap[:], ident_d.ap[:, :], reads=[ident_d], writes=[ident])
    dma("sp", ctab.ap[:], ctab_d.ap[:, :], reads=[ctab_d], writes=[ctab])
    dma("sp", lgT.ap[:], dec_d.ap[0:1, :].partition_broadcast(128), reads=[dec_d], writes=[lgT])
    dma("sp", msel.ap[:], msel_d.ap[:, :], reads=[msel_d], writes=[msel])
    dma("sp", invc.ap[:].rearrange("p a g t -> p (a g t)"), invc_d.ap[0:1, 0:1024].partition_broadcast(128), reads=[invc_d], writes=[invc])
    dma("sp", S32f.ap[:], sinit_d.ap[:, :], reads=[sinit_d], writes=S32f.all)
    op("dve", lambda E: E.memset(epsb.ap[:], EPS), writes=[epsb])
    op("dve", lambda E: E.memset(ones_d.ap[:], 1.0 / D), writes=[ones_d])
    op("dve", lambda E: E.memset(ones_v.ap[:], 1.0 / 128.0), writes=[ones_v])
    op("dve", lambda E: E.tensor_copy(out=identb.ap[:], in_=ident.ap[:]), reads=[ident], writes=[identb])
    act(lgT, lgT.ap[:], lgT.ap[:], AF.Exp, [lgT])
    ts(lgT, lgT.ap[:], lgT.ap[:], -1.0, ALU.mult, [lgT])
    REL1, M1s, REL2, M2s, IDX1, IDXR = [ctab.ap[:, i * 128:(i + 1) * 128] for i in range(6)]
    kidx = ctab.ap[:, 768:770]
    SC = 128.0 ** -0.5
    for st_ in range(2):
        for h in range(4):
            cf = st_ * 8 + h
            cb = st_ * 8 + 4 + h
            t1 = nxt("t32", tmp32)
            act(t1, t1.ap[:, 0:128], REL1, AF.Exp, [ctab, lgT], scale=lgT.ap[:, cf:cf + 1])
            tt(t1, t1.ap[:, 0:128], t1.ap[:, 0:128], M1s, ALU.mult, [t1, ctab])
            t2 = nxt("t32", tmp32)
            act(t2, t2.ap[:, 0:128], REL2, AF.Exp, [ctab, lgT], scale=lgT.ap[:, cb:cb + 1])
            tt(t2, t2.ap[:, 0:128], t2.ap[:, 0:128], M2s, ALU.mult, [t2, ctab])
            tt(Dmask, Dmask.ap[:, st_, h, :], t1.ap[:, 0:128], t2.ap[:, 0:128], ALU.add, [t1, t2])
            act(QD, QD.ap[:, st_, 0, h, :], IDX1, AF.Exp, [ctab, lgT], scale=lgT.ap[:, cf:cf + 1])
            act(QD, QD.ap[:, st_, 1, h, :], IDXR, AF.Exp, [ctab, lgT], scale=lgT.ap[:, cb:cb + 1])
            act(KD, KD.ap[:, cf:cf + 1], kidx[:, 0:1], AF.Exp, [ctab, lgT], scale=lgT.ap[:, cf:cf + 1])
            act(KD, KD.ap[:, cb:cb + 1], kidx[:, 1:2], AF.Exp, [ctab, lgT], scale=lgT.ap[:, cb:cb + 1])
    ts(KD, KD.ap[:], KD.ap[:], SC, ALU.mult, [KD])
    act(CDt, CDt.ap[:], lgT.ap[:], AF.Exp, [lgT], scale=128.0)

    nfin = vecs.ap[:, 24:32]
    gn = vecs.ap[:, 32:36]
    pscale = vecs.ap[:, 36:40]

    def load_x(tok0, h):
        f.tag = "load_x"
        dma("sp", h.ap[:], x_T.ap[:, :, tok0:tok0 + NT], reads=[x_T], writes=h.all)

    def rms_rstd(src_tiles, src_aps, ones, n):
        f.tag = "rms_rstd"
        pb = nxt("pb", PB)
        for i in range(n):
            s = nxt("st", stmp)
            act(s, s.ap[:], src_aps[i], AF.Square, [src_tiles[i]])
            mm(pb, pb.ap[:], ones.ap[:], s.ap[:], i == 0, i == n - 1, [ones, s])
        r = nxt("rs", rstd)
        act(r, r.ap[:], pb.ap[:], AF.Ln, [pb, epsb], bias=epsb.ap[:, 0:1], scale=1.0)
        act(r, r.ap[:], r.ap[:], AF.Exp, [r], scale=-0.5)
        return r

    def h_sq(h, slot):
        f.tag = "rms_rstd"
        for half in range(2):
            act(sq8[slot], sq8[slot].ap[:, half * 4:(half + 1) * 4, :], h.ap[:, half * 4:(half + 1) * 4, :], AF.Square, h.k[half * 4:(half + 1) * 4])

    def h_rstd(h, slot):
        f.tag = "rms_rstd"
        pb = nxt("pb", PB)
        for kc in range(8):
            mm(pb, pb.ap[:], ones_d.ap[:], sq8[slot].ap[:, kc, :], kc == 0, kc == 7, [ones_d, sq8[slot]])
        r = nxt("rs", rstd)
        act(r, r.ap[:], pb.ap[:], AF.Ln, [pb, epsb], bias=epsb.ap[:, 0:1], scale=1.0)
        act(r, r.ap[:], r.ap[:], AF.Exp, [r], scale=-0.5)
        return r

    def norm_mod(h, a, cond, m, stats=None, slot=0):
        r = stats.finish() if stats is not None else h_rstd(h, slot)
        f.tag = "norm_mod"
        for kc in range(8):
            t = nxt("t32", tmp32)
            tt(t, t.ap[:], h.ap[:, kc, :], r.ap[:], ALU.mult, [h.k[kc], r])
            act(a.k[kc], a.ap[:, kc, :], t.ap[:], AF.Identity, [t, mder.k[cond * 3 + m]],
                scale=mder.ap[:, cond, 3 * m, kc:kc + 1], bias=mder.ap[:, cond, 3 * m + 1, kc:kc + 1])

    fills = []
    for c0_ in range(0, 2560, 512):
        fills.append((w_in_c, w_in, c0_))
    for c0_ in range(0, 1024, 512):
        fills.append((w_out_c, w_out, c0_))

    def fill_one():
        if fills:
            dst_, src_, c0_ = fills.pop(0)
            dma("pool", dst_.ap[:, c0_:c0_ + 512], src_.ap[:, c0_:c0_ + 512], reads=[src_], writes=[dst_])

    def ffn_group(tiles, cond, m, which, want_stats=True):
        f.tag = "ffn_group"
        stats = None
        pend = []
        n1, n3, n2 = [f"ffn{which + 1}_w{x}" for x in (1, 3, 2)]
        for half in range(2):
            ft0 = half * 11
            col, rem = ft0 * 128, 11 * 128
            while rem > 0:
                ncol = min(512, rem)
                cut = lambda a_, col=col, ncol=ncol: a_.rearrange("(kc p) n -> p kc n", p=128)[:, :, col: col + ncol]
                s1, v1 = ring.get(n1, cut, 8, ncol)
                fill_one()
                s3, v3 = ring.get(n3, cut, 8, ncol)
                fill_one()
                for j in range(ncol // 128):
                    ftl = (col - ft0 * 128) // 128 + j
                    for ti, (h, a) in enumerate(tiles):
                        p1 = nxt("pb", PB)
                        for kc in range(8):
                            mm(p1, p1.ap[:], v1[:, kc, j * 128:(j + 1) * 128], a.ap[:, kc, :], kc == 0, kc == 7, [s1, a.k[kc]])
                        p3 = nxt("pb", PB)
                        for kc in range(8):
                            mm(p3, p3.ap[:], v3[:, kc, j * 128:(j + 1) * 128], a.ap[:, kc, :], kc == 0, kc == 7, [s3, a.k[kc]])
                        s = nxt("st", stmp)
                        act(s, s.ap[:], p1.ap[:], AF.Silu, [p1])
                        g = gT[ti * 11 + ftl]
                        tt(g, g.ap[:], p3.ap[:], s.ap[:], ALU.mult, [p3, s])
                    if ftl % 2 == 1:
                        bg()
                        f.tag = "ffn_group"
                col += ncol
                rem -= ncol
            if half == 1 and want_stats:
                stats = [Stats() for _ in tiles]
            for d2 in range(4):
                s2, v2 = ring.get(n2, lambda a_, ft0=ft0, d2=d2: a_.rearrange("(ft p) n -> p ft n", p=128)[:, ft0:ft0 + 11, d2 * 256:(d2 + 1) * 256], 11, 256)
                for dj in range(2):
                    dc = d2 * 2 + dj
                    for ti, (h, a) in enumerate(tiles):
                        py = nxt("pb", PB)
                        for ftl in range(11):
                            g = gT[ti * 11 + ftl]
                            mm(py, py.ap[:], v2[:, ftl, dj * 128:(dj + 1) * 128], g.ap[:], ftl == 0, ftl == 10, [s2, g])
                        stt(h.k[dc], h.ap[:, dc, :], py.ap[:], mder.ap[:, cond, 3 * m + 2, dc:dc + 1], h.ap[:, dc, :], ALU.mult, ALU.add, [py, mder.k[cond * 3 + m], h.k[dc]])
                        if half == 1 and want_stats:
                            nxt_pend = stats[ti].add(h.k[dc], h.ap[:, dc, :])
                            for p_ in pend:
                                p_()
                            pend = [nxt_pend]
                            f.tag = "ffn_group"
        for p_ in pend:
            p_()
        return stats

    bgq = []

    def bg():
        if bgq:
            tg = f.tag
            bgq.pop(0)()
            f.tag = tg

    def w_unit(c0):
        while fills:
            fill_one()
        return ring.get("w_in_c", lambda a_, c0=c0: a_.rearrange("(kc p) n -> p kc n", p=128)[:, :, c0:c0 + 512], 8, 512)

    def proj_fm(a, c0, c0_sw, outs, is_sample, tok_off, silu=False):
        f.tag = "proj_fm"
        s, v = w_unit(c0)
        rot = c0_sw is not None and is_sample
        pend = None
        for h in range(4):
            p = nxt("pb", PB)
            for kc in range(8):
                mm(p, p.ap[:], v[:, kc, h * 128:(h + 1) * 128], a.ap[:, kc, :], kc == 0, kc == 7, [s, a.k[kc]])
            if rot:
                p_idx = PB.index(p)
                busy_banks.add(p_idx)
                qb = nxt("st", stmp)
                act(qb, qb.ap[:], p.ap[:], AF.Copy, [p])

                def fin(h=h, p=p, p_idx=p_idx, qb=qb):
                    p2 = nxt("pb", PB)
                    mm(p2, p2.ap[:], rmat.ap[:], qb.ap[:], True, True, [rmat, qb])
                    t1 = nxt("t32", tmp32)
                    tt(t1, t1.ap[:], p.ap[:], rotT.ap[:, 0, tok_off:tok_off + NT], ALU.mult, [p, rotT])
                    busy_banks.discard(p_idx)
                    t2 = nxt("t32", tmp32)
                    tt(t2, t2.ap[:], p2.ap[:], rotT.ap[:, 1, tok_off:tok_off + NT], ALU.mult, [p2, rotT])
                    tt(outs[h], outs[h].ap[:], t1.ap[:], t2.ap[:], ALU.add, [t1, t2], eng="pool")

                if pend is not None:
                    pend()
                pend = fin
            else:
                act(outs[h], outs[h].ap[:], p.ap[:], AF.Silu if silu else AF.Copy, [p])
            bg()
        if pend is not None:
            pend()

    def proj_tm(a, c0, outs, chunks):
        f.tag = "proj_tm"
        s, v = w_unit(c0)
        for c in chunks:
            p = nxt("pb", PB)
            for kc in range(8):
                mm(p, p.ap[:], a.ap[:, kc, c * 128:(c + 1) * 128], v[:, kc, :], kc == 0, kc == 7, [s, a.k[kc]])
            act(outs[c], outs[c].ap[:], p.ap[:], AF.Copy, [p])
            bg()

    def k_tok_decay(st_, want_f, want_b):
        f.tag = "k_tok_decay"
        for c in range(4):
            pb_ = nxt("psb", PSB)
            for h in range(4):
                op("pe", lambda E, o=pb_.ap[:, h * 128:(h + 1) * 128], i=kT[h].ap[:, c * 128:(c + 1) * 128]: E.transpose(o, i, identb.ap[:]),
                   reads=[kT[h], identb], writes=[pb_])
            for h in range(4):
                hs = slice(h * 128, (h + 1) * 128)
                if want_f:
                    act(kdf[c], kdf[c].ap[:, hs], pb_.ap[:, hs], AF.Copy, [pb_, KD], scale=KD.ap[:, st_ * 8 + h: st_ * 8 + h + 1])
                if want_b:
                    if not want_f and h % 2 == 1:
                        act(kdb[c], kdb[c].ap[:, hs], pb_.ap[:, hs], AF.Copy, [pb_, KD], scale=KD.ap[:, st_ * 8 + 4 + h: st_ * 8 + 4 + h + 1])
                    else:
                        ts(kdb[c], kdb[c].ap[:, hs], pb_.ap[:, hs], KD.ap[:, st_ * 8 + 4 + h: st_ * 8 + 4 + h + 1], ALU.mult, [pb_, KD])

    def kv_mm(kd, c, h):
        f.tag = "kv_mm"
        pq = nxt("pq", PQ)
        mm(pq, pq.ap, kd[c].ap[:, h * 128:(h + 1) * 128], v_tok[c].ap[:, h * 128:(h + 1) * 128], True, True, [kd[c], v_tok[c]])
        return pq

    def scan_step(S32, kd, c, h, cdcol):
        pq = kv_mm(kd, c, h)
        hs = slice(h * 128, (h + 1) * 128)
        stt(S32.k[h], S32.ap[:, hs], S32.ap[:, hs], CDt.ap[:, cdcol:cdcol + 1], pq.ap, ALU.mult, ALU.add, [S32.k[h], CDt, pq])

    def scan_chunk(S32, kd, c, cd0, pre=None):
        f.tag = "kv_mm"
        pb_ = nxt("pb", PB)
        for h in range(4):
            hs = slice(h * 128, (h + 1) * 128)
            mm(pb_, pb_.ap[:, hs], kd[c].ap[:, hs], v_tok[c].ap[:, hs], True, True, [kd[c], v_tok[c]])
        for h in range(4):
            hs = slice(h * 128, (h + 1) * 128)
            if pre is not None:
                pre(h, hs)
            stt(S32.k[h], S32.ap[:, hs], S32.ap[:, hs], CDt.ap[:, cd0 + h:cd0 + h + 1], pb_.ap[:, hs], ALU.mult, ALU.add, [S32.k[h], CDt, pb_])

    def retention_out(mix, st_, have_f, have_b):
        st = {}

        def s1(h):
            f.tag = "retention_out"
            qf = nxt("qd", qdr)
            qb = nxt("qd", qdr)
            for c in range(4):
                cs = slice(c * 128, (c + 1) * 128)
                if have_f[c]:
                    tt(qf, qf.ap[:, cs], qT[h].ap[:, cs], QD.ap[:, st_, 0, h, :], ALU.mult, [qT[h], QD], eng="pool")
                if have_b[c]:
                    tt(qb, qb.ap[:, cs], qT[h].ap[:, cs], QD.ap[:, st_, 1, h, :], ALU.mult, [qT[h], QD], eng="pool")
            pts = []
            for c in range(4):
                cs = slice(c * 128, (c + 1) * 128)
                pq = nxt("pq", PQ)
                mm(pq, pq.ap, kT[h].ap[:, cs], qT[h].ap[:, cs], True, True, [kT[h], qT[h]])
                pt = nxt("ptb", ptb)
                tt(pt, pt.ap[:], pq.ap, Dmask.ap[:, st_, h, :], ALU.mult, [pq, Dmask])
                pts.append(pt)
            st[h] = dict(qf=qf, qb=qb, pts=pts)

        def s2(h):
            f.tag = "retention_out"
            hs = slice(h * 128, (h + 1) * 128)
            d_ = st[h]
            po = nxt("pb", PB)
            po_idx = PB.index(po)
            busy_banks.add(po_idx)
            for c in range(4):
                cs = slice(c * 128, (c + 1) * 128)
                last = not (have_f[c] or have_b[c])
                mm(po, po.ap[:, cs], v_tok[c].ap[:, hs], d_["pts"][c].ap[:], True, last, [v_tok[c], d_["pts"][c]])
                if have_f[c]:
                    mm(po, po.ap[:, cs], Sf_bf.ap[:, c, hs], d_["qf"].ap[:, cs], False, not have_b[c], [Sf_bf, d_["qf"]])
                if have_b[c]:
                    mm(po, po.ap[:, cs], Sb_bf.ap[:, c, hs], d_["qb"].ap[:, cs], False, True, [Sb_bf, d_["qb"]])
            s_ = nxt("st", stmp)
            act(s_, s_.ap[:], po.ap[:], AF.Square, [po])
            d_.update(po=po, po_idx=po_idx, s_=s_)

        def s3(h):
            f.tag = "ret_epi"
            d_ = st[h]
            po, s_ = d_["po"], d_["s_"]
            p2 = nxt("pb", PB)
            mm(p2, p2.ap[:], ones_v.ap[:], s_.ap[:], True, True, [ones_v, s_])
            r = nxt("rs", rstd)
            act(r, r.ap[:], p2.ap[:], AF.Ln, [p2, epsb], bias=epsb.ap[:, 0:1], scale=1.0)
            act(r, r.ap[:], r.ap[:], AF.Exp, [r], scale=-0.5)
            t = nxt("t32", tmp32)
            stt(t, t.ap[:], po.ap[:], gn[:, h:h + 1], r.ap[:], ALU.mult, ALU.mult, [po, vecs, r])
            busy_banks.discard(d_["po_idx"])
            tt(mix.k[h], mix.ap[:, h, :], t.ap[:], sg[h].ap[:], ALU.mult, [t, sg[h]], eng="pool")

        s1(0)
        s1(1)
        s2(0)
        s1(2)
        s2(1)
        s3(0)
        s1(3)
        s2(2)
        s3(1)
        s2(3)
        s3(2)
        s3(3)

    def pool_mix(mix, srcs, cats):
        def band(g):
            f.tag = "pool_mix"
            gs_ = slice(g * 128, (g + 1) * 128)
            pb_ = nxt("pb", PB)
            for c in range(4):
                n = len(srcs[c])
                for i, (ut, bi) in enumerate(srcs[c]):
                    mm(pb_, pb_.ap[:, c * 128:(c + 1) * 128], ut.ap[:, gs_], bands.ap[:, bi, g, :], i == 0, i == n - 1, [ut, bands])
            for c in range(4):
                cs = slice(c * 128, (c + 1) * 128)
                tt(dmT[g], dmT[g].ap[:, cs], pb_.ap[:, cs], invc.ap[:, cats[c], g, :], ALU.mult, [pb_, invc])

        def proj(g):
            f.tag = "pool_mix"
            p = nxt("pb", PB)
            mm(p, p.ap[:], poolw.ap[:, g, :], dmT[g].ap[:], True, True, [poolw, dmT[g]])
            act(mix.k[4 + g], mix.ap[:, 4 + g, :], p.ap[:], AF.Copy, [p, vecs], scale=pscale[:, g:g + 1])

        band(0)
        band(1)
        proj(0)
        band(2)
        proj(1)
        band(3)
        proj(2)
        proj(3)

    def w_out_stage(h, mix, cond):
        f.tag = "w_out_stage"
        stats = Stats()
        pend = []
        for half in range(2):
            s, v = ring.get("w_out_c", lambda a_, half=half: a_.rearrange("(kc p) n -> p kc n", p=128)[:, :, half * 512:(half + 1) * 512], 8, 512)
            for j in range(4):
                dc = half * 4 + j
                p = nxt("pb", PB)
                for kc in range(8):
                    mm(p, p.ap[:], v[:, kc, j * 128:(j + 1) * 128], mix.ap[:, kc, :], kc == 0, kc == 7, [s, mix.k[kc]])
                stt(h.k[dc], h.ap[:, dc, :], p.ap[:], mder.ap[:, cond, 5, dc:dc + 1], h.ap[:, dc, :], ALU.mult, ALU.add, [p, mder.k[cond * 3 + 1], h.k[dc]])
                nxt_pend = stats.add(h.k[dc], h.ap[:, dc, :])
                for p_ in pend:
                    p_()
                pend = [nxt_pend]
        for p_ in pend:
            p_()
        return stats

    def final_out(h, tok0, stats):
        r = stats.finish()
        f.tag = "final_out"
        for kc in range(8):
            stt(h.k[kc], h.ap[:, kc, :], h.ap[:, kc, :], nfin[:, kc:kc + 1], r.ap[:], ALU.mult, ALU.mult, [h.k[kc], vecs, r])
        dma("sp", y_T.ap[:, :, tok0:tok0 + NT], h.ap[:], reads=h.all, writes=[y_T])

    def prompt_mixer(pt_, h, a):
        proj_fm(a, 512, 3072, kT, False, 0)
        k_tok_decay(0, True, True)
        proj_tm(a, 1024, v_tok, range(4))
        proj_fm(a, 0, 2560, qT, False, 0)
        proj_fm(a, 1536, None, sg, False, 0, silu=True)
        proj_tm(a, 2048, u_tok, range(4))
        for sq_ in range(2):
            c0, c1 = 2 * sq_, 2 * sq_ + 1
            seq = pt_ * 2 + sq_
            stf = nxt("sst", sst)
            stb = nxt("sst", sst)
            for hh in range(4):
                hs = slice(hh * 128, (hh + 1) * 128)
                pq = kv_mm(kdf, c0, hh)
                act(Sf_bf, Sf_bf.ap[:, c1, hs], pq.ap, AF.Copy, [pq])
                op("dve", lambda E, o=S32t.ap[:, hs], i=pq.ap: E.tensor_copy(out=o, in_=i), reads=[pq], writes=[S32t.k[hh]])
                pq2 = kv_mm(kdf, c1, hh)
                stt(stf, stf.ap[:, hs], S32t.ap[:, hs], CDt.ap[:, hh:hh + 1], pq2.ap, ALU.mult, ALU.add, [S32t.k[hh], CDt, pq2])
                pq3 = kv_mm(kdb, c1, hh)
                act(Sb_bf, Sb_bf.ap[:, c0, hs], pq3.ap, AF.Copy, [pq3])
                op("dve", lambda E, o=S32b.ap[:, hs], i=pq3.ap: E.tensor_copy(out=o, in_=i), reads=[pq3], writes=[S32b.k[hh]])
                pq4 = kv_mm(kdb, c0, hh)
                stt(stb, stb.ap[:, hs], S32b.ap[:, hs], CDt.ap[:, 4 + hh:5 + hh], pq4.ap, ALU.mult, ALU.add, [S32b.k[hh], CDt, pq4])
            dma("sp", nsf.ap[seq].rearrange("h d v -> d h v"), stf.ap[:].rearrange("p (h v) -> p h v", v=128), reads=[stf], writes=[nsf])
            dma("sp", nsb.ap[seq].rearrange("h d v -> d h v"), stb.ap[:].rearrange("p (h v) -> p h v", v=128), reads=[stb], writes=[nsb])
        retention_out(a, 0, [False, True, False, True], [True, False, True, False])
        pool_mix(a, [[(u_tok[0], 0), (u_tok[1], 1)], [(u_tok[1], 2), (u_tok[0], 3)],
                     [(u_tok[2], 0), (u_tok[3], 1)], [(u_tok[3], 2), (u_tok[2], 3)]], [0, 1, 0, 1])
        st2 = w_out_stage(h, a, 0)
        norm_mod(h, a, 0, 2, st2)

    H1 = {0: hT[2], 1: hT[0], 2: hT[1], 3: hT[2]}
    H2 = {3: hT[2], 2: hT[1], 1: hT[0], 0: hT[2]}
    load_x(0, hT[0])
    h_sq(hT[0], 0)
    dma("sp", csel.ap[:], csel_d.ap[:, :], reads=[csel_d], writes=[csel])
    act(scT, scT.ap[:].rearrange("p k c -> p (k c)"), condT.ap[:], AF.Silu, [condT])
    modps = nxt("pb", PB)
    col = 0
    while col < CPC * 128:
        ncol = min(512, CPC * 128 - col)
        slot, wv = ring.get("ada_w", lambda a_, col=col, ncol=ncol: a_.rearrange("(kc p) n -> p kc n", p=128)[:, :, col:col + ncol], 8, ncol)
        for j4 in range(ncol // 128):
            j = col // 128 + j4
            for kc in range(8):
                mm(modps, modps.ap[:, NCD * j:NCD * j + NCD], wv[:, kc, j4 * 128:(j4 + 1) * 128], scT.ap[:, kc, :], kc == 0, kc == 7, [slot, scT])
        col += ncol
    mloc = nxt("t32", tmp32)
    op("dve", lambda E: E.tensor_copy(out=mloc.ap[:, 0:CPC * NCD], in_=modps.ap[:, 0:CPC * NCD]), reads=[modps], writes=[mloc])
    dma("sp", mpay.ap[:, :], mloc.ap[:, 0:CPC * NCD], reads=[mloc], writes=[mpay])
    load_x(NT, hT[1])
    h_sq(hT[1], 1)
    load_x(1024, H1[0])
    f.custom_dma("pool", lambda E: E.collective_compute("AllGather", ALU.bypass, replica_groups=[list(range(AG * g, AG * g + AG)) for g in range(n_cores // AG)],
                                                        ins=[mpay.ap[:, :]], outs=[mgat.ap[:, :]]), reads=[mpay], writes=[mgat], inc=1)
    dma("pool", rotT.ap[:], rot_d.ap[:, :, :], reads=[rot_d], writes=[rotT])
    dma("pool", rmat.ap[:], rmat_d.ap[:, :], reads=[rmat_d], writes=[rmat])
    dma("pool", bands.ap[:, 0:4].rearrange("p a g t -> p (a g t)"), bands_d.ap[:, 0:2048], reads=[bands_d], writes=[bands])
    dma("pool", poolw.ap[:].rearrange("p g t -> p (g t)"), pool_w_d.ap[:, :], reads=[pool_w_d], writes=[poolw])
    dma("sp", G2.ap[:].rearrange("p (r n) -> p r n", r=AG), mgat.ap.rearrange("(r p) n -> p r n", p=128), reads=[mgat], writes=[G2])
    G2v = G2.ap[:].rearrange("p (j c) -> p j c", c=NCD)
    adab = vecs.ap[:, 40:112]
    tt(modT.k[0], modT.ap[:, :, 0], G2v[:, :, 0], adab, ALU.add, [G2, vecs])
    if NCD == 2:
        tt(modT.k[1], modT.ap[:, :, 1], G2v[:, :, 1], adab, ALU.add, [G2, vecs])
    else:
        tsel = nxt("t32", tmp32)
        ts(tsel, tsel.ap[:, 0:72], G2v[:, :, 1], csel.ap[:, 0:1], ALU.mult, [G2, csel])
        stt(tsel, tsel.ap[:, 0:72], G2v[:, :, 2], csel.ap[:, 1:2], tsel.ap[:, 0:72], ALU.mult, ALU.add, [G2, csel, tsel])
        tt(modT.k[1], modT.ap[:, :, 1], tsel.ap[:, 0:72], adab, ALU.add, [tsel, vecs])
    for c in range(2):
        for m in range(3):
            shj, scj, gj = 3 * m, 3 * m + 1, 3 * m + 2
            nrm = vecs.ap[:, m * 8:(m + 1) * 8]
            stt(mder.k[c * 3 + m], mder.ap[:, c, 3 * m + 0, :], modT.ap[:, scj * 8:(scj + 1) * 8, c], 1.0, nrm, ALU.add, ALU.mult, [modT.k[c], vecs])
            ts(mder.k[c * 3 + m], mder.ap[:, c, 3 * m + 1, :], modT.ap[:, shj * 8:(shj + 1) * 8, c], 1.0, ALU.mult, [modT.k[c]])
            ts(mder.k[c * 3 + m], mder.ap[:, c, 3 * m + 2, :], modT.ap[:, gj * 8:(gj + 1) * 8, c], (1.0 if m == 1 else 0.5), ALU.mult, [modT.k[c]])
    dump("modT", modT.k[0], [128, 72, 2], modT.ap[:])
    dump("mder", mder.k[0], [128, 2 * 9 * 8], mder.ap[:].rearrange("p a b c -> p (a b c)"))
    dump("Dmask", Dmask, [128, 8 * 128], Dmask.ap[:].rearrange("p a b c -> p (a b c)"))
    dump("QD", QD, [128, 16 * 128], QD.ap[:].rearrange("p a b c d -> p (a b c d)"))
    dump("KD", KD, [128, 16])
    dump("CDt", CDt, [128, 16])
    dump("lgT", lgT, [128, 16])
    grp = [(hT[0], aT[0]), (hT[1], aT[1])]
    for pt_ in range(2):
        norm_mod(hT[pt_], aT[pt_], 0, 0, slot=pt_)
    st1 = ffn_group(grp, 0, 0, 0)
    for pt_ in range(2):
        norm_mod(hT[pt_], aT[pt_], 0, 1, st1[pt_])
    for pt_ in range(2):
        prompt_mixer(pt_, hT[pt_], aT[pt_])
    st3 = ffn_group(grp, 0, 2, 1)
    for pt_ in range(2):
        final_out(hT[pt_], pt_ * NT, st3[pt_])

    dma("sp", invc.ap[:].rearrange("p a g t -> p (a g t)"), invc_d.ap[0:1, 1024:2048].partition_broadcast(128), reads=[invc_d], writes=[invc])
    dma("pool", bands.ap[:].rearrange("p a g t -> p (a g t)"), bands_d.ap[:, 2048:4608], reads=[bands_d], writes=[bands])

    for g0 in (0, 2):
        if g0 == 0:
            load_x(1024 + NT, H1[1])
        for i in (g0, g0 + 1):
            h_sq(H1[i], i % 2)
        if g0 == 0:
            load_x(1024 + 2 * NT, H1[2])
        for i in (g0, g0 + 1):
            norm_mod(H1[i], aT[i % 2], 1, 0, slot=i % 2)
        st1 = ffn_group([(H1[g0], aT[g0 % 2]), (H1[g0 + 1], aT[(g0 + 1) % 2])], 1, 0, 0)
        while bgq:
            bg()
        for k_, i in enumerate((g0, g0 + 1)):
            norm_mod(H1[i], aT[i % 2], 1, 1, st1[k_])
            if i == 0:
                dma("sp", hscr.ap[i], H1[i].ap[:].rearrange("p k t -> p (k t)"), reads=H1[i].all, writes=[hscr])
                load_x(1024 + 3 * NT, H1[3])
            if g0 == 0:
                dma("sp", ascr.ap[i], aT[i % 2].ap[:].rearrange("p k t -> p (k t)"), reads=aT[i % 2].all, writes=[ascr])
        for i in (g0, g0 + 1):
            h, a = H1[i], aT[i % 2]
            proj_fm(a, 512, 3072, kT, True, i * NT)
            while bgq:
                bg()
            k_tok_decay(1, True, False)
            proj_tm(a, 1024, v_tok, range(4))
            for j_ in range(4):
                for w_, lst_ in enumerate((kT, v_tok, kdf)):
                    dma("sp", kvscr.ap[i, w_][:, j_ * 512:(j_ + 1) * 512], lst_[j_].ap[:], reads=[lst_[j_]], writes=[kvscr])
            proj_tm(a, 2048, u_tok, [0, 3])
            op("pool", lambda E, o=u_save.ap[:, 2 * i, :], s=u_tok[0].ap[:]: E.tensor_copy(out=o, in_=s), reads=[u_tok[0]], writes=[u_save])
            op("pool", lambda E, o=u_save.ap[:, 2 * i + 1, :], s=u_tok[3].ap[:]: E.tensor_copy(out=o, in_=s), reads=[u_tok[3]], writes=[u_save])
            dma("sp", sscr.ap[i], S32f.ap[:], reads=S32f.all, writes=[sscr])
            for c in range(4):
                bgq.append(lambda c=c: scan_chunk(S32f, kdf, c, 8))
            if i == 3:
                while bgq:
                    bg()

    dma("sp", pay.ap[0:128, :], S32f.ap[:], reads=S32f.all, writes=[pay])
    dma("pool", pay.ap[128:256, :], u_save.ap[:, 7, :], reads=[u_save], writes=[pay])
    f.custom_dma("pool", lambda E: E.collective_compute("AllGather", ALU.bypass, replica_groups=[[2 * g, 2 * g + 1] for g in range(n_cores // 2)],
                                                        ins=[pay.ap[:, :]], outs=[gat.ap[:, :]]), reads=[pay], writes=[gat], inc=1)

    def load_kv(i):
        for j_ in range(4):
            for w_, lst_ in enumerate((kT, v_tok, kdf)):
                dma("sp", lst_[j_].ap[:], kvscr.ap[i, w_][:, j_ * 512:(j_ + 1) * 512], reads=[kvscr], writes=[lst_[j_]])

    first = True
    for g0 in (3, 1):
        for i in (g0, g0 - 1):
            h, a = H2[i], aT[i % 2]
            if i in (2, 0):
                load_kv(i)
            k_tok_decay(1, False, True)
            if first:
                first = False
                gv = gat.ap.rearrange("(r s p) n -> s p r n", r=2, s=2, p=128)
                dma("sp", G_S.ap[:], gv[0], reads=[gat], writes=[G_S])
                dma("sp", G_U.ap[:], gv[1], reads=[gat], writes=[G_U])
                ts(S32b.all, S32b.ap[:], G_S.ap[:, 0, :], msel.ap[:, 0:1], ALU.mult, [G_S, msel])
                stt(S32b.all, S32b.ap[:], G_S.ap[:, 1, :], msel.ap[:, 1:2], S32b.ap[:], ALU.mult, ALU.add, [G_S, msel] + S32b.all)
                t = nxt("t32", tmp32)
                ts(t, t.ap[:], G_U.ap[:, 0, :], msel.ap[:, 0:1], ALU.mult, [G_U, msel])
                stt(u_halo, u_halo.ap[:], G_U.ap[:, 1, :], msel.ap[:, 1:2], t.ap[:], ALU.mult, ALU.add, [G_U, msel, t])
            dma("sp", S32t.ap[:], sscr.ap[i], reads=[sscr], writes=S32t.all)

            def fwd_c(c):
                cpf = lambda hh, hs: act(Sf_bf, Sf_bf.ap[:, c, hs], S32t.ap[:, hs], AF.Copy, [S32t.k[hh]])
                if c < 3:
                    scan_chunk(S32t, kdf, c, 8, pre=cpf)
                else:
                    for hh in range(4):
                        cpf(hh, slice(hh * 128, (hh + 1) * 128))

            def bwd_c(c):
                cpb = lambda hh, hs: act(Sb_bf, Sb_bf.ap[:, c, hs], S32b.ap[:, hs], AF.Copy, [S32b.k[hh]])
                scan_chunk(S32b, kdb, c, 12, pre=cpb)

            for k2 in range(4):
                bgq.append(lambda c=3 - k2: bwd_c(c))
                bgq.append(lambda c=k2: fwd_c(c))
            proj_fm(a, 0, 2560, qT, True, i * NT)
            proj_fm(a, 1536, None, sg, True, i * NT, silu=True)
            proj_tm(a, 2048, u_tok, range(4))
            while bgq:
                bg()
            retention_out(a, 1, [True] * 4, [True] * 4)
            u_prev = Tile(u_save.ap[:, 2 * (i - 1) + 1, :], u_save.reg) if i > 0 else None
            u_next = Tile(u_save.ap[:, 2 * (i + 1), :], u_save.reg) if i < 3 else None
            srcs = []
            for c in range(4):
                l = [(u_tok[c], 0 if (i == 0 and c == 0) else 1)]
                if c > 0:
                    l.append((u_tok[c - 1], 3))
                elif u_prev is not None:
                    l.append((u_prev, 3))
                if c < 3:
                    l.append((u_tok[c + 1], 2))
                elif u_next is not None:
                    l.append((u_next, 2))
                else:
                    l.append((u_halo, 4))
                srcs.append(l)
            pool_mix(a, srcs, [0 if (i == 0 and c == 0) else 1 for c in range(4)])
            st2 = w_out_stage(h, a, 1)
            norm_mod(h, a, 1, 2, st2)
        st3 = ffn_group([(H2[g0], aT[g0 % 2]), (H2[g0 - 1], aT[(g0 - 1) % 2])], 1, 2, 1)
        if g0 == 3:
            load_kv(1)
            for i2 in (1, 0):
                dma("sp", aT[i2 % 2].ap[:].rearrange("p k t -> p (k t)"), ascr.ap[i2], reads=[ascr], writes=aT[i2 % 2].all)
        for k_, i in enumerate((g0, g0 - 1)):
            final_out(H2[i], 1024 + i * NT, st3[k_])
            if i == 3:
                dma("sp", H2[0].ap[:].rearrange("p k t -> p (k t)"), hscr.ap[0], reads=[hscr], writes=H2[0].all)
    return ring


_CACHE = {}


def build(n_cores=8):
    if ("nc", n_cores) in _CACHE:
        return _CACHE[("nc", n_cores)]
    nc0 = bass.Bass("TRN2", target_bir_lowering=False)
    f0 = FW(nc0, dry=True)
    r0 = record(nc0, f0, None, n_cores)
    seq = list(r0.rec)
    f0.close()
    nc = bass.Bass("TRN2", target_bir_lowering=False)
    f = FW(nc)
    r = record(nc, f, seq, n_cores)
    f.emit()
    f.close()
    _CACHE["sbuf_used"] = (f.sb_ptr - nc.sbuf_base, nc.sbuf_top - nc.sbuf_base)
    _CACHE[("nc", n_cores)] = nc
    _CACHE["stats"] = f.stats
    _CACHE["ops"] = [dict(eng=o["eng"], tag=o.get("tag", "dma")) for o in f.ops]
    return nc


def _pool_tables(role_b):
    WS = (2, 4, 8, 16)

    def mats(pos_t, pos_s, L, same):
        M = np.zeros((4, 128, 128), np.float32)
        V = np.zeros((4, 128), np.float32)
        for g, w in enumerate(WS):
            lo = np.clip(pos_t - w // 2, 0, L)
            hi = np.clip(pos_t + w // 2, 0, L)
            cnt = (hi - lo).astype(np.float32)
            inw = (pos_s[:, None] >= lo[None, :]) & (pos_s[:, None] < hi[None, :])
            M[g] = inw.astype(np.float32)
            if same:
                M[g][np.arange(128), np.arange(128)] -= cnt
            V[g] = 1.0 / cnt
        return M, V

    ar = np.arange(128)
    out_m, out_v = [], []
    m, v0 = mats(ar, ar, 256, True); out_m.append(m)
    m, _ = mats(ar, 128 + ar, 256, False); out_m.append(m)
    m, v1 = mats(128 + ar, 128 + ar, 256, True); out_m.append(m)
    m, _ = mats(128 + ar, ar, 256, False); out_m.append(m)
    L = 4096
    lpos = (lambda lc: 4095 - (lc * 128 + ar)) if role_b else (lambda lc: lc * 128 + ar)
    ppos = (lambda lc: lc * 128 + ar) if role_b else (lambda lc: 4095 - (lc * 128 + ar))
    m, v2 = mats(lpos(0), lpos(0), L, True); out_m.append(m)
    m, v3 = mats(lpos(1), lpos(1), L, True); out_m.append(m)
    m, _ = mats(lpos(1), lpos(2), L, False); out_m.append(m)
    m, _ = mats(lpos(1), lpos(0), L, False); out_m.append(m)
    m, _ = mats(lpos(15), ppos(15), L, False); out_m.append(m)
    bands = np.stack(out_m, 0)
    bands = np.ascontiguousarray(bands.transpose(2, 0, 1, 3)).reshape(128, 9 * 512)
    invc = np.stack([v0, v1, v2, v3], 0).reshape(1, 2048)
    return bands.astype(np.float32), invc.astype(np.float32)


def _rot_tables(role_b):
    l = np.arange(2048)
    pos = (4095 - l) if role_b else l
    row = (pos // 64).astype(np.float32)
    col = (pos % 64).astype(np.float32)
    n_half = 32
    freqs = (np.float32(10000.0) ** (-np.arange(n_half, dtype=np.float32) / np.float32(n_half))).astype(np.float32)
    ang = np.concatenate([row[:, None] * freqs, col[:, None] * freqs], axis=-1).astype(np.float32)
    cos = np.cos(ang).astype(np.float32).T
    sin = np.sin(ang).astype(np.float32).T
    C = np.concatenate([cos, cos], 0)
    S = np.concatenate([-sin, sin], 0)
    return np.ascontiguousarray(np.stack([C, S], 1)).astype(np.float32)


def _rmat():
    r = np.zeros((128, 128), np.float32)
    d = np.arange(128)
    r[(d + 64) % 128, d] = 1.0
    return r


def _ctab():
    j = np.arange(128, dtype=np.float32)[:, None]
    i = np.arange(128, dtype=np.float32)[None, :]
    s = np.float32(128.0 ** -0.5)
    rel1 = np.maximum(i - j, 0)
    m1 = (i >= j).astype(np.float32) * s
    rel2 = np.maximum(j - i, 0)
    m2 = (j >= i).astype(np.float32) * s
    idx1 = np.broadcast_to(i + 1, (128, 128))
    idxr = np.broadcast_to(128 - i, (128, 128))
    kidx = np.concatenate([127 - j, j], 1)
    return np.ascontiguousarray(np.concatenate([rel1, m1, rel2, m2, idx1, idxr, kidx], 1)).astype(np.float32)


def kernel(x_prompt, x_sample, state_ret_fwd, state_ret_bwd, c, c_ctx, ada_w, ada_b, norm_ffn1,
           ffn1_w1, ffn1_w3, ffn1_w2, norm_mix, w_in, ret_decay_fwd, ret_decay_bwd, ret_gn, pool_w,
           pool_scale, w_out, norm_ffn2, ffn2_w1, ffn2_w3, ffn2_w2, norm_final):
    in_maps = _prep(x_prompt, x_sample, state_ret_fwd, state_ret_bwd, c, c_ctx, ada_w, ada_b, norm_ffn1,
                    ffn1_w1, ffn1_w3, ffn1_w2, norm_mix, w_in, ret_decay_fwd, ret_decay_bwd, ret_gn, pool_w,
                    pool_scale, w_out, norm_ffn2, ffn2_w1, ffn2_w3, ffn2_w2, norm_final)
    nc = build()
    res = run_bass_kernel_spmd(nc, in_maps, core_ids=list(range(8)))
    return _assemble(res.results)


def _prep(x_prompt, x_sample, state_ret_fwd, state_ret_bwd, c, c_ctx, ada_w, ada_b, norm_ffn1,
          ffn1_w1, ffn1_w3, ffn1_w2, norm_mix, w_in, ret_decay_fwd, ret_decay_bwd, ret_gn, pool_w,
          pool_scale, w_out, norm_ffn2, ffn2_w1, ffn2_w3, ffn2_w2, norm_final, cores=range(8)):
    f32 = lambda a: np.ascontiguousarray(np.asarray(a, dtype=np.float32))
    x_prompt, x_sample = f32(x_prompt), f32(x_sample)
    w_in0 = f32(w_in)[0]
    sw = np.concatenate([np.r_[h * 128 + 64:h * 128 + 128, h * 128:h * 128 + 64] for h in range(4)])
    w_in_aug = np.ascontiguousarray(np.concatenate([w_in0, w_in0[:, sw], w_in0[:, 512 + sw]], axis=1))
    fm = lambda v, n: np.ascontiguousarray(f32(v).reshape(n, 128).T)
    vecs = np.concatenate([fm(norm_ffn1[0], 8), fm(norm_mix[0], 8), fm(norm_ffn2[0], 8), fm(norm_final, 8),
                           fm(ret_gn[0], 4), fm(pool_scale[0], 4), fm(ada_b[0], 72)], axis=1)
    pool_w_l = np.ascontiguousarray(f32(pool_w)[0].transpose(1, 0, 2).reshape(128, 512))
    ada_full = f32(ada_w)[0]
    n_cores = len(list(cores))
    shared = dict(ffn1_w1=f32(ffn1_w1)[0], ffn1_w3=f32(ffn1_w3)[0], ffn1_w2=f32(ffn1_w2)[0],
                  ffn2_w1=f32(ffn2_w1)[0], ffn2_w3=f32(ffn2_w3)[0], ffn2_w2=f32(ffn2_w2)[0], w_in=w_in_aug,
                  w_out=f32(w_out)[0], pool_w=pool_w_l,  vecs=np.ascontiguousarray(vecs), ctab=_ctab(),
                  ident=np.eye(128, dtype=np.float32), rmat=_rmat())
    tabs = {rb: (_pool_tables(rb), _rot_tables(rb)) for rb in (False, True)}
    df, db = f32(ret_decay_fwd)[0], f32(ret_decay_bwd)[0]
    in_maps = []
    for core in cores:
        b, rb = core // 2, bool(core % 2)
        xp = x_prompt[4 * core:4 * core + 4].reshape(1024, D)
        xs_ = x_sample[b, 2048:4096][::-1] if rb else x_sample[b, 0:2048]
        x_tok = np.concatenate([xp, xs_], 0)
        x_T = np.ascontiguousarray(x_tok.T.reshape(8, 128, 3072).transpose(1, 0, 2))
        ag = 4 if n_cores >= 4 else 2
        if ag == 4:
            b_lo = (core // 4) * 2
            cond = np.stack([f32(c_ctx), f32(c)[b_lo], f32(c)[b_lo + 1]], 0)
        else:
            cond = np.stack([f32(c_ctx), f32(c)[b]], 0)
        ncd = cond.shape[0]
        condT = np.ascontiguousarray(cond.reshape(ncd, 8, 128).transpose(2, 1, 0).reshape(128, 8 * ncd))
        csel = np.zeros((128, 2), np.float32)
        csel[:, b % 2] = 1.0
        cw = 9216 // ag
        dec = np.concatenate([df, db, (db if rb else df), (df if rb else db)]).reshape(1, 16)
        st = f32(state_ret_bwd if rb else state_ret_fwd)[b, 0]
        s_init = np.ascontiguousarray(st.transpose(1, 0, 2).reshape(128, 512))
        msel = np.zeros((128, 2), np.float32)
        msel[:, 0 if rb else 1] = 1.0
        (bands, invc), rot = tabs[rb]
        m = dict(shared)
        m.update(ada_w=np.ascontiguousarray(ada_full[:, (core % ag) * cw:(core % ag + 1) * cw]), csel=csel, x_T=x_T, condT=condT, dec=np.ascontiguousarray(dec.astype(np.float32)), s_init=s_init,
                 rot=rot, msel=msel, bands=bands, invcnt=invc)
        in_maps.append(m)
    return in_maps


def _assemble(results, cores=range(8)):
    y_prompt = np.empty((32, 256, D), np.float32)
    y_sample = np.empty((4, 4096, D), np.float32)
    new_f = np.empty((32, 1, 4, 128, 128), np.float32)
    new_b = np.empty((32, 1, 4, 128, 128), np.float32)
    for k_, core in enumerate(cores):
        r = results[k_]
        b, rb = core // 2, bool(core % 2)
        y = np.asarray(r["y_T"], dtype=np.float32).transpose(2, 1, 0).reshape(3072, D)
        y_prompt[4 * core:4 * core + 4] = y[0:1024].reshape(4, 256, D)
        if rb:
            y_sample[b, 2048:4096] = y[1024:][::-1]
        else:
            y_sample[b, 0:2048] = y[1024:]
        new_f[4 * core:4 * core + 4, 0] = np.asarray(r["nsf"], dtype=np.float32)
        new_b[4 * core:4 * core + 4, 0] = np.asarray(r["nsb"], dtype=np.float32)
    return (y_prompt, y_sample, new_f, new_b)
```

```python
import contextlib
import numpy as np
import concourse.bass as bass
import concourse.mybir as mybir
from concourse.bass_utils import run_bass_kernel_spmd

F32 = mybir.dt.float32
BF16 = mybir.dt.bfloat16
ALU = mybir.AluOpType
AF = mybir.ActivationFunctionType
ENGS = ("pe", "act", "dve", "pool", "sp")

D = 1024
DFF = 2816
NFT = 22
NT = 512
EPS = 1e-6
RING_SLOTS = 4
RING_ELEMS = 4096


class Reg:
    __slots__ = ("name", "last_w", "readers", "aliases", "lo", "hi", "psum")

    def __init__(self, name):
        self.name = name
        self.psum = False
        self.last_w = None
        self.readers = {}
        self.aliases = []
        self.lo = self.hi = None


class Tile:
    def __init__(self, ap, reg):
        self.ap = ap
        self.reg = reg


class FW:
    def __init__(self, nc, n_dma_sems=24, dry=False):
        self.nc = nc
        self.dry = dry
        self.ops = []
        self.stack = contextlib.ExitStack()
        self.n_dma_sems = n_dma_sems
        self.sb_regs = []
        self.tag = ""
        self.sb_ptr = nc.sbuf_base
        self.sb_top = nc.sbuf_top

    def reg(self, name):
        return Reg(name)

    def sbuf(self, name, shape, dtype, at=None):
        esz = 4 if dtype == F32 else 2
        nbytes = int(np.prod(shape[1:])) * esz
        if at is None:
            off = (self.sb_ptr + 31) // 32 * 32
            self.sb_ptr = off + nbytes
            assert self.sb_ptr <= self.sb_top, f"SBUF overflow at {name}: {self.sb_ptr} > {self.sb_top}"
        else:
            off = at
        t = self.nc.alloc_sbuf_tensor_at(name, list(shape), dtype, offset=off)
        r = Reg(name)
        r.lo, r.hi = off, off + nbytes
        for o in self.sb_regs:
            if o.lo < r.hi and r.lo < o.hi:
                o.aliases.append(r)
                r.aliases.append(o)
        self.sb_regs.append(r)
        return Tile(t, r)

    def reserve(self, nbytes):
        off = (self.sb_ptr + 31) // 32 * 32
        self.sb_ptr = off + nbytes
        assert self.sb_ptr <= self.sb_top, f"SBUF overflow (reserve): {self.sb_ptr} > {self.sb_top}"
        return off

    def psum(self, name, shape, dtype):
        t = self.stack.enter_context(self.nc.psum_tensor(name, list(shape), dtype))
        r = Reg(name)
        r.psum = True
        return Tile(t, r)

    def dram(self, name, shape, dtype, kind="ExternalInput", **kw):
        t = self.nc.dram_tensor(name, list(shape), dtype, kind=kind, **kw)
        return Tile(t.ap(), Reg(name))

    def view(self, ap, name):
        return Tile(ap, Reg(name))

    def _deps(self, idx, eng, is_dma, reads, writes):
        ps_reads = [t for t in reads if t.reg.psum]
        if ps_reads:
            reads = [t for t in reads if not t.reg.psum]
            writes = list(writes) + [t for t in ps_reads if all(t.reg is not w.reg for w in writes)]
        deps = set()
        for t in reads:
            r = t.reg
            for rr in [r] + r.aliases:
                if rr.last_w is not None:
                    deps.add(rr.last_w)
        for t in writes:
            r = t.reg
            for rr in [r] + r.aliases:
                if rr.last_w is not None:
                    deps.add(rr.last_w)
                deps.update(rr.readers.values())
        key = ("dma", idx) if is_dma else eng
        for t in reads:
            t.reg.readers[key] = idx
        for t in writes:
            t.reg.last_w = idx
            t.reg.readers = {}
        deps.discard(idx)
        return deps

    def op(self, eng, fn, reads=(), writes=()):
        idx = len(self.ops)
        deps = self._deps(idx, eng, False, reads, writes)
        self.ops.append(dict(eng=eng, fn=fn, deps=deps, is_dma=False, flag=False, tag=self.tag))

    def dma(self, eng, out, in_, reads=(), writes=(), **kw):
        idx = len(self.ops)
        deps = self._deps(idx, eng, True, reads, writes)
        self.ops.append(dict(eng=eng, fn=None, out=out, in_=in_, kw=kw, deps=deps, is_dma=True, flag=False))

    def custom_dma(self, eng, fn, reads=(), writes=(), inc=16):
        idx = len(self.ops)
        deps = self._deps(idx, eng, True, reads, writes)
        self.ops.append(dict(eng=eng, fn=fn, deps=deps, is_dma=True, flag=False, inc=inc))

    def emit(self):
        nc, ops = self.nc, self.ops
        for o in ops:
            comp, dmas = {}, []
            for d in o["deps"]:
                p = ops[d]
                if p["is_dma"]:
                    dmas.append(d)
                else:
                    if p["eng"] == "pe" and o["eng"] == "pe" and not o["is_dma"]:
                        continue
                    if p["eng"] not in comp or comp[p["eng"]] < d:
                        comp[p["eng"]] = d
            o["cdeps"] = comp
            o["ddeps"] = sorted(dmas)
            for d in comp.values():
                ops[d]["flag"] = True
        cnt = {e: 0 for e in ENGS}
        dma_i = {"hw": 0, "sw": 0}
        n_hw = self.n_dma_sems // 2
        sem_uses = [0] * self.n_dma_sems
        for o in ops:
            if o["is_dma"]:
                if o["fn"] is not None:
                    s = self.n_dma_sems - 1
                elif o["eng"] == "pool":
                    s = n_hw + dma_i["sw"] % (self.n_dma_sems - 1 - n_hw)
                    dma_i["sw"] += 1
                else:
                    s = dma_i["hw"] % n_hw
                    dma_i["hw"] += 1
                o["dsem"] = s
                o["dprev"] = sem_uses[s]
                sem_uses[s] += o.get("inc", 16)
                o["dval"] = sem_uses[s]
            elif o["flag"]:
                cnt[o["eng"]] += 1
                o["cnt"] = cnt[o["eng"]]
        st = self.stack
        esem = {e: st.enter_context(nc.semaphore(f"s_{e}")) for e in ENGS if e != "sp"}
        dsem = [st.enter_context(nc.semaphore(f"s_dma{k}")) for k in range(self.n_dma_sems)]
        block = st.enter_context(nc.Block())
        per_eng = {e: [] for e in ENGS}
        for i, o in enumerate(ops):
            per_eng[o["eng"]].append(i)
        self.stats = {e: len(v) for e, v in per_eng.items()}

        def run(eng_name, E):
            seen_c = {e: 0 for e in ENGS}
            seen_d = [0] * self.n_dma_sems
            for i in per_eng[eng_name]:
                o = ops[i]
                for pe_, d in o["cdeps"].items():
                    v = ops[d]["cnt"]
                    if seen_c[pe_] < v:
                        E.wait_ge(esem[pe_], v)
                        seen_c[pe_] = v
                for d in o["ddeps"]:
                    p = ops[d]
                    if seen_d[p["dsem"]] < p["dval"]:
                        E.wait_ge(dsem[p["dsem"]], p["dval"])
                        seen_d[p["dsem"]] = p["dval"]
                if o["is_dma"]:
                    s = o["dsem"]
                    if seen_d[s] < o["dprev"]:
                        E.wait_ge(dsem[s], o["dprev"])
                        seen_d[s] = o["dprev"]
                    if o["fn"] is None:
                        ins = E.dma_start(out=o["out"], in_=o["in_"], **o["kw"])
                    else:
                        ins = o["fn"](E)
                    ins.then_inc(dsem[s], o.get("inc", 16))
                else:
                    ins = o["fn"](E)
                    if o["flag"]:
                        ins.then_inc(esem[eng_name], 1)
            if eng_name == "sp":
                for s in range(self.n_dma_sems):
                    if sem_uses[s] > seen_d[s]:
                        E.wait_ge(dsem[s], sem_uses[s])
                for e in ENGS:
                    if e != "sp" and cnt[e] > 0:
                        E.wait_ge(esem[e], cnt[e])

        @block.tensor
        def _(E):
            run("pe", E)

        @block.scalar
        def _(E):
            run("act", E)

        @block.vector
        def _(E):
            run("dve", E)

        @block.gpsimd
        def _(E):
            run("pool", E)

        @block.sync
        def _(E):
            run("sp", E)

    def close(self):
        self.stack.close()


class Ring:
    def __init__(self, f, slots, seq, tiles):
        self.f = f
        self.slots = slots
        self.seq = seq
        self.tiles = tiles
        self.rec = []
        self.i = 0
        self.loaded = 0

    def get(self, name, apfn, k, n):
        i = self.i
        self.i += 1
        self.rec.append((name, apfn, k, n))
        S = len(self.slots)
        if self.seq is not None:
            while self.loaded < min(len(self.seq), i + S - 1):
                j = self.loaded
                nm, fn, kk, nn = self.seq[j]
                st = self.tiles[nm]
                slot = self.slots[j % S]
                dst = slot.ap[:, 0:kk * nn].rearrange("p (k n) -> p k n", n=nn)
                self.f.dma("pool", dst, fn(st.ap), reads=[st], writes=[slot])
                self.loaded += 1
        slot = self.slots[i % S]
        return slot, slot.ap[:, 0:k * n].rearrange("p (k n) -> p k n", n=n)


def record(nc, f, ring_seq, n_cores=8):
    dr = f.dram
    x_T = dr("x_T", [128, 8, 3072], F32)
    y_T = dr("y_T", [128, 8, 3072], F32, kind="ExternalOutput")
    nsf = dr("nsf", [4, 4, 128, 128], F32, kind="ExternalOutput")
    nsb = dr("nsb", [4, 4, 128, 128], F32, kind="ExternalOutput")
    condT_d = dr("condT", [128, 8 * (3 if n_cores >= 4 else 2)], F32)
    dec_d = dr("dec", [1, 16], F32)
    sinit_d = dr("s_init", [128, 512], F32)
    rot_d = dr("rot", [128, 2, 2048], F32)
    msel_d = dr("msel", [128, 2], F32)
    bands_d = dr("bands", [128, 9 * 512], F32)
    invc_d = dr("invcnt", [1, 2048], F32)
    ctab_d = dr("ctab", [128, 6 * 128 + 2], F32)
    ident_d = dr("ident", [128, 128], F32)
    rmat_d = dr("rmat", [128, 128], F32)
    vecs_d = dr("vecs", [128, 8 * 4 + 4 + 4 + 72], F32)
    AG = 4 if n_cores >= 4 else 2
    NCD = 3 if AG == 4 else 2
    CPC = 72 // AG
    ada_w = dr("ada_w", [D, CPC * 128], F32)
    csel_d = dr("csel", [128, 2], F32)
    mpay = dr("mpay", [128, CPC * NCD], F32, kind="Internal", addr_space="Local")
    mgat = dr("mgat", [AG * 128, CPC * NCD], F32, kind="Internal", addr_space="Local")
    w1 = [dr("ffn1_w1", [D, DFF], F32), dr("ffn2_w1", [D, DFF], F32)]
    w3 = [dr("ffn1_w3", [D, DFF], F32), dr("ffn2_w3", [D, DFF], F32)]
    w2 = [dr("ffn1_w2", [DFF, D], F32), dr("ffn2_w2", [DFF, D], F32)]
    w_in = dr("w_in", [D, 3584], F32)
    w_out = dr("w_out", [D, D], F32)
    pool_w_d = dr("pool_w", [128, 512], F32)
    w_in_c = dr("w_in_c", [D, 3584], BF16, kind="Internal")
    w_out_c = dr("w_out_c", [D, D], BF16, kind="Internal")
    hscr = dr("hscr", [4, 128, 8 * NT], F32, kind="Internal")
    sscr = dr("sscr", [4, 128, 512], F32, kind="Internal")
    ascr = dr("ascr", [2, 128, 8 * NT], BF16, kind="Internal")
    kvscr = dr("kvscr", [4, 3, 128, 2048], BF16, kind="Internal")
    pay = dr("pay", [256, 512], F32, kind="Internal", addr_space="Local")
    gat = dr("gat", [512, 512], F32, kind="Internal", addr_space="Local")

    sb = f.sbuf
    class Multi:
        def __init__(self, t, n):
            self.ap = t.ap
            self.k = [Tile(t.ap, Reg(f"{t.reg.name}_{j}")) for j in range(n)]
            self.all = list(self.k)

    hT = [Multi(sb(f"hT{i}", [128, 8, NT], F32), 8) for i in range(3)]
    slots = [sb(f"ring{i}", [128, RING_ELEMS], BF16) for i in range(RING_SLOTS)]
    wt = dict(ada_w=ada_w, ffn1_w1=w1[0], ffn2_w1=w1[1], ffn1_w3=w3[0], ffn2_w3=w3[1], ffn1_w2=w2[0], ffn2_w2=w2[1],
              w_in=w_in, w_out=w_out, w_in_c=w_in_c, w_out_c=w_out_c)
    ring = Ring(f, slots, ring_seq, wt)
    u_save = sb("u_save", [128, 8, 512], BF16)
    rotT = sb("rotT", [128, 2, 2048], BF16)
    bands = sb("bands", [128, 5, 4, 128], BF16)
    invc = sb("invc", [128, 2, 4, 128], F32)
    Dmask = sb("Dmask", [128, 2, 4, 128], F32)
    QD = sb("QD", [128, 2, 2, 4, 128], BF16)
    KD = sb("KD", [128, 16], F32)
    CDt = sb("CDt", [128, 16], F32)
    lgT = sb("lgT", [128, 16], F32)
    modT = Multi(sb("modT", [128, 72, 2], F32), 2)
    vecs = sb("vecs", [128, 112], F32)
    mder = Multi(sb("mder", [128, 2, 9, 8], F32), 6)
    condT = sb("condT", [128, 8 * NCD], F32)
    scT = sb("scT", [128, 8, NCD], BF16)
    G2 = sb("G2", [128, 72 * NCD], F32)
    csel = sb("csel", [128, 2], F32)
    ident = sb("ident", [128, 128], F32)
    identb = sb("identb", [128, 128], BF16)
    rmat = sb("rmat", [128, 128], BF16)
    ones_d = sb("ones_d", [128, 128], BF16)
    ones_v = sb("ones_v", [128, 128], BF16)
    epsb = sb("epsb", [128, 1], F32)
    msel = sb("msel", [128, 2], F32)
    poolw = sb("poolw", [128, 4, 128], BF16)
    S32f = Multi(sb("S32f", [128, 512], F32), 4)
    S32b = Multi(sb("S32b", [128, 512], F32), 4)
    S32t = Multi(sb("S32t", [128, 512], F32), 4)
    sq8 = None
    Sf_bf = sb("Sf_bf", [128, 4, 512], BF16)
    Sb_bf = sb("Sb_bf", [128, 4, 512], BF16)
    sst = [sb(f"sst{i}", [128, 512], F32) for i in range(2)]
    rstd = [sb(f"rstd{i}", [128, NT], F32) for i in range(2)]
    tmp32 = [sb(f"tmp32_{i}", [128, NT], F32) for i in range(3)]
    stmp = [sb(f"stmp{i}", [128, NT], BF16) for i in range(3)]
    ptb = [sb(f"ptb{i}", [128, 128], BF16) for i in range(12)]
    aT = [Multi(sb(f"aT{i}", [128, 8, NT], BF16), 8) for i in range(2)]
    SCR = f.reserve(36864)
    ctab = sb("ctab", [128, 6 * 128 + 2], F32, at=SCR)
    gT = [sb(f"gT{ft}", [128, NT], BF16, at=SCR + ft * 1024) for ft in range(NFT)]
    yT = [sb(f"yT{kc}", [128, NT], F32, at=SCR + kc * 2048) for kc in range(8)]
    qT = [sb(f"qT{h}", [128, NT], BF16, at=SCR + h * 1024) for h in range(4)]
    kT = [sb(f"kT{h}", [128, NT], BF16, at=SCR + 4096 + h * 1024) for h in range(4)]
    qdr = [sb(f"qdr{h}", [128, NT], BF16, at=SCR + 8192 + h * 1024) for h in range(4)] + [sb(f"qdr{4 + h}", [128, NT], BF16) for h in range(2)]
    sg = [sb(f"sg{h}", [128, NT], BF16, at=SCR + 12288 + h * 1024) for h in range(4)]
    v_tok = [sb(f"v_tok{c}", [128, 512], BF16, at=SCR + 16384 + c * 1024) for c in range(4)]
    u_tok = [sb(f"u_tok{c}", [128, 512], BF16, at=SCR + 20480 + c * 1024) for c in range(4)]
    dmT = [sb(f"dmT{g}", [128, NT], BF16, at=SCR + 24576 + g * 1024) for g in range(4)]
    kdf = [sb(f"kdf{c}", [128, 512], BF16, at=SCR + 28672 + c * 1024) for c in range(4)]
    kdb = [sb(f"kdb{c}", [128, 512], BF16, at=SCR + 32768 + c * 1024) for c in range(4)]
    sq8 = [sb(f"sq8_{i}", [128, 8, NT], BF16, at=SCR + i * 8192) for i in range(2)]
    G_S = sb("G_S", [128, 2, 512], F32, at=SCR + 24576)
    G_U = sb("G_U", [128, 2, 512], F32, at=SCR + 8192)
    u_halo = sb("u_halo", [128, 512], BF16)


    PB = [f.psum(f"pb{i}", [128, NT], F32) for i in range(8)]
    PQ = [Tile(p.ap[:, 0:128], p.reg) for p in PB]
    PSB = [Tile(p.ap[:, 0:256].bitcast(BF16), p.reg) for p in PB]
    ctr = dict(pb=0, pq=0, psb=0, t32=0, st=0, ptb=0, rs=0, xs=0, ys=0, sst=0, qd=0, ssp=0)

    busy_banks = set()

    def nxt(key, lst):
        if key in ("pq", "psb", "pb"):
            while (ctr["pb"] % 8) in busy_banks:
                ctr["pb"] += 1
            v = lst[ctr["pb"] % 8]
            ctr["pb"] += 1
            return v
        v = lst[ctr[key] % len(lst)]
        ctr[key] += 1
        return v

    class Stats:
        def __init__(self, n=8, ones=None):
            while (ctr["pb"] % 8) in busy_banks:
                ctr["pb"] += 1
            self.idx = ctr["pb"] % 8
            ctr["pb"] += 1
            busy_banks.add(self.idx)
            self.pb = PB[self.idx]
            self.n = n
            self.i = 0
            self.ones = ones if ones is not None else ones_d

        def add(self, src_t, src_ap):
            s_ = nxt("st", stmp)
            act(s_, s_.ap[:], src_ap, AF.Square, [src_t])
            i = self.i
            self.i += 1
            return lambda: mm(self.pb, self.pb.ap[:], self.ones.ap[:], s_.ap[:], i == 0, i == self.n - 1, [self.ones, s_])

        def finish(self):
            r = nxt("rs", rstd)
            act(r, r.ap[:], self.pb.ap[:], AF.Ln, [self.pb, epsb], bias=epsb.ap[:, 0:1], scale=1.0)
            act(r, r.ap[:], r.ap[:], AF.Exp, [r], scale=-0.5)
            busy_banks.discard(self.idx)
            return r

    op, dma = f.op, f.dma
    import os as _os
    DBG = bool(_os.environ.get("K_DEBUG"))

    def dump(name, t, shape, ap=None):
        if not DBG:
            return
        d_ = dr("dbg_" + name, list(shape), F32 if True else None, kind="ExternalOutput")
        src = ap if ap is not None else t.ap[:]
        if len(shape) == 3:
            dst = d_.ap[:, :, :]
        else:
            dst = d_.ap[:, :]
        dma("pool", dst, src, reads=[t], writes=[d_])

    def wl(t):
        return t if isinstance(t, list) else [t]

    def mm(out_t, out_ap, lhsT, rhs, start, stop, reads):
        op("pe", lambda E: E.matmul(out_ap, lhsT=lhsT, rhs=rhs, start=start, stop=stop), reads=reads, writes=wl(out_t))

    def act(out_t, out_ap, in_ap, func, reads, scale=None, bias=None):
        kw = {}
        if scale is not None:
            kw["scale"] = scale
        if bias is not None:
            kw["bias"] = bias
        op("act", lambda E: E.activation(out=out_ap, in_=in_ap, func=func, **kw), reads=reads, writes=wl(out_t))

    def tt(out_t, out_ap, in0, in1, alu, reads, eng="dve"):
        op(eng, lambda E: E.tensor_tensor(out=out_ap, in0=in0, in1=in1, op=alu), reads=reads, writes=wl(out_t))

    def stt(out_t, out_ap, in0, scalar, in1, op0, op1, reads):
        op("dve", lambda E: E.scalar_tensor_tensor(out=out_ap, in0=in0, scalar=scalar, in1=in1, op0=op0, op1=op1),
           reads=reads, writes=wl(out_t))

    def ts(out_t, out_ap, in0, s1, op0, reads, s2=None, op1=None, eng="dve"):
        if op1 is None:
            op(eng, lambda E: E.tensor_scalar(out=out_ap, in0=in0, scalar1=s1, scalar2=None, op0=op0), reads=reads, writes=wl(out_t))
        else:
            op(eng, lambda E: E.tensor_scalar(out=out_ap, in0=in0, scalar1=s1, scalar2=s2, op0=op0, op1=op1), reads=reads, writes=wl(out_t))

    dma("sp", condT.ap[:], condT_d.ap[:, :], reads=[condT_d], writes=[condT])
    dma("sp", vecs.ap[:], vecs_d.ap[:, :], reads=[vecs_d], writes=[vecs])
    dma("sp", ident.ap[:], ident_d.ap[:, :], reads=[ident_d], writes=[ident])
    dma("sp", ctab.ap[:], ctab_d.ap[:, :], reads=[ctab_d], writes=[ctab])
    dma("sp", lgT.ap[:], dec_d.ap[0:1, :].partition_broadcast(128), reads=[dec_d], writes=[lgT])
    dma("sp", msel.ap[:], msel_d.ap[:, :], reads=[msel_d], writes=[msel])
    dma("sp", invc.ap[:].rearrange("p a g t -> p (a g t)"), invc_d.ap[0:1, 0:1024].partition_broadcast(128), reads=[invc_d], writes=[invc])
    dma("sp", S32f.ap[:], sinit_d.ap[:, :], reads=[sinit_d], writes=S32f.all)
    op("dve", lambda E: E.memset(epsb.ap[:], EPS), writes=[epsb])
    op("dve", lambda E: E.memset(ones_d.ap[:], 1.0 / D), writes=[ones_d])
    op("dve", lambda E: E.memset(ones_v.ap[:], 1.0 / 128.0), writes=[ones_v])
    op("dve", lambda E: E.tensor_copy(out=identb.ap[:], in_=ident.ap[:]), reads=[ident], writes=[identb])
    act(lgT, lgT.ap[:], lgT.ap[:], AF.Exp, [lgT])
    ts(lgT, lgT.ap[:], lgT.ap[:], -1.0, ALU.mult, [lgT])
    REL1, M1s, REL2, M2s, IDX1, IDXR = [ctab.ap[:, i * 128:(i + 1) * 128] for i in range(6)]
    kidx = ctab.ap[:, 768:770]
    SC = 128.0 ** -0.5
    for st_ in range(2):
        for h in range(4):
            cf = st_ * 8 + h
            cb = st_ * 8 + 4 + h
            t1 = nxt("t32", tmp32)
            act(t1, t1.ap[:, 0:128], REL1, AF.Exp, [ctab, lgT], scale=lgT.ap[:, cf:cf + 1])
            tt(t1, t1.ap[:, 0:128], t1.ap[:, 0:128], M1s, ALU.mult, [t1, ctab])
            t2 = nxt("t32", tmp32)
            act(t2, t2.ap[:, 0:128], REL2, AF.Exp, [ctab, lgT], scale=lgT.ap[:, cb:cb + 1])
            tt(t2, t2.ap[:, 0:128], t2.ap[:, 0:128], M2s, ALU.mult, [t2, ctab])
            tt(Dmask, Dmask.ap[:, st_, h, :], t1.ap[:, 0:128], t2.ap[:, 0:128], ALU.add, [t1, t2])
            act(QD, QD.ap[:, st_, 0, h, :], IDX1, AF.Exp, [ctab, lgT], scale=lgT.ap[:, cf:cf + 1])
            act(QD, QD.ap[:, st_, 1, h, :], IDXR, AF.Exp, [ctab, lgT], scale=lgT.ap[:, cb:cb + 1])
            act(KD, KD.ap[:, cf:cf + 1], kidx[:, 0:1], AF.Exp, [ctab, lgT], scale=lgT.ap[:, cf:cf + 1])
            act(KD, KD.ap[:, cb:cb + 1], kidx[:, 1:2], AF.Exp, [ctab, lgT], scale=lgT.ap[:, cb:cb + 1])
    ts(KD, KD.ap[:], KD.ap[:], SC, ALU.mult, [KD])
    act(CDt, CDt.ap[:], lgT.ap[:], AF.Exp, [lgT], scale=128.0)

    nfin = vecs.ap[:, 24:32]
    gn = vecs.ap[:, 32:36]
    pscale = vecs.ap[:, 36:40]

    def load_x(tok0, h):
        f.tag = "load_x"
        dma("sp", h.ap[:], x_T.ap[:, :, tok0:tok0 + NT], reads=[x_T], writes=h.all)

    def rms_rstd(src_tiles, src_aps, ones, n):
        f.tag = "rms_rstd"
        pb = nxt("pb", PB)
        for i in range(n):
            s = nxt("st", stmp)
            act(s, s.ap[:], src_aps[i], AF.Square, [src_tiles[i]])
            mm(pb, pb.ap[:], ones.ap[:], s.ap[:], i == 0, i == n - 1, [ones, s])
        r = nxt("rs", rstd)
        act(r, r.ap[:], pb.ap[:], AF.Ln, [pb, epsb], bias=epsb.ap[:, 0:1], scale=1.0)
        act(r, r.ap[:], r.ap[:], AF.Exp, [r], scale=-0.5)
        return r

    def h_sq(h, slot):
        f.tag = "rms_rstd"
        for half in range(2):
            act(sq8[slot], sq8[slot].ap[:, half * 4:(half + 1) * 4, :], h.ap[:, half * 4:(half + 1) * 4, :], AF.Square, h.k[half * 4:(half + 1) * 4])

    def h_rstd(h, slot):
        f.tag = "rms_rstd"
        pb = nxt("pb", PB)
        for kc in range(8):
            mm(pb, pb.ap[:], ones_d.ap[:], sq8[slot].ap[:, kc, :], kc == 0, kc == 7, [ones_d, sq8[slot]])
        r = nxt("rs", rstd)
        act(r, r.ap[:], pb.ap[:], AF.Ln, [pb, epsb], bias=epsb.ap[:, 0:1], scale=1.0)
        act(r, r.ap[:], r.ap[:], AF.Exp, [r], scale=-0.5)
        return r

    def norm_mod(h, a, cond, m, stats=None, slot=0):
        r = stats.finish() if stats is not None else h_rstd(h, slot)
        f.tag = "norm_mod"
        for kc in range(8):
            t = nxt("t32", tmp32)
            tt(t, t.ap[:], h.ap[:, kc, :], r.ap[:], ALU.mult, [h.k[kc], r])
            act(a.k[kc], a.ap[:, kc, :], t.ap[:], AF.Identity, [t, mder.k[cond * 3 + m]],
                scale=mder.ap[:, cond, 3 * m, kc:kc + 1], bias=mder.ap[:, cond, 3 * m + 1, kc:kc + 1])

    fills = []
    for c0_ in range(0, 2560, 512):
        fills.append((w_in_c, w_in, c0_))
    for c0_ in range(0, 1024, 512):
        fills.append((w_out_c, w_out, c0_))

    def fill_one():
        if fills:
            dst_, src_, c0_ = fills.pop(0)
            dma("pool", dst_.ap[:, c0_:c0_ + 512], src_.ap[:, c0_:c0_ + 512], reads=[src_], writes=[dst_])

    def ffn_group(tiles, cond, m, which, want_stats=True):
        f.tag = "ffn_group"
        stats = None
        pend = []
        n1, n3, n2 = [f"ffn{which + 1}_w{x}" for x in (1, 3, 2)]
        for half in range(2):
            ft0 = half * 11
            col, rem = ft0 * 128, 11 * 128
            while rem > 0:
                ncol = min(512, rem)
                cut = lambda a_, col=col, ncol=ncol: a_.rearrange("(kc p) n -> p kc n", p=128)[:, :, col: col + ncol]
                s1, v1 = ring.get(n1, cut, 8, ncol)
                fill_one()
                s3, v3 = ring.get(n3, cut, 8, ncol)
                fill_one()
                for j in range(ncol // 128):
                    ftl = (col - ft0 * 128) // 128 + j
                    for ti, (h, a) in enumerate(tiles):
                        p1 = nxt("pb", PB)
                        for kc in range(8):
                            mm(p1, p1.ap[:], v1[:, kc, j * 128:(j + 1) * 128], a.ap[:, kc, :], kc == 0, kc == 7, [s1, a.k[kc]])
                        p3 = nxt("pb", PB)
                        for kc in range(8):
                            mm(p3, p3.ap[:], v3[:, kc, j * 128:(j + 1) * 128], a.ap[:, kc, :], kc == 0, kc == 7, [s3, a.k[kc]])
                        s = nxt("st", stmp)
                        act(s, s.ap[:], p1.ap[:], AF.Silu, [p1])
                        g = gT[ti * 11 + ftl]
                        tt(g, g.ap[:], p3.ap[:], s.ap[:], ALU.mult, [p3, s])
                    if ftl % 2 == 1:
                        bg()
                        f.tag = "ffn_group"
                col += ncol
                rem -= ncol
            if half == 1 and want_stats:
                stats = [Stats() for _ in tiles]
            for d2 in range(4):
                s2, v2 = ring.get(n2, lambda a_, ft0=ft0, d2=d2: a_.rearrange("(ft p) n -> p ft n", p=128)[:, ft0:ft0 + 11, d2 * 256:(d2 + 1) * 256], 11, 256)
                for dj in range(2):
                    dc = d2 * 2 + dj
                    for ti, (h, a) in enumerate(tiles):
                        py = nxt("pb", PB)
                        for ftl in range(11):
                            g = gT[ti * 11 + ftl]
                            mm(py, py.ap[:], v2[:, ftl, dj * 128:(dj + 1) * 128], g.ap[:], ftl == 0, ftl == 10, [s2, g])
                        stt(h.k[dc], h.ap[:, dc, :], py.ap[:], mder.ap[:, cond, 3 * m + 2, dc:dc + 1], h.ap[:, dc, :], ALU.mult, ALU.add, [py, mder.k[cond * 3 + m], h.k[dc]])
                        if half == 1 and want_stats:
                            nxt_pend = stats[ti].add(h.k[dc], h.ap[:, dc, :])
                            for p_ in pend:
                                p_()
                            pend = [nxt_pend]
                            f.tag = "ffn_group"
        for p_ in pend:
            p_()
        return stats

    bgq = []

    def bg():
        if bgq:
            tg = f.tag
            bgq.pop(0)()
            f.tag = tg

    def w_unit(c0):
        while fills:
            fill_one()
        return ring.get("w_in_c", lambda a_, c0=c0: a_.rearrange("(kc p) n -> p kc n", p=128)[:, :, c0:c0 + 512], 8, 512)

    def proj_fm(a, c0, c0_sw, outs, is_sample, tok_off, silu=False):
        f.tag = "proj_fm"
        s, v = w_unit(c0)
        rot = c0_sw is not None and is_sample
        pend = None
        for h in range(4):
            p = nxt("pb", PB)
            for kc in range(8):
                mm(p, p.ap[:], v[:, kc, h * 128:(h + 1) * 128], a.ap[:, kc, :], kc == 0, kc == 7, [s, a.k[kc]])
            if rot:
                p_idx = PB.index(p)
                busy_banks.add(p_idx)
                qb = nxt("st", stmp)
                act(qb, qb.ap[:], p.ap[:], AF.Copy, [p])

                def fin(h=h, p=p, p_idx=p_idx, qb=qb):
                    p2 = nxt("pb", PB)
                    mm(p2, p2.ap[:], rmat.ap[:], qb.ap[:], True, True, [rmat, qb])
                    t1 = nxt("t32", tmp32)
                    tt(t1, t1.ap[:], p.ap[:], rotT.ap[:, 0, tok_off:tok_off + NT], ALU.mult, [p, rotT])
                    busy_banks.discard(p_idx)
                    t2 = nxt("t32", tmp32)
                    tt(t2, t2.ap[:], p2.ap[:], rotT.ap[:, 1, tok_off:tok_off + NT], ALU.mult, [p2, rotT])
                    tt(outs[h], outs[h].ap[:], t1.ap[:], t2.ap[:], ALU.add, [t1, t2], eng="pool")

                if pend is not None:
                    pend()
                pend = fin
            else:
                act(outs[h], outs[h].ap[:], p.ap[:], AF.Silu if silu else AF.Copy, [p])
            bg()
        if pend is not None:
            pend()

    def proj_tm(a, c0, outs, chunks):
        f.tag = "proj_tm"
        s, v = w_unit(c0)
        for c in chunks:
            p = nxt("pb", PB)
            for kc in range(8):
                mm(p, p.ap[:], a.ap[:, kc, c * 128:(c + 1) * 128], v[:, kc, :], kc == 0, kc == 7, [s, a.k[kc]])
            if c % 2 == 0:
                act(outs[c], outs[c].ap[:], p.ap[:], AF.Copy, [p])
            else:
                op("dve", lambda E, o=outs[c].ap[:], i=p.ap[:]: E.tensor_copy(out=o, in_=i), reads=[p], writes=[outs[c]])
            bg()

    def k_tok_decay(st_, want_f, want_b):
        f.tag = "k_tok_decay"
        for c in range(4):
            pb_ = nxt("psb", PSB)
            for h in range(4):
                op("pe", lambda E, o=pb_.ap[:, h * 128:(h + 1) * 128], i=kT[h].ap[:, c * 128:(c + 1) * 128]: E.transpose(o, i, identb.ap[:]),
                   reads=[kT[h], identb], writes=[pb_])
            for h in range(4):
                hs = slice(h * 128, (h + 1) * 128)
                if want_f:
                    act(kdf[c], kdf[c].ap[:, hs], pb_.ap[:, hs], AF.Copy, [pb_, KD], scale=KD.ap[:, st_ * 8 + h: st_ * 8 + h + 1])
                if want_b:
                    ts(kdb[c], kdb[c].ap[:, hs], pb_.ap[:, hs], KD.ap[:, st_ * 8 + 4 + h: st_ * 8 + 4 + h + 1], ALU.mult, [pb_, KD])

    def kv_mm(kd, c, h):
        f.tag = "kv_mm"
        pq = nxt("pq", PQ)
        mm(pq, pq.ap, kd[c].ap[:, h * 128:(h + 1) * 128], v_tok[c].ap[:, h * 128:(h + 1) * 128], True, True, [kd[c], v_tok[c]])
        return pq

    def scan_step(S32, kd, c, h, cdcol):
        pq = kv_mm(kd, c, h)
        hs = slice(h * 128, (h + 1) * 128)
        stt(S32.k[h], S32.ap[:, hs], S32.ap[:, hs], CDt.ap[:, cdcol:cdcol + 1], pq.ap, ALU.mult, ALU.add, [S32.k[h], CDt, pq])

    def scan_chunk(S32, kd, c, cd0, pre=None):
        f.tag = "kv_mm"
        pb_ = nxt("pb", PB)
        for h in range(4):
            hs = slice(h * 128, (h + 1) * 128)
            mm(pb_, pb_.ap[:, hs], kd[c].ap[:, hs], v_tok[c].ap[:, hs], True, True, [kd[c], v_tok[c]])
        for h in range(4):
            hs = slice(h * 128, (h + 1) * 128)
            if pre is not None:
                pre(h, hs)
            stt(S32.k[h], S32.ap[:, hs], S32.ap[:, hs], CDt.ap[:, cd0 + h:cd0 + h + 1], pb_.ap[:, hs], ALU.mult, ALU.add, [S32.k[h], CDt, pb_])

    def retention_out(mix, st_, have_f, have_b):
        st = {}

        def s1(h):
            f.tag = "retention_out"
            qf = nxt("qd", qdr)
            qb = nxt("qd", qdr)
            for c in range(4):
                cs = slice(c * 128, (c + 1) * 128)
                if have_f[c]:
                    tt(qf, qf.ap[:, cs], qT[h].ap[:, cs], QD.ap[:, st_, 0, h, :], ALU.mult, [qT[h], QD], eng="pool")
                if have_b[c]:
                    tt(qb, qb.ap[:, cs], qT[h].ap[:, cs], QD.ap[:, st_, 1, h, :], ALU.mult, [qT[h], QD], eng="pool")
            pts = []
            for c in range(4):
                cs = slice(c * 128, (c + 1) * 128)
                pq = nxt("pq", PQ)
                mm(pq, pq.ap, kT[h].ap[:, cs], qT[h].ap[:, cs], True, True, [kT[h], qT[h]])
                pt = nxt("ptb", ptb)
                tt(pt, pt.ap[:], pq.ap, Dmask.ap[:, st_, h, :], ALU.mult, [pq, Dmask])
                pts.append(pt)
            st[h] = dict(qf=qf, qb=qb, pts=pts)

        def s2(h):
            f.tag = "retention_out"
            hs = slice(h * 128, (h + 1) * 128)
            d_ = st[h]
            po = nxt("pb", PB)
            po_idx = PB.index(po)
            busy_banks.add(po_idx)
            for c in range(4):
                cs = slice(c * 128, (c + 1) * 128)
                last = not (have_f[c] or have_b[c])
                mm(po, po.ap[:, cs], v_tok[c].ap[:, hs], d_["pts"][c].ap[:], True, last, [v_tok[c], d_["pts"][c]])
                if have_f[c]:
                    mm(po, po.ap[:, cs], Sf_bf.ap[:, c, hs], d_["qf"].ap[:, cs], False, not have_b[c], [Sf_bf, d_["qf"]])
                if have_b[c]:
                    mm(po, po.ap[:, cs], Sb_bf.ap[:, c, hs], d_["qb"].ap[:, cs], False, True, [Sb_bf, d_["qb"]])
            s_ = nxt("st", stmp)
            act(s_, s_.ap[:], po.ap[:], AF.Square, [po])
            d_.update(po=po, po_idx=po_idx, s_=s_)

        def s3(h):
            f.tag = "ret_epi"
            d_ = st[h]
            po, s_ = d_["po"], d_["s_"]
            p2 = nxt("pb", PB)
            mm(p2, p2.ap[:], ones_v.ap[:], s_.ap[:], True, True, [ones_v, s_])
            r = nxt("rs", rstd)
            act(r, r.ap[:], p2.ap[:], AF.Ln, [p2, epsb], bias=epsb.ap[:, 0:1], scale=1.0)
            act(r, r.ap[:], r.ap[:], AF.Exp, [r], scale=-0.5)
            t = nxt("t32", tmp32)
            stt(t, t.ap[:], po.ap[:], gn[:, h:h + 1], r.ap[:], ALU.mult, ALU.mult, [po, vecs, r])
            busy_banks.discard(d_["po_idx"])
            tt(mix.k[h], mix.ap[:, h, :], t.ap[:], sg[h].ap[:], ALU.mult, [t, sg[h]], eng="pool")

        s1(0)
        s1(1)
        s2(0)
        s1(2)
        s2(1)
        s3(0)
        s1(3)
        s2(2)
        s3(1)
        s2(3)
        s3(2)
        s3(3)

    def pool_mix(mix, srcs, cats):
        def band(g):
            f.tag = "pool_mix"
            gs_ = slice(g * 128, (g + 1) * 128)
            pb_ = nxt("pb", PB)
            for c in range(4):
                n = len(srcs[c])
                for i, (ut, bi) in enumerate(srcs[c]):
                    mm(pb_, pb_.ap[:, c * 128:(c + 1) * 128], ut.ap[:, gs_], bands.ap[:, bi, g, :], i == 0, i == n - 1, [ut, bands])
            for c in range(4):
                cs = slice(c * 128, (c + 1) * 128)
                tt(dmT[g], dmT[g].ap[:, cs], pb_.ap[:, cs], invc.ap[:, cats[c], g, :], ALU.mult, [pb_, invc])

        def proj(g):
            f.tag = "pool_mix"
            p = nxt("pb", PB)
            mm(p, p.ap[:], poolw.ap[:, g, :], dmT[g].ap[:], True, True, [poolw, dmT[g]])
            act(mix.k[4 + g], mix.ap[:, 4 + g, :], p.ap[:], AF.Copy, [p, vecs], scale=pscale[:, g:g + 1])

        band(0)
        band(1)
        proj(0)
        band(2)
        proj(1)
        band(3)
        proj(2)
        proj(3)

    def w_out_stage(h, mix, cond):
        f.tag = "w_out_stage"
        stats = Stats()
        pend = []
        for half in range(2):
            s, v = ring.get("w_out_c", lambda a_, half=half: a_.rearrange("(kc p) n -> p kc n", p=128)[:, :, half * 512:(half + 1) * 512], 8, 512)
            for j in range(4):
                dc = half * 4 + j
                p = nxt("pb", PB)
                for kc in range(8):
                    mm(p, p.ap[:], v[:, kc, j * 128:(j + 1) * 128], mix.ap[:, kc, :], kc == 0, kc == 7, [s, mix.k[kc]])
                stt(h.k[dc], h.ap[:, dc, :], p.ap[:], mder.ap[:, cond, 5, dc:dc + 1], h.ap[:, dc, :], ALU.mult, ALU.add, [p, mder.k[cond * 3 + 1], h.k[dc]])
                nxt_pend = stats.add(h.k[dc], h.ap[:, dc, :])
                for p_ in pend:
                    p_()
                pend = [nxt_pend]
        for p_ in pend:
            p_()
        return stats

    def final_out(h, tok0, stats):
        r = stats.finish()
        f.tag = "final_out"
        for kc in range(8):
            stt(h.k[kc], h.ap[:, kc, :], h.ap[:, kc, :], nfin[:, kc:kc + 1], r.ap[:], ALU.mult, ALU.mult, [h.k[kc], vecs, r])
        dma("sp", y_T.ap[:, :, tok0:tok0 + NT], h.ap[:], reads=h.all, writes=[y_T])

    def prompt_mixer(pt_, h, a):
        proj_fm(a, 512, 3072, kT, False, 0)
        k_tok_decay(0, True, True)
        proj_tm(a, 1024, v_tok, range(4))
        proj_fm(a, 0, 2560, qT, False, 0)
        proj_fm(a, 1536, None, sg, False, 0, silu=True)
        proj_tm(a, 2048, u_tok, range(4))
        for sq_ in range(2):
            c0, c1 = 2 * sq_, 2 * sq_ + 1
            seq = pt_ * 2 + sq_
            stf = nxt("sst", sst)
            stb = nxt("sst", sst)
            for hh in range(4):
                hs = slice(hh * 128, (hh + 1) * 128)
                pq = kv_mm(kdf, c0, hh)
                act(Sf_bf, Sf_bf.ap[:, c1, hs], pq.ap, AF.Copy, [pq])
                op("dve", lambda E, o=S32t.ap[:, hs], i=pq.ap: E.tensor_copy(out=o, in_=i), reads=[pq], writes=[S32t.k[hh]])
                pq2 = kv_mm(kdf, c1, hh)
                stt(stf, stf.ap[:, hs], S32t.ap[:, hs], CDt.ap[:, hh:hh + 1], pq2.ap, ALU.mult, ALU.add, [S32t.k[hh], CDt, pq2])
                pq3 = kv_mm(kdb, c1, hh)
                act(Sb_bf, Sb_bf.ap[:, c0, hs], pq3.ap, AF.Copy, [pq3])
                op("dve", lambda E, o=S32b.ap[:, hs], i=pq3.ap: E.tensor_copy(out=o, in_=i), reads=[pq3], writes=[S32b.k[hh]])
                pq4 = kv_mm(kdb, c0, hh)
                stt(stb, stb.ap[:, hs], S32b.ap[:, hs], CDt.ap[:, 4 + hh:5 + hh], pq4.ap, ALU.mult, ALU.add, [S32b.k[hh], CDt, pq4])
            dma("sp", nsf.ap[seq].rearrange("h d v -> d h v"), stf.ap[:].rearrange("p (h v) -> p h v", v=128), reads=[stf], writes=[nsf])
            dma("sp", nsb.ap[seq].rearrange("h d v -> d h v"), stb.ap[:].rearrange("p (h v) -> p h v", v=128), reads=[stb], writes=[nsb])
        retention_out(a, 0, [False, True, False, True], [True, False, True, False])
        pool_mix(a, [[(u_tok[0], 0), (u_tok[1], 1)], [(u_tok[1], 2), (u_tok[0], 3)],
                     [(u_tok[2], 0), (u_tok[3], 1)], [(u_tok[3], 2), (u_tok[2], 3)]], [0, 1, 0, 1])
        st2 = w_out_stage(h, a, 0)
        norm_mod(h, a, 0, 2, st2)

    H1 = {0: hT[2], 1: hT[0], 2: hT[1], 3: hT[2]}
    H2 = {3: hT[2], 2: hT[1], 1: hT[0], 0: hT[2]}
    load_x(0, hT[0])
    h_sq(hT[0], 0)
    dma("sp", csel.ap[:], csel_d.ap[:, :], reads=[csel_d], writes=[csel])
    act(scT, scT.ap[:].rearrange("p k c -> p (k c)"), condT.ap[:], AF.Silu, [condT])
    modps = nxt("pb", PB)
    col = 0
    while col < CPC * 128:
        ncol = min(512, CPC * 128 - col)
        slot, wv = ring.get("ada_w", lambda a_, col=col, ncol=ncol: a_.rearrange("(kc p) n -> p kc n", p=128)[:, :, col:col + ncol], 8, ncol)
        for j4 in range(ncol // 128):
            j = col // 128 + j4
            for kc in range(8):
                mm(modps, modps.ap[:, NCD * j:NCD * j + NCD], wv[:, kc, j4 * 128:(j4 + 1) * 128], scT.ap[:, kc, :], kc == 0, kc == 7, [slot, scT])
        col += ncol
    mloc = nxt("t32", tmp32)
    op("dve", lambda E: E.tensor_copy(out=mloc.ap[:, 0:CPC * NCD], in_=modps.ap[:, 0:CPC * NCD]), reads=[modps], writes=[mloc])
    dma("sp", mpay.ap[:, :], mloc.ap[:, 0:CPC * NCD], reads=[mloc], writes=[mpay])
    load_x(NT, hT[1])
    h_sq(hT[1], 1)
    load_x(1024, H1[0])
    f.custom_dma("pool", lambda E: E.collective_compute("AllGather", ALU.bypass, replica_groups=[list(range(AG * g, AG * g + AG)) for g in range(n_cores // AG)],
                                                        ins=[mpay.ap[:, :]], outs=[mgat.ap[:, :]]), reads=[mpay], writes=[mgat], inc=1)
    dma("pool", rotT.ap[:], rot_d.ap[:, :, :], reads=[rot_d], writes=[rotT])
    dma("pool", rmat.ap[:], rmat_d.ap[:, :], reads=[rmat_d], writes=[rmat])
    dma("pool", bands.ap[:, 0:4].rearrange("p a g t -> p (a g t)"), bands_d.ap[:, 0:2048], reads=[bands_d], writes=[bands])
    dma("pool", poolw.ap[:].rearrange("p g t -> p (g t)"), pool_w_d.ap[:, :], reads=[pool_w_d], writes=[poolw])
    dma("sp", G2.ap[:].rearrange("p (r n) -> p r n", r=AG), mgat.ap.rearrange("(r p) n -> p r n", p=128), reads=[mgat], writes=[G2])
    G2v = G2.ap[:].rearrange("p (j c) -> p j c", c=NCD)
    adab = vecs.ap[:, 40:112]
    tt(modT.k[0], modT.ap[:, :, 0], G2v[:, :, 0], adab, ALU.add, [G2, vecs])
    if NCD == 2:
        tt(modT.k[1], modT.ap[:, :, 1], G2v[:, :, 1], adab, ALU.add, [G2, vecs])
    else:
        tsel = nxt("t32", tmp32)
        ts(tsel, tsel.ap[:, 0:72], G2v[:, :, 1], csel.ap[:, 0:1], ALU.mult, [G2, csel])
        stt(tsel, tsel.ap[:, 0:72], G2v[:, :, 2], csel.ap[:, 1:2], tsel.ap[:, 0:72], ALU.mult, ALU.add, [G2, csel, tsel])
        tt(modT.k[1], modT.ap[:, :, 1], tsel.ap[:, 0:72], adab, ALU.add, [tsel, vecs])
    for c in range(2):
        for m in range(3):
            shj, scj, gj = 3 * m, 3 * m + 1, 3 * m + 2
            nrm = vecs.ap[:, m * 8:(m + 1) * 8]
            stt(mder.k[c * 3 + m], mder.ap[:, c, 3 * m + 0, :], modT.ap[:, scj * 8:(scj + 1) * 8, c], 1.0, nrm, ALU.add, ALU.mult, [modT.k[c], vecs])
            ts(mder.k[c * 3 + m], mder.ap[:, c, 3 * m + 1, :], modT.ap[:, shj * 8:(shj + 1) * 8, c], 1.0, ALU.mult, [modT.k[c]])
            ts(mder.k[c * 3 + m], mder.ap[:, c, 3 * m + 2, :], modT.ap[:, gj * 8:(gj + 1) * 8, c], (1.0 if m == 1 else 0.5), ALU.mult, [modT.k[c]])
    dump("modT", modT.k[0], [128, 72, 2], modT.ap[:])
    dump("mder", mder.k[0], [128, 2 * 9 * 8], mder.ap[:].rearrange("p a b c -> p (a b c)"))
    dump("Dmask", Dmask, [128, 8 * 128], Dmask.ap[:].rearrange("p a b c -> p (a b c)"))
    dump("QD", QD, [128, 16 * 128], QD.ap[:].rearrange("p a b c d -> p (a b c d)"))
    dump("KD", KD, [128, 16])
    dump("CDt", CDt, [128, 16])
    dump("lgT", lgT, [128, 16])
    grp = [(hT[0], aT[0]), (hT[1], aT[1])]
    for pt_ in range(2):
        norm_mod(hT[pt_], aT[pt_], 0, 0, slot=pt_)
    st1 = ffn_group(grp, 0, 0, 0)
    for pt_ in range(2):
        norm_mod(hT[pt_], aT[pt_], 0, 1, st1[pt_])
    for pt_ in range(2):
        prompt_mixer(pt_, hT[pt_], aT[pt_])
    st3 = ffn_group(grp, 0, 2, 1)
    for pt_ in range(2):
        final_out(hT[pt_], pt_ * NT, st3[pt_])

    dma("sp", invc.ap[:].rearrange("p a g t -> p (a g t)"), invc_d.ap[0:1, 1024:2048].partition_broadcast(128), reads=[invc_d], writes=[invc])
    dma("pool", bands.ap[:].rearrange("p a g t -> p (a g t)"), bands_d.ap[:, 2048:4608], reads=[bands_d], writes=[bands])

    for g0 in (0, 2):
        if g0 == 0:
            load_x(1024 + NT, H1[1])
        for i in (g0, g0 + 1):
            h_sq(H1[i], i % 2)
        if g0 == 0:
            load_x(1024 + 2 * NT, H1[2])
        for i in (g0, g0 + 1):
            norm_mod(H1[i], aT[i % 2], 1, 0, slot=i % 2)
        st1 = ffn_group([(H1[g0], aT[g0 % 2]), (H1[g0 + 1], aT[(g0 + 1) % 2])], 1, 0, 0)
        while bgq:
            bg()
        for k_, i in enumerate((g0, g0 + 1)):
            norm_mod(H1[i], aT[i % 2], 1, 1, st1[k_])
            if i == 0:
                dma("sp", hscr.ap[i], H1[i].ap[:].rearrange("p k t -> p (k t)"), reads=H1[i].all, writes=[hscr])
                load_x(1024 + 3 * NT, H1[3])
            if g0 == 0:
                dma("sp", ascr.ap[i], aT[i % 2].ap[:].rearrange("p k t -> p (k t)"), reads=aT[i % 2].all, writes=[ascr])
        for i in (g0, g0 + 1):
            h, a = H1[i], aT[i % 2]
            proj_fm(a, 512, 3072, kT, True, i * NT)
            while bgq:
                bg()
            k_tok_decay(1, True, False)
            proj_tm(a, 1024, v_tok, range(4))
            for j_ in range(4):
                for w_, lst_ in enumerate((kT, v_tok, kdf)):
                    dma("sp", kvscr.ap[i, w_][:, j_ * 512:(j_ + 1) * 512], lst_[j_].ap[:], reads=[lst_[j_]], writes=[kvscr])
            proj_tm(a, 2048, u_tok, [0, 3])
            op("pool", lambda E, o=u_save.ap[:, 2 * i, :], s=u_tok[0].ap[:]: E.tensor_copy(out=o, in_=s), reads=[u_tok[0]], writes=[u_save])
            op("pool", lambda E, o=u_save.ap[:, 2 * i + 1, :], s=u_tok[3].ap[:]: E.tensor_copy(out=o, in_=s), reads=[u_tok[3]], writes=[u_save])
            dma("sp", sscr.ap[i], S32f.ap[:], reads=S32f.all, writes=[sscr])
            for c in range(4):
                bgq.append(lambda c=c: scan_chunk(S32f, kdf, c, 8))
            if i == 3:
                while bgq:
                    bg()

    dma("sp", pay.ap[0:128, :], S32f.ap[:], reads=S32f.all, writes=[pay])
    dma("pool", pay.ap[128:256, :], u_save.ap[:, 7, :], reads=[u_save], writes=[pay])
    f.custom_dma("pool", lambda E: E.collective_compute("AllGather", ALU.bypass, replica_groups=[[2 * g, 2 * g + 1] for g in range(n_cores // 2)],
                                                        ins=[pay.ap[:, :]], outs=[gat.ap[:, :]]), reads=[pay], writes=[gat], inc=1)

    def load_kv(i):
        for j_ in range(4):
            for w_, lst_ in enumerate((kT, v_tok, kdf)):
                dma("sp", lst_[j_].ap[:], kvscr.ap[i, w_][:, j_ * 512:(j_ + 1) * 512], reads=[kvscr], writes=[lst_[j_]])

    first = True
    for g0 in (3, 1):
        for i in (g0, g0 - 1):
            h, a = H2[i], aT[i % 2]
            if i in (2, 0):
                load_kv(i)
            k_tok_decay(1, False, True)
            if first:
                first = False
                gv = gat.ap.rearrange("(r s p) n -> s p r n", r=2, s=2, p=128)
                dma("sp", G_S.ap[:], gv[0], reads=[gat], writes=[G_S])
                dma("sp", G_U.ap[:], gv[1], reads=[gat], writes=[G_U])
                ts(S32b.all, S32b.ap[:], G_S.ap[:, 0, :], msel.ap[:, 0:1], ALU.mult, [G_S, msel])
                stt(S32b.all, S32b.ap[:], G_S.ap[:, 1, :], msel.ap[:, 1:2], S32b.ap[:], ALU.mult, ALU.add, [G_S, msel] + S32b.all)
                t = nxt("t32", tmp32)
                ts(t, t.ap[:], G_U.ap[:, 0, :], msel.ap[:, 0:1], ALU.mult, [G_U, msel])
                stt(u_halo, u_halo.ap[:], G_U.ap[:, 1, :], msel.ap[:, 1:2], t.ap[:], ALU.mult, ALU.add, [G_U, msel, t])
            dma("sp", S32t.ap[:], sscr.ap[i], reads=[sscr], writes=S32t.all)

            def fwd_c(c):
                cpf = lambda hh, hs: act(Sf_bf, Sf_bf.ap[:, c, hs], S32t.ap[:, hs], AF.Copy, [S32t.k[hh]])
                if c < 3:
                    scan_chunk(S32t, kdf, c, 8, pre=cpf)
                else:
                    for hh in range(4):
                        cpf(hh, slice(hh * 128, (hh + 1) * 128))

            def bwd_c(c):
                cpb = lambda hh, hs: act(Sb_bf, Sb_bf.ap[:, c, hs], S32b.ap[:, hs], AF.Copy, [S32b.k[hh]])
                scan_chunk(S32b, kdb, c, 12, pre=cpb)

            for k2 in range(4):
                bgq.append(lambda c=3 - k2: bwd_c(c))
                bgq.append(lambda c=k2: fwd_c(c))
            proj_fm(a, 0, 2560, qT, True, i * NT)
            proj_fm(a, 1536, None, sg, True, i * NT, silu=True)
            proj_tm(a, 2048, u_tok, range(4))
            while bgq:
                bg()
            retention_out(a, 1, [True] * 4, [True] * 4)
            u_prev = Tile(u_save.ap[:, 2 * (i - 1) + 1, :], u_save.reg) if i > 0 else None
            u_next = Tile(u_save.ap[:, 2 * (i + 1), :], u_save.reg) if i < 3 else None
            srcs = []
            for c in range(4):
                l = [(u_tok[c], 0 if (i == 0 and c == 0) else 1)]
                if c > 0:
                    l.append((u_tok[c - 1], 3))
                elif u_prev is not None:
                    l.append((u_prev, 3))
                if c < 3:
                    l.append((u_tok[c + 1], 2))
                elif u_next is not None:
                    l.append((u_next, 2))
                else:
                    l.append((u_halo, 4))
                srcs.append(l)
            pool_mix(a, srcs, [0 if (i == 0 and c == 0) else 1 for c in range(4)])
            st2 = w_out_stage(h, a, 1)
            norm_mod(h, a, 1, 2, st2)
        st3 = ffn_group([(H2[g0], aT[g0 % 2]), (H2[g0 - 1], aT[(g0 - 1) % 2])], 1, 2, 1)
        if g0 == 3:
            load_kv(1)
            for i2 in (1, 0):
                dma("sp", aT[i2 % 2].ap[:].rearrange("p k t -> p (k t)"), ascr.ap[i2], reads=[ascr], writes=aT[i2 % 2].all)
        for k_, i in enumerate((g0, g0 - 1)):
            final_out(H2[i], 1024 + i * NT, st3[k_])
            if i == 3:
                dma("sp", H2[0].ap[:].rearrange("p k t -> p (k t)"), hscr.ap[0], reads=[hscr], writes=H2[0].all)
    return ring


_CACHE = {}


def build(n_cores=8):
    if ("nc", n_cores) in _CACHE:
        return _CACHE[("nc", n_cores)]
    nc0 = bass.Bass("TRN2", target_bir_lowering=False)
    f0 = FW(nc0, dry=True)
    r0 = record(nc0, f0, None, n_cores)
    seq = list(r0.rec)
    f0.close()
    nc = bass.Bass("TRN2", target_bir_lowering=False)
    f = FW(nc)
    r = record(nc, f, seq, n_cores)
    f.emit()
    f.close()
    _CACHE["sbuf_used"] = (f.sb_ptr - nc.sbuf_base, nc.sbuf_top - nc.sbuf_base)
    _CACHE[("nc", n_cores)] = nc
    _CACHE["stats"] = f.stats
    _CACHE["ops"] = [dict(eng=o["eng"], tag=o.get("tag", "dma")) for o in f.ops]
    return nc


def _pool_tables(role_b):
    WS = (2, 4, 8, 16)

    def mats(pos_t, pos_s, L, same):
        M = np.zeros((4, 128, 128), np.float32)
        V = np.zeros((4, 128), np.float32)
        for g, w in enumerate(WS):
            lo = np.clip(pos_t - w // 2, 0, L)
            hi = np.clip(pos_t + w // 2, 0, L)
            cnt = (hi - lo).astype(np.float32)
            inw = (pos_s[:, None] >= lo[None, :]) & (pos_s[:, None] < hi[None, :])
            M[g] = inw.astype(np.float32)
            if same:
                M[g][np.arange(128), np.arange(128)] -= cnt
            V[g] = 1.0 / cnt
        return M, V

    ar = np.arange(128)
    out_m, out_v = [], []
    m, v0 = mats(ar, ar, 256, True); out_m.append(m)
    m, _ = mats(ar, 128 + ar, 256, False); out_m.append(m)
    m, v1 = mats(128 + ar, 128 + ar, 256, True); out_m.append(m)
    m, _ = mats(128 + ar, ar, 256, False); out_m.append(m)
    L = 4096
    lpos = (lambda lc: 4095 - (lc * 128 + ar)) if role_b else (lambda lc: lc * 128 + ar)
    ppos = (lambda lc: lc * 128 + ar) if role_b else (lambda lc: 4095 - (lc * 128 + ar))
    m, v2 = mats(lpos(0), lpos(0), L, True); out_m.append(m)
    m, v3 = mats(lpos(1), lpos(1), L, True); out_m.append(m)
    m, _ = mats(lpos(1), lpos(2), L, False); out_m.append(m)
    m, _ = mats(lpos(1), lpos(0), L, False); out_m.append(m)
    m, _ = mats(lpos(15), ppos(15), L, False); out_m.append(m)
    bands = np.stack(out_m, 0)
    bands = np.ascontiguousarray(bands.transpose(2, 0, 1, 3)).reshape(128, 9 * 512)
    invc = np.stack([v0, v1, v2, v3], 0).reshape(1, 2048)
    return bands.astype(np.float32), invc.astype(np.float32)


def _rot_tables(role_b):
    l = np.arange(2048)
    pos = (4095 - l) if role_b else l
    row = (pos // 64).astype(np.float32)
    col = (pos % 64).astype(np.float32)
    n_half = 32
    freqs = (np.float32(10000.0) ** (-np.arange(n_half, dtype=np.float32) / np.float32(n_half))).astype(np.float32)
    ang = np.concatenate([row[:, None] * freqs, col[:, None] * freqs], axis=-1).astype(np.float32)
    cos = np.cos(ang).astype(np.float32).T
    sin = np.sin(ang).astype(np.float32).T
    C = np.concatenate([cos, cos], 0)
    S = np.concatenate([-sin, sin], 0)
    return np.ascontiguousarray(np.stack([C, S], 1)).astype(np.float32)


def _rmat():
    r = np.zeros((128, 128), np.float32)
    d = np.arange(128)
    r[(d + 64) % 128, d] = 1.0
    return r


def _ctab():
    j = np.arange(128, dtype=np.float32)[:, None]
    i = np.arange(128, dtype=np.float32)[None, :]
    s = np.float32(128.0 ** -0.5)
    rel1 = np.maximum(i - j, 0)
    m1 = (i >= j).astype(np.float32) * s
    rel2 = np.maximum(j - i, 0)
    m2 = (j >= i).astype(np.float32) * s
    idx1 = np.broadcast_to(i + 1, (128, 128))
    idxr = np.broadcast_to(128 - i, (128, 128))
    kidx = np.concatenate([127 - j, j], 1)
    return np.ascontiguousarray(np.concatenate([rel1, m1, rel2, m2, idx1, idxr, kidx], 1)).astype(np.float32)


def kernel(x_prompt, x_sample, state_ret_fwd, state_ret_bwd, c, c_ctx, ada_w, ada_b, norm_ffn1,
           ffn1_w1, ffn1_w3, ffn1_w2, norm_mix, w_in, ret_decay_fwd, ret_decay_bwd, ret_gn, pool_w,
           pool_scale, w_out, norm_ffn2, ffn2_w1, ffn2_w3, ffn2_w2, norm_final):
    in_maps = _prep(x_prompt, x_sample, state_ret_fwd, state_ret_bwd, c, c_ctx, ada_w, ada_b, norm_ffn1,
                    ffn1_w1, ffn1_w3, ffn1_w2, norm_mix, w_in, ret_decay_fwd, ret_decay_bwd, ret_gn, pool_w,
                    pool_scale, w_out, norm_ffn2, ffn2_w1, ffn2_w3, ffn2_w2, norm_final)
    nc = build()
    res = run_bass_kernel_spmd(nc, in_maps, core_ids=list(range(8)))
    return _assemble(res.results)


def _prep(x_prompt, x_sample, state_ret_fwd, state_ret_bwd, c, c_ctx, ada_w, ada_b, norm_ffn1,
          ffn1_w1, ffn1_w3, ffn1_w2, norm_mix, w_in, ret_decay_fwd, ret_decay_bwd, ret_gn, pool_w,
          pool_scale, w_out, norm_ffn2, ffn2_w1, ffn2_w3, ffn2_w2, norm_final, cores=range(8)):
    f32 = lambda a: np.ascontiguousarray(np.asarray(a, dtype=np.float32))
    x_prompt, x_sample = f32(x_prompt), f32(x_sample)
    w_in0 = f32(w_in)[0]
    sw = np.concatenate([np.r_[h * 128 + 64:h * 128 + 128, h * 128:h * 128 + 64] for h in range(4)])
    w_in_aug = np.ascontiguousarray(np.concatenate([w_in0, w_in0[:, sw], w_in0[:, 512 + sw]], axis=1))
    fm = lambda v, n: np.ascontiguousarray(f32(v).reshape(n, 128).T)
    vecs = np.concatenate([fm(norm_ffn1[0], 8), fm(norm_mix[0], 8), fm(norm_ffn2[0], 8), fm(norm_final, 8),
                           fm(ret_gn[0], 4), fm(pool_scale[0], 4), fm(ada_b[0], 72)], axis=1)
    pool_w_l = np.ascontiguousarray(f32(pool_w)[0].transpose(1, 0, 2).reshape(128, 512))
    ada_full = f32(ada_w)[0]
    n_cores = len(list(cores))
    shared = dict(ffn1_w1=f32(ffn1_w1)[0], ffn1_w3=f32(ffn1_w3)[0], ffn1_w2=f32(ffn1_w2)[0],
                  ffn2_w1=f32(ffn2_w1)[0], ffn2_w3=f32(ffn2_w3)[0], ffn2_w2=f32(ffn2_w2)[0], w_in=w_in_aug,
                  w_out=f32(w_out)[0], pool_w=pool_w_l,  vecs=np.ascontiguousarray(vecs), ctab=_ctab(),
                  ident=np.eye(128, dtype=np.float32), rmat=_rmat())
    tabs = {rb: (_pool_tables(rb), _rot_tables(rb)) for rb in (False, True)}
    df, db = f32(ret_decay_fwd)[0], f32(ret_decay_bwd)[0]
    in_maps = []
    for core in cores:
        b, rb = core // 2, bool(core % 2)
        xp = x_prompt[4 * core:4 * core + 4].reshape(1024, D)
        xs_ = x_sample[b, 2048:4096][::-1] if rb else x_sample[b, 0:2048]
        x_tok = np.concatenate([xp, xs_], 0)
        x_T = np.ascontiguousarray(x_tok.T.reshape(8, 128, 3072).transpose(1, 0, 2))
        ag = 4 if n_cores >= 4 else 2
        if ag == 4:
            b_lo = (core // 4) * 2
            cond = np.stack([f32(c_ctx), f32(c)[b_lo], f32(c)[b_lo + 1]], 0)
        else:
            cond = np.stack([f32(c_ctx), f32(c)[b]], 0)
        ncd = cond.shape[0]
        condT = np.ascontiguousarray(cond.reshape(ncd, 8, 128).transpose(2, 1, 0).reshape(128, 8 * ncd))
        csel = np.zeros((128, 2), np.float32)
        csel[:, b % 2] = 1.0
        cw = 9216 // ag
        dec = np.concatenate([df, db, (db if rb else df), (df if rb else db)]).reshape(1, 16)
        st = f32(state_ret_bwd if rb else state_ret_fwd)[b, 0]
        s_init = np.ascontiguousarray(st.transpose(1, 0, 2).reshape(128, 512))
        msel = np.zeros((128, 2), np.float32)
        msel[:, 0 if rb else 1] = 1.0
        (bands, invc), rot = tabs[rb]
        m = dict(shared)
        m.update(ada_w=np.ascontiguousarray(ada_full[:, (core % ag) * cw:(core % ag + 1) * cw]), csel=csel, x_T=x_T, condT=condT, dec=np.ascontiguousarray(dec.astype(np.float32)), s_init=s_init,
                 rot=rot, msel=msel, bands=bands, invcnt=invc)
        in_maps.append(m)
    return in_maps


def _assemble(results, cores=range(8)):
    y_prompt = np.empty((32, 256, D), np.float32)
    y_sample = np.empty((4, 4096, D), np.float32)
    new_f = np.empty((32, 1, 4, 128, 128), np.float32)
    new_b = np.empty((32, 1, 4, 128, 128), np.float32)
    for k_, core in enumerate(cores):
        r = results[k_]
        b, rb = core // 2, bool(core % 2)
        y = np.asarray(r["y_T"], dtype=np.float32).transpose(2, 1, 0).reshape(3072, D)
        y_prompt[4 * core:4 * core + 4] = y[0:1024].reshape(4, 256, D)
        if rb:
            y_sample[b, 2048:4096] = y[1024:][::-1]
        else:
            y_sample[b, 0:2048] = y[1024:]
        new_f[4 * core:4 * core + 4, 0] = np.asarray(r["nsf"], dtype=np.float32)
        new_b[4 * core:4 * core + 4, 0] = np.asarray(r["nsb"], dtype=np.float32)
    return (y_prompt, y_sample, new_f, new_b)
```
